# Optimizing a Trainium2 kernel written in Bass

```python
import math
import jax, jax.numpy as jnp
from jax import lax
import numpy as np

D_MODEL = 1024
BATCH = 8
SEQ = 2048
DEPTH = 4
DEC_BATCH = 128
DEC_SEQ = 8
PAST_LEN = 16384
PAGE_SIZE = 128

SSD_D_INNER = D_MODEL
SSD_HEADDIM = 64
SSD_HEADS = SSD_D_INNER // SSD_HEADDIM
SSD_D_STATE = 64
SSD_GROUPS = 4
SSD_CONV = 4
SSD_CHUNK = 128
SSD_CONV_CH = SSD_D_INNER + 2 * SSD_GROUPS * SSD_D_STATE
S5_WIDTH = D_MODEL // 2
S5_GROUP = 16
S5_GROUPS = S5_WIDTH // S5_GROUP
S5_STATE = 64
HG_WIDTH = D_MODEL // 2
HG_HEADS = 4
HG_HEADDIM = HG_WIDTH // HG_HEADS
HG_CHUNK = 64
RW_WIDTH = D_MODEL // 2
RW_HEADDIM = 64
RW_HEADS = RW_WIDTH // RW_HEADDIM
RW_W_RANK = 64
RW_A_RANK = 64
RW_G_RANK = 128
RW_PROJ = 3 * RW_WIDTH + RW_W_RANK + RW_A_RANK + RW_G_RANK
RW_LN_EPS = 64e-5
N_BRANCH = 4
D_FF = 2816
FFN_CONV = 3
EPS = 1e-6

IN_SIZES = (SSD_D_INNER, SSD_CONV_CH, SSD_HEADS, S5_WIDTH, HG_WIDTH, HG_WIDTH, HG_WIDTH, HG_WIDTH, RW_PROJ)
IN_WIDTH = sum(IN_SIZES)
IN_OFFSETS = tuple(int(o) for o in np.cumsum(IN_SIZES)[:-1])
RW_OFFSETS = (RW_WIDTH, 2 * RW_WIDTH, 3 * RW_WIDTH, 3 * RW_WIDTH + RW_W_RANK, 3 * RW_WIDTH + RW_W_RANK + RW_A_RANK)

kernel_name = 'hybrid_ssd_s5_hgrn2_rwkv7_step'

F32 = jnp.float32


def rmsnorm(x, w):
    xf = x.astype(F32)
    xf = xf * lax.rsqrt(jnp.mean(xf * xf, axis=-1, keepdims=True) + EPS)
    return (xf * w.astype(F32)).astype(x.dtype)


def causal_dwconv(u, buf, w, b):
    k = w.shape[0]
    n = u.shape[1]
    full = jnp.concatenate([buf.astype(u.dtype), u], axis=1)
    out = b
    for j in range(k):
        out = out + full[:, j:j + n] * w[j]
    return out, full[:, n:]


def pad_len(a, pad):
    return jnp.pad(a, [(0, 0), (0, pad)] + [(0, 0)] * (a.ndim - 2))


def segsum(a):
    t = a.shape[-1]
    ae = jnp.broadcast_to(a[..., None], a.shape + (t,))
    ae = jnp.where(jnp.tril(jnp.ones((t, t), bool), -1), ae, 0.0)
    ss = jnp.cumsum(ae, axis=-2)
    return jnp.where(jnp.tril(jnp.ones((t, t), bool), 0), ss, -jnp.inf)


def ssd_scan(xdt, dta, bm, cm, h0):
    bsz, n, h, pdim = xdt.shape
    c = min(SSD_CHUNK, n)
    pad = (-n) % c
    xdt, dta, bm, cm = (pad_len(t, pad) for t in (xdt, dta, bm, cm))
    nc = (n + pad) // c
    xc = xdt.reshape(bsz, nc, c, h, pdim)
    bc = bm.reshape(bsz, nc, c, h, SSD_D_STATE)
    cc = cm.reshape(bsz, nc, c, h, SSD_D_STATE)
    a = dta.reshape(bsz, nc, c, h).transpose(0, 3, 1, 2)
    a_cum = jnp.cumsum(a, axis=-1)
    lmat = jnp.exp(segsum(a))
    y_diag = jnp.einsum('bclhn,bcshn,bhcls,bcshp->bclhp', cc, bc, lmat, xc)
    decay_states = jnp.exp(a_cum[..., -1:] - a_cum)
    states = jnp.einsum('bclhn,bhcl,bclhp->bchpn', bc, decay_states, xc)
    states = jnp.concatenate([h0[:, None], states], axis=1)
    chunk_tot = jnp.pad(a_cum[..., -1], ((0, 0), (0, 0), (1, 0)))
    decay_chunk = jnp.exp(segsum(chunk_tot))
    states = jnp.einsum('bhzc,bchpn->bzhpn', decay_chunk, states)
    y_off = jnp.einsum('bclhn,bchpn,bhcl->bclhp', cc, states[:, :-1], jnp.exp(a_cum))
    y = (y_diag + y_off).reshape(bsz, nc * c, h, pdim)[:, :n]
    return y, states[:, -1]


def ssd_mixer(z, xbc, dt_raw, h0, conv_buf, p):
    bsz, n, _ = z.shape
    xbc, new_buf = causal_dwconv(xbc, conv_buf, p['ssd_conv_w'], p['ssd_conv_b'])
    xbc = jax.nn.silu(xbc.astype(F32))
    xs, bm, cm = jnp.split(xbc, [SSD_D_INNER, SSD_D_INNER + SSD_GROUPS * SSD_D_STATE], axis=-1)
    rep = SSD_HEADS // SSD_GROUPS
    xs = xs.reshape(bsz, n, SSD_HEADS, SSD_HEADDIM)
    bm = jnp.repeat(bm.reshape(bsz, n, SSD_GROUPS, SSD_D_STATE), rep, axis=2)
    cm = jnp.repeat(cm.reshape(bsz, n, SSD_GROUPS, SSD_D_STATE), rep, axis=2)
    dt = jax.nn.softplus(dt_raw.astype(F32) + p['ssd_dt_bias'].astype(F32))
    a = -jnp.exp(p['ssd_a_log'].astype(F32))
    y, h_new = ssd_scan(xs * dt[..., None], dt * a, bm, cm, h0.astype(F32))
    y = y + xs * p['ssd_d'].astype(F32)[:, None]
    y = y.reshape(bsz, n, SSD_D_INNER)
    y = rmsnorm(y * jax.nn.silu(z.astype(F32)), p['ssd_norm_w'])
    return y.astype(z.dtype), h_new, new_buf


def s5_combine(e1, e2):
    a1r, a1i, b1r, b1i = e1
    a2r, a2i, b2r, b2i = e2
    return (a2r * a1r - a2i * a1i, a2r * a1i + a2i * a1r,
            a2r * b1r - a2i * b1i + b2r, a2r * b1i + a2i * b1r + b2i)


def s5_mixer(u, h0_re, h0_im, p):
    bsz, n, _ = u.shape
    uf = u.astype(F32)
    ug = uf.reshape(bsz, n, S5_GROUPS, S5_GROUP)
    dt = jnp.exp(p['s5_log_dt'].astype(F32))[:, None]
    a_re = p['s5_a_re'].astype(F32)
    a_im = p['s5_a_im'].astype(F32)
    mag = jnp.exp(dt * a_re)
    ab_re = mag * jnp.cos(dt * a_im)
    ab_im = mag * jnp.sin(dt * a_im)
    den = a_re * a_re + a_im * a_im
    q_re = ((ab_re - 1.0) * a_re + ab_im * a_im) / den
    q_im = (ab_im * a_re - (ab_re - 1.0) * a_im) / den
    b_re = p['s5_b_re'].astype(F32)
    b_im = p['s5_b_im'].astype(F32)
    bb_re = q_re[..., None] * b_re - q_im[..., None] * b_im
    bb_im = q_re[..., None] * b_im + q_im[..., None] * b_re
    bu_re = jnp.einsum('gnj,blgj->blgn', bb_re, ug)
    bu_im = jnp.einsum('gnj,blgj->blgn', bb_im, ug)
    a_seq_re = jnp.broadcast_to(ab_re, bu_re.shape)
    a_seq_im = jnp.broadcast_to(ab_im, bu_im.shape)
    cr, ci, hr, hi = lax.associative_scan(s5_combine, (a_seq_re, a_seq_im, bu_re, bu_im), axis=1)
    h0r = h0_re.astype(F32)[:, None]
    h0i = h0_im.astype(F32)[:, None]
    hr = hr + cr * h0r - ci * h0i
    hi = hi + cr * h0i + ci * h0r
    y = (jnp.einsum('gjn,blgn->blgj', p['s5_c_re'].astype(F32), hr)
         - jnp.einsum('gjn,blgn->blgj', p['s5_c_im'].astype(F32), hi))
    y = y.reshape(bsz, n, S5_WIDTH) + p['s5_d'].astype(F32) * uf
    y = jax.nn.gelu(y)
    y = y * jax.nn.sigmoid(y @ p['s5_glu_w'].astype(F32) + p['s5_glu_b'].astype(F32))
    return y.astype(u.dtype), hr[:, -1], hi[:, -1]


def hgrn_chunked(q, k, v, logf, s0):
    bsz, n, h, _ = q.shape
    c = min(HG_CHUNK, n)
    pad = (-n) % c
    nc = (n + pad) // c

    def blocks(t):
        t = pad_len(t, pad).reshape(bsz, nc, c, h, t.shape[-1])
        return t.transpose(1, 0, 3, 2, 4)

    causal = jnp.tril(jnp.ones((c, c), bool))

    def step(s, blk):
        qc, kc, vc, gc = blk
        b = jnp.cumsum(gc, axis=-2)
        o_inter = jnp.einsum('bhtk,bhkv->bhtv', qc * jnp.exp(b), s)
        diff = b[..., :, None, :] - b[..., None, :, :]
        decay = jnp.exp(jnp.where(causal[:, :, None], diff, -jnp.inf))
        att = jnp.einsum('bhtk,bhsk,bhtsk->bhts', qc, kc, decay)
        o_intra = jnp.einsum('bhts,bhsv->bhtv', att, vc)
        b_last = b[..., -1:, :]
        s = (jnp.exp(b_last[..., 0, :])[..., None] * s
             + jnp.einsum('bhsk,bhsv->bhkv', kc * jnp.exp(b_last - b), vc))
        return s, o_inter + o_intra

    s, o = lax.scan(step, s0, tuple(blocks(t) for t in (q, k, v, logf)))
    o = o.transpose(1, 0, 3, 2, 4).reshape(bsz, nc * c, h, -1)[:, :n]
    return o, s


def hgrn_mixer(q, f_raw, i_in, g, s0, lb, p):
    bsz, n, _ = q.shape
    shp = (bsz, n, HG_HEADS, HG_HEADDIM)
    lbf = lb.astype(F32)
    f = lbf + (1.0 - lbf) * jax.nn.sigmoid(f_raw.astype(F32))
    logf = jnp.log(f).reshape(shp)
    k = (1.0 - f).reshape(shp)
    qf = jax.nn.silu(q.astype(F32)).reshape(shp)
    v = i_in.astype(F32).reshape(shp)
    o, s = hgrn_chunked(qf, k, v, logf, s0.astype(F32))
    o = rmsnorm(o, p['hg_norm_w']) * jax.nn.sigmoid(g.astype(F32)).reshape(shp)
    return o.reshape(bsz, n, HG_WIDTH).astype(q.dtype), s


def rwkv_scan(r, w, k, v, a, b, s0):
    def step(s, inp):
        rt, wt, kt, vt, at, bt = inp
        sa = jnp.einsum('bhvk,bhk->bhv', s, at)
        s = s * wt[:, :, None, :] + sa[..., None] * bt[:, :, None, :] + vt[..., None] * kt[:, :, None, :]
        return s, jnp.einsum('bhvk,bhk->bhv', s, rt)

    seq = tuple(jnp.swapaxes(t, 0, 1) for t in (r, w, k, v, a, b))
    s, y = lax.scan(step, s0, seq)
    return jnp.swapaxes(y, 0, 1), s


def rwkv_mixer(pr, shift0, s0, p):
    bsz, n, _ = pr.shape
    prev = jnp.concatenate([shift0[:, None].astype(pr.dtype), pr[:, :-1]], axis=1)
    xm = (pr + (prev - pr) * p['rw_mu']).astype(F32)
    r, k, v, wd, ad, gd = jnp.split(xm, RW_OFFSETS, axis=-1)
    w = -jax.nn.softplus(-(p['rw_w0'].astype(F32) + jnp.tanh(wd) @ p['rw_w_up'].astype(F32))) - 0.5
    decay = jnp.exp(-jnp.exp(w))
    a = jax.nn.sigmoid(p['rw_a0'].astype(F32) + ad @ p['rw_a_up'].astype(F32))
    g = jax.nn.sigmoid(gd) @ p['rw_g_up'].astype(F32)
    shp = (bsz, n, RW_HEADS, RW_HEADDIM)
    r, k, v, decay, a = (t.reshape(shp) for t in (r, k, v, decay, a))
    kk = k * p['rw_k_k'].astype(F32)
    kk = kk * lax.rsqrt(jnp.maximum(jnp.sum(kk * kk, axis=-1, keepdims=True), 1e-24))
    k = k * (1.0 + (a - 1.0) * p['rw_k_a'].astype(F32))
    y, s = rwkv_scan(r, decay, k, v, -kk, kk * a, s0.astype(F32))
    mu = jnp.mean(y, axis=-1, keepdims=True)
    var = jnp.mean(jnp.square(y - mu), axis=-1, keepdims=True)
    y = (y - mu) * lax.rsqrt(var + RW_LN_EPS) * p['rw_ln_w'].astype(F32) + p['rw_ln_b'].astype(F32)
    y = y + jnp.sum(r * k * p['rw_r_k'].astype(F32), axis=-1, keepdims=True) * v
    y = y.reshape(bsz, n, RW_WIDTH) * g
    return y.astype(pr.dtype), s, pr[:, -1]


def block(x, st, p, lb):
    h_ssd, buf_ssd, s5r, s5i, s_hg, s_rw, sh_rw, buf_ffn = st
    bsz, n, _ = x.shape
    xn = rmsnorm(x, p['norm1_w'])
    proj = xn @ p['w_in']
    z, xbc, dt_raw, u_s5, q_hg, f_hg, i_hg, g_hg, p_rw = jnp.split(proj, IN_OFFSETS, axis=-1)
    y_ssd, h_ssd, buf_ssd = ssd_mixer(z, xbc, dt_raw, h_ssd, buf_ssd, p)
    y_s5, s5r, s5i = s5_mixer(u_s5, s5r, s5i, p)
    y_hg, s_hg = hgrn_mixer(q_hg, f_hg, i_hg, g_hg, s_hg, lb, p)
    y_rw, s_rw, sh_rw = rwkv_mixer(p_rw, sh_rw, s_rw, p)
    gates = jax.nn.sigmoid((xn @ p['w_merge'] + p['b_merge']).astype(F32)).astype(x.dtype)
    gates = gates.reshape(bsz, n, N_BRANCH, D_MODEL)
    merged = (gates[:, :, 0] * (y_ssd @ p['w_br_ssd'])
              + gates[:, :, 1] * (y_s5 @ p['w_br_s5'])
              + gates[:, :, 2] * (y_hg @ p['w_br_hg'])
              + gates[:, :, 3] * (y_rw @ p['w_br_rw']))
    x = x + merged @ p['w_out']
    xn2 = rmsnorm(x, p['norm2_w'])
    up, buf_ffn = causal_dwconv(xn2 @ p['ffn_up'], buf_ffn, p['ffn_conv_w'], p['ffn_conv_b'])
    gate_h, val_h = jnp.split(up, 2, axis=-1)
    x = x + (jax.nn.gelu(gate_h) * val_h) @ p['ffn_down']
    new_st = (h_ssd, buf_ssd, s5r, s5i, s_hg, s_rw, sh_rw, buf_ffn)
    return x, tuple(s.astype(x.dtype) for s in new_st)


def zero_states(bsz, dtype):
    return (jnp.zeros((bsz, SSD_HEADS, SSD_HEADDIM, SSD_D_STATE), dtype),
            jnp.zeros((bsz, SSD_CONV - 1, SSD_CONV_CH), dtype),
            jnp.zeros((bsz, S5_GROUPS, S5_STATE), dtype),
            jnp.zeros((bsz, S5_GROUPS, S5_STATE), dtype),
            jnp.zeros((bsz, HG_HEADS, HG_HEADDIM, HG_HEADDIM), dtype),
            jnp.zeros((bsz, RW_HEADS, RW_HEADDIM, RW_HEADDIM), dtype),
            jnp.zeros((bsz, RW_PROJ), dtype),
            jnp.zeros((bsz, FFN_CONV - 1, 2 * D_FF), dtype))


def trunk(x, init_states, params, lb_all, final_norm_w):
    new_states = []
    for l in range(DEPTH):
        p = {name: arr[l] for name, arr in params.items()}
        x, st = block(x, init_states[l], p, lb_all[l])
        new_states.append(st)
    stacked = tuple(jnp.stack([st[i] for st in new_states], axis=0) for i in range(len(new_states[0])))
    return rmsnorm(x, final_norm_w), stacked


def setup_inputs(seed: int = 0) -> dict:
    key = jax.random.key(seed)
    ks = iter(jax.random.split(key, 128))

    def nrm(shape, scale):
        return scale * jax.random.normal(next(ks), shape, F32)

    def unif(shape, lo, hi):
        return jax.random.uniform(next(ks), shape, F32, lo, hi)

    L = DEPTH
    dt0 = jnp.exp(unif((L, SSD_HEADS), math.log(1e-3), math.log(1e-1)))
    inp = {}
    inp['x_prompt'] = nrm((BATCH, SEQ, D_MODEL), 1.0)
    inp['x_sample'] = nrm((DEC_BATCH, DEC_SEQ, D_MODEL), 1.0)
    inp['state_ssd'] = nrm((L, DEC_BATCH, SSD_HEADS, SSD_HEADDIM, SSD_D_STATE), 0.5)
    inp['state_ssd_conv'] = nrm((L, DEC_BATCH, SSD_CONV - 1, SSD_CONV_CH), 1.0)
    inp['state_s5_re'] = nrm((L, DEC_BATCH, S5_GROUPS, S5_STATE), 0.1)
    inp['state_s5_im'] = nrm((L, DEC_BATCH, S5_GROUPS, S5_STATE), 0.1)
    inp['state_hgrn'] = nrm((L, DEC_BATCH, HG_HEADS, HG_HEADDIM, HG_HEADDIM), 0.5)
    inp['state_rwkv'] = nrm((L, DEC_BATCH, RW_HEADS, RW_HEADDIM, RW_HEADDIM), 0.3)
    inp['state_rwkv_shift'] = nrm((L, DEC_BATCH, RW_PROJ), 1.0)
    inp['state_ffn_conv'] = nrm((L, DEC_BATCH, FFN_CONV - 1, 2 * D_FF), 1.0)
    inp['norm1_w'] = 1.0 + nrm((L, D_MODEL), 0.02)
    inp['w_in'] = nrm((L, D_MODEL, IN_WIDTH), D_MODEL ** -0.5)
    inp['ssd_conv_w'] = nrm((L, SSD_CONV, SSD_CONV_CH), SSD_CONV ** -0.5)
    inp['ssd_conv_b'] = nrm((L, SSD_CONV_CH), 0.02)
    inp['ssd_dt_bias'] = dt0 + jnp.log(-jnp.expm1(-dt0))
    inp['ssd_a_log'] = jnp.log(unif((L, SSD_HEADS), 1.0, 16.0))
    inp['ssd_d'] = 1.0 + nrm((L, SSD_HEADS), 0.1)
    inp['ssd_norm_w'] = 1.0 + nrm((L, SSD_D_INNER), 0.02)
    inp['s5_a_re'] = -0.5 + nrm((L, S5_GROUPS, S5_STATE), 0.01)
    inp['s5_a_im'] = (jnp.broadcast_to(math.pi * jnp.arange(S5_STATE, dtype=F32), (L, S5_GROUPS, S5_STATE))
                      + nrm((L, S5_GROUPS, S5_STATE), 0.01))
    inp['s5_log_dt'] = unif((L, S5_GROUPS), math.log(1e-3), math.log(1e-1))
    inp['s5_b_re'] = nrm((L, S5_GROUPS, S5_STATE, S5_GROUP), (2.0 * S5_GROUP) ** -0.5)
    inp['s5_b_im'] = nrm((L, S5_GROUPS, S5_STATE, S5_GROUP), (2.0 * S5_GROUP) ** -0.5)
    inp['s5_c_re'] = nrm((L, S5_GROUPS, S5_GROUP, S5_STATE), (2.0 * S5_STATE) ** -0.5)
    inp['s5_c_im'] = nrm((L, S5_GROUPS, S5_GROUP, S5_STATE), (2.0 * S5_STATE) ** -0.5)
    inp['s5_d'] = nrm((L, S5_WIDTH), 1.0)
    inp['s5_glu_w'] = nrm((L, S5_WIDTH, S5_WIDTH), S5_WIDTH ** -0.5)
    inp['s5_glu_b'] = nrm((L, S5_WIDTH), 0.02)
    inp['hg_lb_raw'] = nrm((L, HG_WIDTH), 0.5)
    inp['hg_norm_w'] = 1.0 + nrm((L, HG_HEADDIM), 0.02)
    inp['rw_mu'] = unif((L, RW_PROJ), 0.0, 1.0)
    inp['rw_w0'] = unif((L, RW_WIDTH), -4.0, 0.0)
    inp['rw_w_up'] = nrm((L, RW_W_RANK, RW_WIDTH), 0.3 * RW_W_RANK ** -0.5)
    inp['rw_a0'] = nrm((L, RW_WIDTH), 0.1)
    inp['rw_a_up'] = nrm((L, RW_A_RANK, RW_WIDTH), 0.5 * RW_A_RANK ** -0.5)
    inp['rw_g_up'] = nrm((L, RW_G_RANK, RW_WIDTH), RW_G_RANK ** -0.5)
    inp['rw_k_k'] = 0.85 + nrm((L, RW_HEADS, RW_HEADDIM), 0.05)
    inp['rw_k_a'] = 1.0 + nrm((L, RW_HEADS, RW_HEADDIM), 0.05)
    inp['rw_r_k'] = nrm((L, RW_HEADS, RW_HEADDIM), 0.1)
    inp['rw_ln_w'] = 1.0 + nrm((L, RW_HEADS, RW_HEADDIM), 0.02)
    inp['rw_ln_b'] = nrm((L, RW_HEADS, RW_HEADDIM), 0.02)
    inp['w_br_ssd'] = nrm((L, SSD_D_INNER, D_MODEL), SSD_D_INNER ** -0.5)
    inp['w_br_s5'] = nrm((L, S5_WIDTH, D_MODEL), S5_WIDTH ** -0.5)
    inp['w_br_hg'] = nrm((L, HG_WIDTH, D_MODEL), HG_WIDTH ** -0.5)
    inp['w_br_rw'] = nrm((L, RW_WIDTH, D_MODEL), RW_WIDTH ** -0.5)
    inp['w_merge'] = nrm((L, D_MODEL, N_BRANCH * D_MODEL), D_MODEL ** -0.5)
    inp['b_merge'] = nrm((L, N_BRANCH * D_MODEL), 0.02)
    inp['w_out'] = nrm((L, D_MODEL, D_MODEL), 0.5 * D_MODEL ** -0.5)
    inp['norm2_w'] = 1.0 + nrm((L, D_MODEL), 0.02)
    inp['ffn_up'] = nrm((L, D_MODEL, 2 * D_FF), D_MODEL ** -0.5)
    inp['ffn_conv_w'] = nrm((L, FFN_CONV, 2 * D_FF), FFN_CONV ** -0.5)
    inp['ffn_conv_b'] = nrm((L, 2 * D_FF), 0.02)
    inp['ffn_down'] = nrm((L, D_FF, D_MODEL), D_FF ** -0.5)
    inp['final_norm_w'] = 1.0 + nrm((D_MODEL,), 0.02)
    return inp


def reference(x_prompt, x_sample, state_ssd, state_ssd_conv, state_s5_re, state_s5_im, state_hgrn,
              state_rwkv, state_rwkv_shift, state_ffn_conv,
              norm1_w, w_in, ssd_conv_w, ssd_conv_b, ssd_dt_bias, ssd_a_log, ssd_d, ssd_norm_w,
              s5_a_re, s5_a_im, s5_log_dt, s5_b_re, s5_b_im, s5_c_re, s5_c_im, s5_d, s5_glu_w, s5_glu_b,
              hg_lb_raw, hg_norm_w,
              rw_mu, rw_w0, rw_w_up, rw_a0, rw_a_up, rw_g_up, rw_k_k, rw_k_a, rw_r_k, rw_ln_w, rw_ln_b,
              w_br_ssd, w_br_s5, w_br_hg, w_br_rw, w_merge, b_merge, w_out,
              norm2_w, ffn_up, ffn_conv_w, ffn_conv_b, ffn_down, final_norm_w):
    params = {'norm1_w': norm1_w, 'w_in': w_in, 'ssd_conv_w': ssd_conv_w, 'ssd_conv_b': ssd_conv_b,
              'ssd_dt_bias': ssd_dt_bias, 'ssd_a_log': ssd_a_log, 'ssd_d': ssd_d, 'ssd_norm_w': ssd_norm_w,
              's5_a_re': s5_a_re, 's5_a_im': s5_a_im, 's5_log_dt': s5_log_dt, 's5_b_re': s5_b_re,
              's5_b_im': s5_b_im, 's5_c_re': s5_c_re, 's5_c_im': s5_c_im, 's5_d': s5_d,
              's5_glu_w': s5_glu_w, 's5_glu_b': s5_glu_b, 'hg_norm_w': hg_norm_w,
              'rw_mu': rw_mu, 'rw_w0': rw_w0, 'rw_w_up': rw_w_up, 'rw_a0': rw_a0, 'rw_a_up': rw_a_up,
              'rw_g_up': rw_g_up, 'rw_k_k': rw_k_k, 'rw_k_a': rw_k_a, 'rw_r_k': rw_r_k,
              'rw_ln_w': rw_ln_w, 'rw_ln_b': rw_ln_b,
              'w_br_ssd': w_br_ssd, 'w_br_s5': w_br_s5, 'w_br_hg': w_br_hg, 'w_br_rw': w_br_rw,
              'w_merge': w_merge, 'b_merge': b_merge, 'w_out': w_out, 'norm2_w': norm2_w,
              'ffn_up': ffn_up, 'ffn_conv_w': ffn_conv_w, 'ffn_conv_b': ffn_conv_b, 'ffn_down': ffn_down}
    lb_all = jnp.cumsum(jax.nn.softmax(hg_lb_raw.astype(F32), axis=0), axis=0)
    lb_all = lb_all - lb_all[:1]
    prompt_init = [zero_states(x_prompt.shape[0], x_prompt.dtype) for _ in range(DEPTH)]
    sample_states = (state_ssd, state_ssd_conv, state_s5_re, state_s5_im, state_hgrn,
                     state_rwkv, state_rwkv_shift, state_ffn_conv)
    sample_init = [tuple(s[l] for s in sample_states) for l in range(DEPTH)]
    y_prompt, (p_ssd, p_ssd_conv, p_s5_re, p_s5_im, p_hgrn, p_rwkv, p_rwkv_shift, p_ffn_conv) = trunk(
        x_prompt, prompt_init, params, lb_all, final_norm_w)
    y_sample, (s_ssd, s_ssd_conv, s_s5_re, s_s5_im, s_hgrn, s_rwkv, s_rwkv_shift, s_ffn_conv) = trunk(
        x_sample, sample_init, params, lb_all, final_norm_w)
    return (y_prompt, y_sample,
            p_ssd, p_ssd_conv, p_s5_re, p_s5_im, p_hgrn, p_rwkv, p_rwkv_shift, p_ffn_conv,
            s_ssd, s_ssd_conv, s_s5_re, s_s5_im, s_hgrn, s_rwkv, s_rwkv_shift, s_ffn_conv)
```

```python
import numpy as np
import concourse.bass as bass
import concourse.mybir as mybir
from concourse.bass_utils import run_bass_kernel_spmd

F32 = mybir.dt.float32
BF16 = mybir.dt.bfloat16
AF = mybir.ActivationFunctionType
ALU = mybir.AluOpType
AX = mybir.AxisListType

D = 1024
KC = D // 128
NCORES = 8
EPS = 1e-6
D_FF = 2816


class Buf:
    def __init__(self, t):
        self.t = t
        self.last_w = None
        self.readers = {}
        self.dsem = None
        self.dcount = 0

    def __getitem__(self, k):
        return self.t[k]


class Sched:
    def __init__(self, nc):
        self.nc = nc
        self.eng = {'pe': nc.tensor, 'dve': nc.vector, 'act': nc.scalar, 'pool': nc.gpsimd, 'sp': nc.sync}
        self.sem = {e: nc.alloc_semaphore(name='s_' + e) for e in ('pe', 'dve', 'act', 'pool')}
        self.cnt = {e: 0 for e in self.sem}
        self.epoch = {e: 0 for e in self.sem}
        self.all_dma = []
        self.LIMIT = 8000
        self.seen = {e: {} for e in self.eng}
        self.bufs = {}
        self.nbuf = 0
        self.pending_pe = []

    def sb(self, shape, dt=F32, name=None):
        self.nbuf += 1
        name = name or ('b%d' % self.nbuf)
        b = Buf(self.nc.alloc_sbuf_tensor(name, list(shape), dt))
        self.bufs[name] = b
        return b

    def ps(self, shape, dt=F32, name=None):
        self.nbuf += 1
        name = name or ('p%d' % self.nbuf)
        b = Buf(self.nc.alloc_psum_tensor(name, list(shape), dt))
        self.bufs[name] = b
        return b

    def _wait(self, e, tok):
        sem, val, key = tok
        if key.startswith('pe#') and e == 'pe':
            return
        if self.seen[e].get(key, 0) >= val:
            return
        self.eng[e].wait_ge(sem, val)
        self.seen[e][key] = val

    def _find(self, ap):
        try:
            return self.bufs.get(ap.tensor.name)
        except Exception:
            return None

    def _deps(self, e, kw, args):
        outs, ins = [], []
        items = list(kw.items()) + [('out' if i == 0 else 'in%d' % i, a) for i, a in enumerate(args)]
        for k, v in items:
            if isinstance(v, bass.AP):
                b = self._find(v)
                if b is None:
                    continue
                (outs if k in ('out', 'accum_out') else ins).append(b)
        for b in ins:
            if b.last_w:
                self._wait(e, b.last_w)
        for b in outs:
            if b.last_w:
                self._wait(e, b.last_w)
            for tok in b.readers.values():
                self._wait(e, tok)
        return outs, ins

    def I(self, e, fn, *args, sig=True, **kw):
        outs, ins = self._deps(e, kw, args)
        ins_obj = getattr(self.eng[e], fn)(*args, **kw)
        key = '%s#%d' % (e, self.epoch[e])
        if sig:
            self.cnt[e] += 1
            ins_obj.then_inc(self.sem[e], 1)
            tok = (self.sem[e], self.cnt[e], key)
        else:
            tok = (self.sem[e], self.cnt[e] + 1, key)
        for b in ins:
            b.readers[key] = tok
        for b in outs:
            b.last_w = tok
            b.readers = {}
        if sig and self.cnt[e] >= self.LIMIT:
            self.epoch[e] += 1
            self.sem[e] = self.nc.alloc_semaphore(name='s_%s_%d' % (e, self.epoch[e]))
            self.cnt[e] = 0
        return ins_obj

    def dma(self, q, out, in_):
        outs, ins = self._deps(q, {'out': out, 'in_': in_}, ())
        ins_obj = self.eng[q].dma_start(out=out, in_=in_)
        b = (outs + ins)[0]
        if b.dsem is None or b.dcount >= self.LIMIT:
            if b.dsem is not None:
                self.all_dma.append((b.dsem, b.dcount, 'd_%s_%d' % (b.t.name, b.depoch)))
            b.depoch = getattr(b, 'depoch', -1) + 1
            b.dsem = self.nc.alloc_semaphore(name='d_%s_%d' % (b.t.name, b.depoch))
            b.dcount = 0
        b.dcount += 16
        ins_obj.then_inc(b.dsem, 16)
        tok = (b.dsem, b.dcount, 'd_%s_%d' % (b.t.name, b.depoch))
        for x in ins:
            x.readers[tok[2]] = tok
        for x in outs:
            x.last_w = tok
            x.readers = {}

    def finish(self):
        for tok in self.all_dma:
            self._wait('sp', tok)
        for b in self.bufs.values():
            if b.dsem is not None:
                self._wait('sp', (b.dsem, b.dcount, 'd_%s_%d' % (b.t.name, b.depoch)))


class Ring:
    def __init__(self, bufs):
        self.bufs = bufs
        self.i = 0

    def next(self):
        b = self.bufs[self.i % len(self.bufs)]
        self.i += 1
        return b


NCONST = 2152
C_IDENT, C_ONES, C_MLE, C_MLT, C_MGT, C_NEG, C_BD64, C_POS, C_R64, C_R8, C_MSKB, C_MSKC, C_R8L = (
    0, 128, 256, 384, 512, 640, 768, 896, 960, 1472, 1600, 1632, 1640)


def make_consts():
    p = np.arange(128)[:, None]
    f = np.arange(128)[None, :]
    parts = [
        (p == f), np.ones((128, 128)), (p <= f), (p < f), (p > f), np.where(p > f, -30000.0, 0.0),
        (p // 64 == f // 64),
        np.broadcast_to(np.arange(1, 65)[None, :], (128, 64)),
        np.broadcast_to((np.arange(512) % 64 != 0)[None, :], (128, 512)),
        np.broadcast_to((np.arange(128) % 8 != 0)[None, :], (128, 128)),
        (np.arange(8)[None, None, :] == 2 * np.arange(4)[None, :, None] + (np.arange(128) // 64)[:, None, None]).reshape(128, 32),
        ((np.arange(128) // 16)[:, None, None] == 2 * np.arange(4)[None, :, None] + np.arange(2)[None, None, :]).reshape(128, 8),
        np.broadcast_to((np.arange(512) % 8 != 0)[None, :], (128, 512)),
    ]
    return np.ascontiguousarray(np.concatenate([np.asarray(a, np.float32) for a in parts], axis=1))


IN_NAMES = ['x_prompt', 'x_sample', 'state_ssd', 'state_ssd_conv', 'state_s5_re', 'state_s5_im', 'state_hgrn',
            'state_rwkv', 'state_rwkv_shift', 'state_ffn_conv',
            'norm1_w', 'w_in', 'ssd_conv_w', 'ssd_conv_b', 'ssd_dt_bias', 'ssd_a_log', 'ssd_d', 'ssd_norm_w',
            's5_a_re', 's5_a_im', 's5_log_dt', 's5_b_re', 's5_b_im', 's5_c_re', 's5_c_im', 's5_d', 's5_glu_w',
            's5_glu_b', 'hg_lb_raw', 'hg_norm_w',
            'rw_mu', 'rw_w0', 'rw_w_up', 'rw_a0', 'rw_a_up', 'rw_g_up', 'rw_k_k', 'rw_k_a', 'rw_r_k', 'rw_ln_w',
            'rw_ln_b', 'w_br_ssd', 'w_br_s5', 'w_br_hg', 'w_br_rw', 'w_merge', 'b_merge', 'w_out',
            'norm2_w', 'ffn_up', 'ffn_conv_w', 'ffn_conv_b', 'ffn_down', 'final_norm_w']
STATE_NAMES = ['ssd', 'ssd_conv', 's5_re', 's5_im', 'hgrn', 'rwkv', 'rwkv_shift', 'ffn_conv']


def build(cfg, shapes):
    nc = bass.Bass("TRN2", target_bir_lowering=False)
    with nc.allow_non_contiguous_dma(reason="small per-channel parameter loads"):
        _build(nc, cfg, shapes)
    return nc


def _build(nc, cfg, shapes):
    L, SEQ, NSEQ = cfg['depth'], cfg['seq'], cfg['nseq']
    EN = cfg.get('enable', ('ssd', 's5', 'hg', 'rw'))
    TS = 8
    S = Sched(nc)
    V = lambda fn, *a, **k: S.I('dve', fn, *a, **k)
    A = lambda fn, *a, **k: S.I('act', fn, *a, **k)

    def MM(out, lhsT, rhs, st=True, sp=True, sg=None):
        return S.I('pe', 'matmul', out, lhsT, rhs, start=st, stop=sp, sig=(sp if sg is None else sg))

    din = {}
    for n in IN_NAMES:
        din[n] = nc.dram_tensor(n, list(shapes[n]), F32, kind="ExternalInput").ap()
    din['consts'] = nc.dram_tensor('consts', [128, NCONST], F32, kind="ExternalInput").ap()
    st_shapes = {'ssd': [16, 64, 64], 'ssd_conv': [3, 1536], 's5_re': [32, 64], 's5_im': [32, 64],
                 'hgrn': [4, 128, 128], 'rwkv': [8, 64, 64], 'rwkv_shift': [1792], 'ffn_conv': [2, 5632]}
    dout = {}
    dout['y_prompt'] = nc.dram_tensor('y_prompt', [SEQ, D], F32, kind="ExternalOutput").ap()
    dout['y_sample'] = nc.dram_tensor('y_sample', [NSEQ * TS, D], F32, kind="ExternalOutput").ap()
    for n in STATE_NAMES:
        dout['p_' + n] = nc.dram_tensor('p_' + n, [L] + st_shapes[n], F32, kind="ExternalOutput").ap()
        dout['s_' + n] = nc.dram_tensor('s_' + n, [L, NSEQ] + st_shapes[n], F32, kind="ExternalOutput").ap()

    cst = S.sb([128, NCONST], F32, 'cst')
    S.dma('sp', cst[:, :], din['consts'][:, :])
    ident = cst[:, C_IDENT:C_IDENT + 128]
    ones = cst[:, C_ONES:C_ONES + 128]
    mle = cst[:, C_MLE:C_MLE + 128]
    mlt = cst[:, C_MLT:C_MLT + 128]
    mgt = cst[:, C_MGT:C_MGT + 128]
    negm = cst[:, C_NEG:C_NEG + 128]
    bd64 = cst[:, C_BD64:C_BD64 + 128]

    def TR(out, in_, n):
        return S.I('pe', 'transpose', out, in_, ident[:n, :n])

    psr = Ring([S.ps([128, 512], F32, 'psb%d' % i) for i in range(8)])
    wring = Ring([S.sb([128, 4096], BF16, 'wr%d' % i) for i in range(3)])
    NTKMAX = 512
    x = S.sb([128, KC, NTKMAX], F32, 'x')
    xn = S.sb([128, KC, NTKMAX], BF16, 'xn')
    mrg = S.sb([128, KC, NTKMAX], F32, 'mrg')
    rstd = S.sb([128, NTKMAX], F32, 'rstd')
    par = S.sb([128, 512], F32, 'par')
    parb = S.sb([128, 256], F32, 'parb')
    S5BIG0 = S.sb([128, 128], F32, 's5big0')
    S5BIG1 = S.sb([128, 128], F32, 's5big1')
    t1 = Ring([S.sb([128, NTKMAX], F32, 't1_%d' % i) for i in range(3)])
    stg = Ring([S.sb([128, NTKMAX + 16], F32, 'stg%d' % i) for i in range(2)])
    PG = [S.sb([128, 2048], F32, 'pg%d' % i) for i in range(9)]
    ST_ssd = [S.sb([128, 2, 256], F32, 'stssd%d' % l) for l in range(L)]
    cv_ssd = [S.sb([128, 12, 3], F32, 'cvssd%d' % l) for l in range(L)]
    cv_ffn = [S.sb([128, 44, 2], F32, 'cvffn%d' % l) for l in range(L)]
    for l in range(L):
        V('memset', ST_ssd[l][:], 0.0)
        V('memset', cv_ssd[l][:], 0.0)
        V('memset', cv_ffn[l][:], 0.0)
    shs = S.sb([128, 14, NSEQ * 3], F32, 'shs')
    shf = PG[3][:, 0:44 * NSEQ * 2].rearrange("p (c r) -> p c r", c=44)

    def loadw(dram2d, c0, ncols, r0=0, K=None):
        K = K or dram2d.shape[0]
        kc = K // 128
        b = wring.next()
        view = b.t[:, 0:kc * ncols].rearrange("p (k n) -> p k n", k=kc)
        S.dma('pool', view, dram2d[r0:r0 + K, :].rearrange("(k p) n -> p k n", p=128)[:, :, c0:c0 + ncols])
        return view

    def colload(dst, vec):
        S.dma('sp', dst, vec.rearrange("(k p) -> p k", p=128))

    def rowload(dst, vec, rows=128):
        S.dma('sp', dst, vec.partition_broadcast(rows))

    def sumsq_rstd(src_chunks, ntk, nfeat, eps):
        ps = psr.next()
        n = len(src_chunks)
        for k, sc in enumerate(src_chunks):
            tq = t1.next()
            A('activation', out=tq[:, :ntk], in_=sc, func=AF.Square)
            MM(ps[:, :ntk], ones, tq[:, :ntk], k == 0, k == n - 1, sg=True)
        V('tensor_scalar', out=rstd[:, :ntk], in0=ps[:, :ntk], scalar1=1.0 / nfeat, scalar2=eps,
          op0=ALU.mult, op1=ALU.add)
        A('sqrt', rstd[:, :ntk], rstd[:, :ntk])
        V('reciprocal', rstd[:, :ntk], rstd[:, :ntk])

    def rmsnorm_x(w_col, ntk, dst):
        sumsq_rstd([x[:, k, :ntk] for k in range(KC)], ntk, D, EPS)
        for k in range(KC):
            V('scalar_tensor_tensor', out=dst[:, k, :ntk], in0=x[:, k, :ntk], scalar=w_col[:, k:k + 1],
              in1=rstd[:, :ntk], op0=ALU.mult, op1=ALU.mult)

    def dense_fm(wv, cb, src, kc, ntk):
        ps = psr.next()
        for k in range(kc):
            MM(ps[:, :ntk], wv[:, k, cb * 128:(cb + 1) * 128], src(k), k == 0, k == kc - 1)
        return ps

    def fm_to_tok_store(src, nch, rows, dram2d):
        for g0 in range(0, nch, 4):
            ng = min(4, nch - g0)
            ps = psr.next()
            for i in range(ng):
                S.I('pe', 'transpose', ps[:rows, i * 128:(i + 1) * 128], src(g0 + i), ident, sig=(i == ng - 1))
            so = t1.next()
            A('activation', out=so[:rows, :ng * 128], in_=ps[:rows, :ng * 128], func=AF.Copy)
            S.dma('sp', dram2d[:, g0 * 128:(g0 + ng) * 128], so[:rows, :ng * 128])

    def tok_to_fm_load(dram2d, nch, rows, dst):
        for g0 in range(0, nch, 4):
            ng = min(4, nch - g0)
            lt = t1.next()
            S.dma('sp', lt[:rows, :ng * 128], dram2d[:, g0 * 128:(g0 + ng) * 128])
            ps = psr.next()
            for i in range(ng):
                S.I('pe', 'transpose', ps[:, i * rows:(i + 1) * rows], lt[:rows, i * 128:(i + 1) * 128],
                    ident[:rows, :rows], sig=(i == ng - 1))
            for i in range(ng):
                A('activation', out=dst(g0 + i), in_=ps[:, i * rows:(i + 1) * rows], func=AF.Copy)

    def conv_block(ps, ntk, kind, hist, newst, wcols, bcol, ntap, out_ap):
        h = ntap - 1
        sg = stg.next()
        if kind == 'p':
            A('activation', out=sg[:, h:h + ntk], in_=ps[:, :ntk], func=AF.Copy)
            V('tensor_copy', out=sg[:, 0:h], in_=hist)
            V('tensor_copy', out=newst, in_=sg[:, ntk:ntk + h])
            full = lambda j: sg[:, j:j + ntk]
            o = out_ap
        else:
            w = TS + h
            v3 = sg[:, 0:NSEQ * w].rearrange("p (b t) -> p b t", t=w)
            A('activation', out=v3[:, :, h:w], in_=ps[:, :ntk].rearrange("p (b t) -> p b t", t=TS), func=AF.Copy)
            V('tensor_copy', out=v3[:, :, 0:h], in_=hist)
            V('tensor_copy', out=newst, in_=v3[:, :, TS:w])
            full = lambda j: v3[:, :, j:j + TS]
            o = out_ap.rearrange("p (b t) -> p b t", t=TS)
        V('tensor_scalar', out=o, in0=full(0), scalar1=wcols[0], scalar2=bcol, op0=ALU.mult, op1=ALU.add)
        for j in range(1, ntap):
            V('scalar_tensor_tensor', out=o, in0=full(j), scalar=wcols[j], in1=o, op0=ALU.mult, op1=ALU.add)

    P_N1, P_N2, P_FCB, P_FCW, P_BM, P_SCW, P_SCB, P_SNW, P_SD = 0, 8, 16, 60, 192, 224, 272, 284, 292
    B_DTB, B_A = 0, 16

    def branch_out(l, bi, ysrc, kc, ntk, wname):
        for u in range(2):
            wb = loadw(din[wname][l], u * 512, 512)
            wm = loadw(din['w_merge'][l], bi * D + u * 512, 512)
            for cb in range(4):
                cc = u * 4 + cb
                psB = dense_fm(wb, cb, ysrc, kc, ntk)
                psG = dense_fm(wm, cb, lambda k: xn[:, k, :ntk], KC, ntk)
                tg = t1.next()
                A('activation', out=tg[:, :ntk], in_=psG[:, :ntk], func=AF.Sigmoid,
                  bias=par[:, P_BM + bi * 8 + cc:P_BM + bi * 8 + cc + 1])
                if S.mrg_first:
                    V('tensor_tensor', out=mrg[:, cc, :ntk], in0=tg[:, :ntk], in1=psB[:, :ntk], op=ALU.mult)
                else:
                    V('tensor_tensor', out=tg[:, :ntk], in0=tg[:, :ntk], in1=psB[:, :ntk], op=ALU.mult)
                    V('tensor_tensor', out=mrg[:, cc, :ntk], in0=mrg[:, cc, :ntk], in1=tg[:, :ntk], op=ALU.add)
        S.mrg_first = False

    def ssd_state_store(l, dram3):
        ps = psr.next()
        for j in range(2):
            for q in range(2):
                S.I('pe', 'transpose', ps[:, (j * 2 + q) * 128:(j * 2 + q + 1) * 128],
                    ST_ssd[l][:, j, q * 128:(q + 1) * 128], ident, sig=(j == 1 and q == 1))
        so = t1.next()
        A('activation', out=so[:, :512], in_=ps[:, :512], func=AF.Copy)
        for j in range(2):
            for g2 in range(2):
                for q in range(2):
                    h0 = 8 * j + 4 * g2 + 2 * q
                    S.dma('sp', dram3[h0:h0 + 2].rearrange("h p n -> (h p) n"),
                          so[:, (j * 2 + q) * 128 + g2 * 64:(j * 2 + q) * 128 + g2 * 64 + 64])

    def ssd_state_load(l, dram3):
        lt = t1.next()
        for j in range(2):
            for g2 in range(2):
                for q in range(2):
                    h0 = 8 * j + 4 * g2 + 2 * q
                    S.dma('sp', lt[:, (j * 2 + q) * 128 + g2 * 64:(j * 2 + q) * 128 + g2 * 64 + 64],
                          dram3[h0:h0 + 2].rearrange("h p n -> (h p) n"))
        ps = psr.next()
        for j in range(2):
            for q in range(2):
                S.I('pe', 'transpose', ps[:, (j * 2 + q) * 128:(j * 2 + q + 1) * 128],
                    lt[:, (j * 2 + q) * 128:(j * 2 + q + 1) * 128], ident, sig=(j == 1 and q == 1))
        A('activation', out=ST_ssd[l][:, :, :], in_=ps[:, :512].rearrange("p (j c) -> p j c", j=2), func=AF.Copy)

    ST_ssdb = S.sb([128, 2, 256], BF16, 'stssdb')

    def ssd_phase(l, kind, ntk, chunks, last):
        XCbv = PG[8][:, 1024:2048].bitcast(BF16)
        XCb = [XCbv[:, j * 512:(j + 1) * 512] for j in range(4)]
        XC = [PG[c // 4][:, (c % 4) * 512:(c % 4) * 512 + 512] for c in range(12)]
        szv = PG[3][:, :].bitcast(BF16)
        sz = [szv[:, k * 512:(k + 1) * 512] for k in range(8)]
        yss = [PG[4 + k // 4][:, (k % 4) * 512:(k % 4) * 512 + 512] for k in range(8)]
        for j in range(4):
            colload(par[:, P_SCW + 12 * j:P_SCW + 12 * j + 12], din['ssd_conv_w'][l, j])
        colload(par[:, P_SCB:P_SCB + 12], din['ssd_conv_b'][l])
        colload(par[:, P_SNW:P_SNW + 8], din['ssd_norm_w'][l])
        for h in range(16):
            S.dma('sp', par[(h % 2) * 64:(h % 2) * 64 + 64, P_SD + h // 2:P_SD + h // 2 + 1],
                  din['ssd_d'][l, h:h + 1].partition_broadcast(64))
        rowload(parb[:, B_DTB:B_DTB + 16], din['ssd_dt_bias'][l])
        rowload(parb[:, B_A:B_A + 16], din['ssd_a_log'][l])
        A('activation', out=parb[:, B_A:B_A + 16], in_=parb[:, B_A:B_A + 16], func=AF.Exp)
        V('tensor_scalar', out=parb[:, B_A:B_A + 16], in0=parb[:, B_A:B_A + 16], scalar1=-1.0, scalar2=None,
          op0=ALU.mult)
        for u in range(2):
            wv = loadw(din['w_in'][l], u * 512, 512)
            for cb in range(4):
                ps = dense_fm(wv, cb, lambda k: xn[:, k, :ntk], KC, ntk)
                tq = t1.next()
                A('activation', out=tq[:, :ntk], in_=ps[:, :ntk], func=AF.Sigmoid)
                V('tensor_tensor', out=sz[u * 4 + cb][:, :ntk], in0=tq[:, :ntk], in1=ps[:, :ntk], op=ALU.mult)
        if kind == 's':
            for b in range(NSEQ):
                pass
            tok_to_fm_load(din['state_ssd_conv'][l].rearrange("b j c -> (b j) c"), 12, NSEQ * 3,
                           lambda c: shs[:, c, :])
        for u in range(3):
            wv = loadw(din['w_in'][l], 1024 + u * 512, 512)
            for cb in range(4):
                c = u * 4 + cb
                ps = dense_fm(wv, cb, lambda k: xn[:, k, :ntk], KC, ntk)
                tq = t1.next()
                if kind == 'p':
                    hist, newst = cv_ssd[l][:, c, :], cv_ssd[l][:, c, :]
                else:
                    hist = newst = shs[:, c, :].rearrange("p (b j) -> p b j", j=3)
                conv_block(ps, ntk, kind, hist, newst, [par[:, P_SCW + 12 * j + c:P_SCW + 12 * j + c + 1] for j in range(4)],
                           par[:, P_SCB + c:P_SCB + c + 1], 4, tq[:, :ntk])
                tq2 = t1.next()
                A('activation', out=tq2[:, :ntk], in_=tq[:, :ntk], func=AF.Sigmoid)
                V('tensor_tensor', out=XC[c][:, :ntk], in0=tq2[:, :ntk], in1=tq[:, :ntk], op=ALU.mult)
                if c >= 8:
                    A('activation', out=XCb[c - 8][:, :ntk], in_=XC[c][:, :ntk], func=AF.Copy)
        if kind == 's':
            fm_to_tok_store(lambda c: shs[:, c, :], 12, NSEQ * 3,
                            dout['s_ssd_conv'][l].rearrange("b j c -> (b j) c"))
        elif last:
            fm_to_tok_store(lambda c: cv_ssd[l][:, c, :], 12, 3, dout['p_ssd_conv'][l])
        wdt = loadw(din['w_in'][l], 2560, 16)
        pg6, pg7, pg8 = PG[6], PG[7], PG[8]
        for ci, (c0, T) in enumerate(chunks):
            if kind == 's':
                ssd_state_load(l, din['state_ssd'][l, ci])
            dtv, dta, acum, wl = pg8[:T, 0:16], pg8[:T, 16:32], pg8[:T, 32:48], pg8[:T, 48:64]
            ETb = pg8[:, 64:80]
            Btok = pg8[:T, 128:256].bitcast(BF16)
            dtw = pg8[:T, 80:96]
            A('activation', out=ST_ssdb[:, :, :], in_=ST_ssd[l][:, :, :], func=AF.Copy)
            tY = pg8[:, 384:512]
            ps = psr.next()
            for k in range(KC):
                MM(ps[:T, 0:16], xn[:, k, c0:c0 + T], wdt[:, k, 0:16], k == 0, k == KC - 1)
            V('tensor_tensor', out=dtv, in0=ps[:T, 0:16], in1=parb[:T, B_DTB:B_DTB + 16], op=ALU.add)
            A('activation', out=dtv, in_=dtv, func=AF.Exp)
            V('tensor_scalar', out=dtv, in0=dtv, scalar1=1.0, scalar2=None, op0=ALU.add)
            A('activation', out=dtv, in_=dtv, func=AF.Ln)
            V('tensor_tensor', out=dta, in0=dtv, in1=parb[:T, B_A:B_A + 16], op=ALU.mult)
            ps = psr.next()
            MM(ps[:T, 0:16], mle[:T, :T], dta)
            MM(ps[:, 16:32], ones[:T, :], dta)
            A('activation', out=acum, in_=ps[:T, 0:16], func=AF.Copy)
            A('activation', out=ETb, in_=ps[:, 16:32], func=AF.Exp)
            V('tensor_tensor', out=wl, in0=ps[:T, 16:32], in1=acum, op=ALU.subtract)
            A('activation', out=wl, in_=wl, func=AF.Exp)
            XDT = pg7[:T, 0:512].bitcast(BF16)
            XDTW = pg7[:T, 512:1024].bitcast(BF16)
            V('tensor_tensor', out=dtw, in0=dtv, in1=wl, op=ALU.mult)
            for half in range(2):
                ps = psr.next()
                for i in range(4):
                    S.I('pe', 'transpose', ps[:T, i * 128:(i + 1) * 128], XC[half * 4 + i][:, c0:c0 + T], ident,
                        sig=(i == 3))
                V('tensor_tensor', out=XDT[:, half * 512:(half + 1) * 512].rearrange("t (h p) -> t h p", p=64),
                  in0=ps[:T, :512].rearrange("t (h p) -> t h p", p=64),
                  in1=dtv[:, half * 8:half * 8 + 8].unsqueeze(2).broadcast_to([T, 8, 64]), op=ALU.mult)
                V('tensor_tensor', out=XDTW[:, half * 512:(half + 1) * 512].rearrange("t (h p) -> t h p", p=64),
                  in0=ps[:T, :512].rearrange("t (h p) -> t h p", p=64),
                  in1=dtw[:, half * 8:half * 8 + 8].unsqueeze(2).broadcast_to([T, 8, 64]), op=ALU.mult)
            ps = psr.next()
            for i in range(2):
                S.I('pe', 'transpose', ps[:T, i * 128:(i + 1) * 128], XC[8 + i][:, c0:c0 + T], ident, sig=(i == 1))
            A('activation', out=Btok, in_=ps[:T, 0:256], func=AF.Copy)
            if T <= 32:
                HT = 16 * T
                D16 = pg6[:T, 0:HT].rearrange("s (h t) -> s h t", h=16)
                E16 = pg6[:T, 512:512 + HT].rearrange("s (h t) -> s h t", h=16)
                EB16 = pg6[:, 1024:1024 + HT]
                M16f = pg6[:T, 1536:1536 + HT // 2].bitcast(BF16)
                M16 = M16f.rearrange("s (h t) -> s h t", h=16)
                V('tensor_tensor', out=D16, in0=dta[:, 0:16].unsqueeze(2).broadcast_to([T, 16, T]),
                  in1=mle[:T, :T].unsqueeze(1).broadcast_to([T, 16, T]), op=ALU.mult)
                psAB = psr.next()
                MM(psAB[:, 0:HT], ones[:T, :], pg6[:T, 0:HT])
                A('activation', out=EB16, in_=psAB[:, 0:HT], func=AF.Exp)
                V('tensor_tensor', out=E16, in0=psAB[:T, 0:HT].rearrange("s (h t) -> s h t", h=16),
                  in1=acum[:, 0:16].unsqueeze(2).broadcast_to([T, 16, T]), op=ALU.subtract)
                V('tensor_tensor', out=E16, in0=E16, in1=negm[:T, :T].unsqueeze(1).broadcast_to([T, 16, T]), op=ALU.add)
                A('activation', out=E16, in_=E16, func=AF.Exp)
                psGp = [psr.next(), psr.next()]
                for g in range(4):
                    gp = slice((g % 2) * 64, (g % 2) * 64 + 64)
                    MM(psGp[g % 2][:T, (g // 2) * T:(g // 2 + 1) * T], XCb[g // 2][gp, c0:c0 + T], XCb[2 + g // 2][gp, c0:c0 + T],
                       sg=(g >= 2))
                E5 = pg6[:T, 512:512 + HT].rearrange("s (j q h t) -> s j q h t", j=2, q=2, h=4)
                M5 = M16f.rearrange("s (j q h t) -> s j q h t", j=2, q=2, h=4)
                for q in range(2):
                    V('tensor_tensor', out=M5[:, :, q, :, :], in0=E5[:, :, q, :, :],
                      in1=psGp[q][:T, 0:2 * T].rearrange("s (j t) -> s j t", j=2).unsqueeze(2).broadcast_to([T, 2, 4, T]),
                      op=ALU.mult)
                psOp = [psr.next(), psr.next()]
                psY = psr.next()
                for g in range(4):
                    gp = slice((g % 2) * 64, (g % 2) * 64 + 64)
                    for hp in range(2):
                        slot = (g // 2) * 2 + hp
                        MM(psOp[g % 2][:, slot * T:(slot + 1) * T], ST_ssdb[gp, g // 2, hp * 128:(hp + 1) * 128],
                           XCb[2 + g // 2][gp, c0:c0 + T], sg=(g >= 2 and hp == 1))
                for h in range(16):
                    MM(psY[:, h * T:(h + 1) * T], XDT[:, (h // 2) * 128:(h // 2 + 1) * 128], M16[:, h, :], sg=(h == 15))
                EB6 = EB16.rearrange("p (j q i h t) -> p j q i h t", j=2, q=2, i=2, h=2)
                Y6 = psY[:, 0:HT].rearrange("p (j q i h t) -> p j q i h t", j=2, q=2, i=2, h=2)
                for q in range(2):
                    O4 = psOp[q][:, 0:4 * T].rearrange("p (j i t) -> p j i t", j=2, i=2)
                    for h2 in range(2):
                        hv = slice(h2 * 64, h2 * 64 + 64)
                        for j in range(2):
                            tYv = tY[hv, 0:2 * T].rearrange("p (i t) -> p i t", i=2)
                            V('tensor_tensor', out=tYv, in0=O4[hv, j, :, :], in1=EB6[hv, j, q, :, h2, :], op=ALU.mult)
                            dstv = PG[4 + j][hv, :].rearrange("p (k n) -> p k n", k=4)[:, 2 * q:2 * q + 2, c0:c0 + T]
                            V('tensor_tensor', out=dstv, in0=tYv, in1=Y6[hv, j, q, :, h2, :], op=ALU.add)
                for j in range(2):
                    tq = t1.next()
                    tqv = tq[:, 0:4 * T].rearrange("p (k t) -> p k t", k=4)
                    xcv = PG[j][:, :].rearrange("p (k n) -> p k n", k=4)[:, :, c0:c0 + T]
                    ysv = PG[4 + j][:, :].rearrange("p (k n) -> p k n", k=4)[:, :, c0:c0 + T]
                    V('tensor_tensor', out=tqv, in0=xcv, in1=par[:, P_SD + 4 * j:P_SD + 4 * j + 4].unsqueeze(2).broadcast_to([128, 4, T]),
                      op=ALU.mult)
                    V('tensor_tensor', out=ysv, in0=ysv, in1=tqv, op=ALU.add)
                for gq in range(4):
                    gp = slice((gq % 2) * 64, (gq % 2) * 64 + 64)
                    psS = psr.next()
                    MM(psS[:, 0:256], Btok[:, (gq // 2) * 128:(gq // 2 + 1) * 128], XDTW[:, gq * 256:(gq + 1) * 256])
                    STv = ST_ssd[l][gp, gq // 2, :]
                    V('tensor_tensor', out=STv.rearrange("n (h p) -> n h p", p=64),
                      in0=STv.rearrange("n (h p) -> n h p", p=64),
                      in1=ETb[gp, 4 * gq:4 * gq + 4].unsqueeze(2).broadcast_to([64, 4, 64]), op=ALU.mult)
                    V('tensor_tensor', out=STv, in0=STv, in1=psS[gp, 0:256], op=ALU.add)
            else:
                for gq in range(4):
                    D4 = pg6[:T, 0:4 * T].rearrange("s (h t) -> s h t", h=4)
                    E4 = pg6[:T, 512:512 + 4 * T].rearrange("s (h t) -> s h t", h=4)
                    EB4 = pg6[:, 1024:1024 + 4 * T].rearrange("s (h t) -> s h t", h=4)
                    M4 = pg6[:T, 1536:1536 + 2 * T].bitcast(BF16).rearrange("s (h t) -> s h t", h=4)
                    V('tensor_tensor', out=D4, in0=dta[:, 4 * gq:4 * gq + 4].unsqueeze(2).broadcast_to([T, 4, T]),
                      in1=mle[:T, :T].unsqueeze(1).broadcast_to([T, 4, T]), op=ALU.mult)
                    psAB = psr.next()
                    MM(psAB[:, 0:4 * T], ones[:T, :], pg6[:T, 0:4 * T])
                    A('activation', out=EB4, in_=psAB[:, 0:4 * T].rearrange("s (h t) -> s h t", h=4), func=AF.Exp)
                    V('tensor_tensor', out=E4, in0=psAB[:T, 0:4 * T].rearrange("s (h t) -> s h t", h=4),
                      in1=acum[:, 4 * gq:4 * gq + 4].unsqueeze(2).broadcast_to([T, 4, T]), op=ALU.subtract)
                    V('tensor_tensor', out=E4, in0=E4, in1=negm[:T, :T].unsqueeze(1).broadcast_to([T, 4, T]), op=ALU.add)
                    A('activation', out=E4, in_=E4, func=AF.Exp)
                    gp = slice((gq % 2) * 64, (gq % 2) * 64 + 64)
                    Bfm = XCb[gq // 2][gp, c0:c0 + T]
                    Cfm = XCb[2 + gq // 2][gp, c0:c0 + T]
                    psG = psr.next()
                    MM(psG[:T, :T], Bfm, Cfm)
                    V('tensor_tensor', out=M4, in0=E4, in1=psG[:T, :T].unsqueeze(1).broadcast_to([T, 4, T]), op=ALU.mult)
                    for hp in range(2):
                        k = 2 * gq + hp
                        psO = psr.next()
                        MM(psO[:, :T], ST_ssdb[gp, gq // 2, hp * 128:(hp + 1) * 128], Cfm)
                        for h2 in range(2):
                            hl = 2 * hp + h2
                            psY = psr.next()
                            MM(psY[:, :T], XDT[:, k * 128:(k + 1) * 128], M4[:, hl, :])
                            hv = slice(h2 * 64, h2 * 64 + 64)
                            V('tensor_tensor', out=tY[hv, :T], in0=psO[hv, :T], in1=EB4[hv, hl, :], op=ALU.mult)
                            V('tensor_tensor', out=yss[k][hv, c0:c0 + T], in0=tY[hv, :T], in1=psY[hv, :T], op=ALU.add)
                        V('scalar_tensor_tensor', out=yss[k][:, c0:c0 + T], in0=XC[k][:, c0:c0 + T],
                          scalar=par[:, P_SD + k:P_SD + k + 1], in1=yss[k][:, c0:c0 + T], op0=ALU.mult, op1=ALU.add)
                    psS = psr.next()
                    MM(psS[:, 0:256], Btok[:, (gq // 2) * 128:(gq // 2 + 1) * 128], XDTW[:, gq * 256:(gq + 1) * 256])
                    STv = ST_ssd[l][gp, gq // 2, :]
                    V('tensor_tensor', out=STv.rearrange("n (h p) -> n h p", p=64),
                      in0=STv.rearrange("n (h p) -> n h p", p=64),
                      in1=ETb[gp, 4 * gq:4 * gq + 4].unsqueeze(2).broadcast_to([64, 4, 64]), op=ALU.mult)
                    V('tensor_tensor', out=STv, in0=STv, in1=psS[gp, 0:256], op=ALU.add)
            if kind == 's':
                ssd_state_store(l, dout['s_ssd'][l, ci])
        if kind == 'p' and last:
            ssd_state_store(l, dout['p_ssd'][l])
        for k in range(8):
            V('tensor_tensor', out=yss[k][:, :ntk], in0=yss[k][:, :ntk], in1=sz[k][:, :ntk], op=ALU.mult)
        sumsq_rstd([yss[k][:, :ntk] for k in range(8)], ntk, 1024, EPS)
        for k in range(8):
            V('scalar_tensor_tensor', out=sz[k][:, :ntk], in0=yss[k][:, :ntk], scalar=par[:, P_SNW + k:P_SNW + k + 1],
              in1=rstd[:, :ntk], op0=ALU.mult, op1=ALU.mult)
        branch_out(l, 0, lambda k: sz[k][:, :ntk], 8, ntk, 'w_br_ssd')

    ST_hg = [S.sb([128, 4, 128], F32, 'sthg%d' % l) for l in range(L)]
    LB = S.sb([128, L, 4], F32, 'lb')
    lbtmp = S.sb([128, L + 2, 4], F32, 'lbtmp')
    for l in range(L):
        V('memset', ST_hg[l][:], 0.0)
        colload(lbtmp[:, l, :], din['hg_lb_raw'][l])
    A('activation', out=lbtmp[:, 0:L, :], in_=lbtmp[:, 0:L, :], func=AF.Exp)
    V('tensor_copy', out=lbtmp[:, L, :], in_=lbtmp[:, 0, :])
    for l in range(1, L):
        V('tensor_tensor', out=lbtmp[:, L, :], in0=lbtmp[:, L, :], in1=lbtmp[:, l, :], op=ALU.add)
    V('reciprocal', lbtmp[:, L + 1, :], lbtmp[:, L, :])
    V('memset', LB[:, 0, :], 0.0)
    for l in range(1, L):
        V('tensor_tensor', out=lbtmp[:, l, :], in0=lbtmp[:, l, :], in1=lbtmp[:, L + 1, :], op=ALU.mult)
        V('tensor_tensor', out=LB[:, l, :], in0=LB[:, l - 1, :], in1=lbtmp[:, l, :], op=ALU.add)
    OML = S.sb([128, L, 4], F32, 'oml')
    V('tensor_scalar', out=OML[:], in0=LB[:], scalar1=-1.0, scalar2=1.0, op0=ALU.mult, op1=ALU.add)
    P_HNW = 304

    def hg_phase(l, kind, ntk, chunks, last):
        def pgv(i):
            return PG[i][:, :].rearrange("p (h n) -> p h n", h=4)
        QT, KT, BB, VV, OO, GS, EBt = (pgv(i) for i in range(7))
        pg7 = PG[7]
        ybv = PG[8][:, :].bitcast(BF16)
        ybf = [ybv[:, h * 512:(h + 1) * 512] for h in range(4)]
        rmask = cst[:, C_R64:C_R64 + 512] if kind == 'p' else cst[:, C_R8:C_R8 + 128]
        colload(par[:, P_HNW:P_HNW + 1], din['hg_norm_w'][l])
        base = 3088
        wv = loadw(din['w_in'][l], base, 512)
        for h in range(4):
            ps = dense_fm(wv, h, lambda k: xn[:, k, :ntk], KC, ntk)
            tq = t1.next()
            A('activation', out=tq[:, :ntk], in_=ps[:, :ntk], func=AF.Sigmoid)
            V('tensor_tensor', out=QT[:, h, :ntk], in0=tq[:, :ntk], in1=ps[:, :ntk], op=ALU.mult)
        wv = loadw(din['w_in'][l], base + 512, 512)
        for h in range(4):
            ps = dense_fm(wv, h, lambda k: xn[:, k, :ntk], KC, ntk)
            tq = t1.next()
            A('activation', out=tq[:, :ntk], in_=ps[:, :ntk], func=AF.Sigmoid)
            V('tensor_scalar', out=tq[:, :ntk], in0=tq[:, :ntk], scalar1=OML[:, l, h:h + 1], scalar2=LB[:, l, h:h + 1],
              op0=ALU.mult, op1=ALU.add)
            V('tensor_scalar', out=KT[:, h, :ntk], in0=tq[:, :ntk], scalar1=-1.0, scalar2=1.0, op0=ALU.mult, op1=ALU.add)
            A('activation', out=tq[:, :ntk], in_=tq[:, :ntk], func=AF.Ln)
            V('tensor_tensor_scan', out=BB[:, h, :ntk], data0=rmask[:, :ntk], data1=tq[:, :ntk], initial=0.0,
              op0=ALU.mult, op1=ALU.add)
            A('activation', out=EBt[:, h, :ntk], in_=BB[:, h, :ntk], func=AF.Exp)
            V('tensor_tensor', out=QT[:, h, :ntk], in0=QT[:, h, :ntk], in1=EBt[:, h, :ntk], op=ALU.mult)
            tq2 = t1.next()
            V('tensor_scalar', out=tq2[:, :ntk], in0=BB[:, h, :ntk], scalar1=-1.0, scalar2=80.0, op0=ALU.mult, op1=ALU.min)
            A('activation', out=tq2[:, :ntk], in_=tq2[:, :ntk], func=AF.Exp)
            V('tensor_tensor', out=KT[:, h, :ntk], in0=KT[:, h, :ntk], in1=tq2[:, :ntk], op=ALU.mult)
        wv = loadw(din['w_in'][l], base + 1024, 512)
        for h in range(4):
            ps = dense_fm(wv, h, lambda k: xn[:, k, :ntk], KC, ntk)
            A('activation', out=VV[:, h, :ntk], in_=ps[:, :ntk], func=AF.Copy)
        wv = loadw(din['w_in'][l], base + 1536, 512)
        for h in range(4):
            ps = dense_fm(wv, h, lambda k: xn[:, k, :ntk], KC, ntk)
            A('activation', out=GS[:, h, :ntk], in_=ps[:, :ntk], func=AF.Sigmoid)
        for ci, (c0, T) in enumerate(chunks):
            if kind == 's':
                S.dma('sp', ST_hg[l][:, :, :], din['state_hgrn'][l, ci].rearrange("h k v -> k h v"))
            SC = pg7[:T, 0:4 * T].rearrange("s (h t) -> s h t", h=4)
            Vtok = pg7[:T, 256:768]
            Ktok = pg7[:T, 768:1280]
            psS = psr.next()
            for h in range(4):
                MM(psS[:T, h * T:(h + 1) * T], KT[:, h, c0:c0 + T], QT[:, h, c0:c0 + T], sg=(h == 3))
            V('tensor_tensor', out=SC, in0=psS[:T, 0:4 * T].rearrange("s (h t) -> s h t", h=4),
              in1=mle[:T, :T].unsqueeze(1).broadcast_to([T, 4, T]), op=ALU.mult)
            psV = psr.next()
            for h in range(4):
                S.I('pe', 'transpose', psV[:T, h * 128:(h + 1) * 128], VV[:, h, c0:c0 + T], ident, sig=(h == 3))
            A('activation', out=Vtok, in_=psV[:T, :512], func=AF.Copy)
            psK = psr.next()
            for h in range(4):
                S.I('pe', 'transpose', psK[:T, h * 128:(h + 1) * 128], KT[:, h, c0:c0 + T], ident, sig=(h == 3))
            A('activation', out=Ktok, in_=psK[:T, :512], func=AF.Copy)
            for h in range(4):
                psO = psr.next()
                MM(psO[:, :T], Vtok[:, h * 128:(h + 1) * 128], SC[:, h, :], True, False)
                MM(psO[:, :T], ST_hg[l][:, h, :], QT[:, h, c0:c0 + T], False, True)
                A('activation', out=OO[:, h, c0:c0 + T], in_=psO[:, :T], func=AF.Copy)
                psU = psr.next()
                MM(psU[:, :128], Ktok[:, h * 128:(h + 1) * 128], Vtok[:, h * 128:(h + 1) * 128])
                V('tensor_tensor', out=ST_hg[l][:, h, :], in0=ST_hg[l][:, h, :], in1=psU[:, :128], op=ALU.add)
                V('tensor_scalar', out=ST_hg[l][:, h, :], in0=ST_hg[l][:, h, :],
                  scalar1=EBt[:, h, c0 + T - 1:c0 + T], scalar2=None, op0=ALU.mult)
            if kind == 's':
                S.dma('sp', dout['s_hgrn'][l, ci].rearrange("h k v -> k h v"), ST_hg[l][:, :, :])
        if kind == 'p' and last:
            S.dma('sp', dout['p_hgrn'][l].rearrange("h k v -> k h v"), ST_hg[l][:, :, :])
        for h in range(4):
            sumsq_rstd([OO[:, h, :ntk]], ntk, 128, EPS)
            tq = t1.next()
            V('scalar_tensor_tensor', out=tq[:, :ntk], in0=OO[:, h, :ntk], scalar=par[:, P_HNW:P_HNW + 1],
              in1=rstd[:, :ntk], op0=ALU.mult, op1=ALU.mult)
            V('tensor_tensor', out=ybf[h][:, :ntk], in0=tq[:, :ntk], in1=GS[:, h, :ntk], op=ALU.mult)
        branch_out(l, 2, lambda k: ybf[k][:, :ntk], 4, ntk, 'w_br_hg')

    import math
    PI = math.pi
    hS = [S.sb([128, 2, 16], F32, 'hs5_%d' % l) for l in range(L)]
    for l in range(L):
        V('memset', hS[l][:], 0.0)
    hs_s = S.sb([128, 2, 16, NSEQ], F32, 'hs5s')
    s5p = S.sb([128, 16, 16], F32, 's5p')
    ubuf = S.sb([128, 4, NTKMAX], F32, 'ubuf')
    P_S5D, P_GLB = 308, 312

    def s5_phase(l, kind, ntk, chunks, last):
        TC = 64
        pos = cst[:, C_POS:C_POS + 64]
        TF_re = PG[0][:, 0:1024].rearrange("p (c t) -> p c t", c=16)
        TF_im = PG[0][:, 1024:2048].rearrange("p (c t) -> p c t", c=16)
        TI_re = PG[1][:, 0:1024].rearrange("p (c t) -> p c t", c=16)
        TI_im = PG[1][:, 1024:2048].rearrange("p (c t) -> p c t", c=16)
        X1 = PG[2][:, 0:1024].rearrange("p (c t) -> p c t", c=16)
        X2 = PG[2][:, 1024:2048].rearrange("p (c t) -> p c t", c=16)
        a_re, a_im, dtc, da_re, da_im, den, q_re, q_im, tm1, tm2, tm3 = (s5p[:, i, :] for i in range(11))
        S.dma('sp', a_re, din['s5_a_re'][l].rearrange("(c g) n -> (g n) c", g=2))
        S.dma('sp', a_im, din['s5_a_im'][l].rearrange("(c g) n -> (g n) c", g=2))
        for g2 in range(2):
            S.dma('sp', s5p[g2 * 64:(g2 + 1) * 64, 2, :],
                  din['s5_log_dt'][l].rearrange("(c g) -> g c", g=2)[g2].partition_broadcast(64))
        colload(par[:, P_S5D:P_S5D + 4], din['s5_d'][l])
        colload(par[:, P_GLB:P_GLB + 4], din['s5_glu_b'][l])
        A('activation', out=dtc, in_=dtc, func=AF.Exp)
        V('tensor_tensor', out=da_re, in0=dtc, in1=a_re, op=ALU.mult)
        V('tensor_tensor', out=da_im, in0=dtc, in1=a_im, op=ALU.mult)
        bc = lambda v: v.unsqueeze(2).broadcast_to([128, 16, 64])
        posb = pos.unsqueeze(1).broadcast_to([128, 16, 64])
        V('tensor_tensor', out=X1, in0=bc(da_re), in1=posb, op=ALU.mult)
        A('activation', out=TF_re, in_=X1, func=AF.Exp)
        A('activation', out=TI_re, in_=X1, func=AF.Exp, scale=-1.0)
        V('tensor_tensor', out=X2, in0=bc(da_im), in1=posb, op=ALU.mult)
        I32 = mybir.dt.int32
        XI = PG[8][:, 0:1024].rearrange("p (c t) -> p c t", c=16).bitcast(I32)
        XF = PG[8][:, 1024:2048].rearrange("p (c t) -> p c t", c=16)

        def rred(dst, src, shift):
            V('tensor_scalar', out=dst, in0=src, scalar1=shift, scalar2=None, op0=ALU.add)
            V('tensor_scalar', out=XF, in0=dst, scalar1=1.0 / (2 * PI), scalar2=None, op0=ALU.mult)
            V('tensor_copy', out=XI, in_=XF)
            V('tensor_copy', out=XF, in_=XI)
            V('scalar_tensor_tensor', out=dst, in0=XF, scalar=-2 * PI, in1=dst, op0=ALU.mult, op1=ALU.add)
            V('tensor_scalar', out=XF, in0=dst, scalar1=PI, scalar2=2 * PI, op0=ALU.is_gt, op1=ALU.mult)
            V('tensor_tensor', out=dst, in0=dst, in1=XF, op=ALU.subtract)
            V('tensor_scalar', out=XF, in0=dst, scalar1=-PI, scalar2=2 * PI, op0=ALU.is_lt, op1=ALU.mult)
            V('tensor_tensor', out=dst, in0=dst, in1=XF, op=ALU.add)
        rred(X1, X2, 0.0)
        A('activation', out=X1, in_=X1, func=AF.Sin)
        rred(X2, X2, 0.5 * PI)
        A('activation', out=X2, in_=X2, func=AF.Sin)
        V('tensor_tensor', out=TF_im, in0=TF_re, in1=X1, op=ALU.mult)
        V('tensor_tensor', out=TF_re, in0=TF_re, in1=X2, op=ALU.mult)
        V('tensor_tensor', out=TI_im, in0=TI_re, in1=X1, op=ALU.mult)
        V('tensor_scalar', out=TI_im, in0=TI_im, scalar1=-1.0, scalar2=None, op0=ALU.mult)
        V('tensor_tensor', out=TI_re, in0=TI_re, in1=X2, op=ALU.mult)
        ab_re, ab_im = TF_re[:, :, 0], TF_im[:, :, 0]
        V('tensor_tensor', out=den, in0=a_re, in1=a_re, op=ALU.mult)
        V('tensor_tensor', out=tm1, in0=a_im, in1=a_im, op=ALU.mult)
        V('tensor_tensor', out=den, in0=den, in1=tm1, op=ALU.add)
        V('reciprocal', den, den)
        V('tensor_scalar', out=tm1, in0=ab_re, scalar1=-1.0, scalar2=None, op0=ALU.add)
        V('tensor_tensor', out=tm2, in0=tm1, in1=a_re, op=ALU.mult)
        V('tensor_tensor', out=tm3, in0=ab_im, in1=a_im, op=ALU.mult)
        V('tensor_tensor', out=tm2, in0=tm2, in1=tm3, op=ALU.add)
        V('tensor_tensor', out=q_re, in0=tm2, in1=den, op=ALU.mult)
        V('tensor_tensor', out=tm2, in0=ab_im, in1=a_re, op=ALU.mult)
        V('tensor_tensor', out=tm3, in0=tm1, in1=a_im, op=ALU.mult)
        V('tensor_tensor', out=tm2, in0=tm2, in1=tm3, op=ALU.subtract)
        V('tensor_tensor', out=q_im, in0=tm2, in1=den, op=ALU.mult)
        pg7 = PG[7]
        b_re = pg7[:, 0:256].rearrange("p (c j) -> p c j", c=16)
        b_im = pg7[:, 256:512].rearrange("p (c j) -> p c j", c=16)
        c_re = pg7[:, 512:768].rearrange("p (m n) -> p m n", m=4)
        c_im = pg7[:, 768:1024].rearrange("p (m n) -> p m n", m=4)
        bb_re = pg7[:, 1024:1280].rearrange("p (c j) -> p c j", c=16)
        bb_im = pg7[:, 1280:1536].rearrange("p (c j) -> p c j", c=16)
        tb = pg7[:, 1536:1792].rearrange("p (c j) -> p c j", c=16)
        S.dma('sp', b_re, din['s5_b_re'][l].rearrange("(c g) n j -> (g n) c j", g=2))
        S.dma('sp', b_im, din['s5_b_im'][l].rearrange("(c g) n j -> (g n) c j", g=2))
        S.dma('sp', c_re, din['s5_c_re'][l].rearrange("(m g) j n -> (g j) m n", g=8))
        S.dma('sp', c_im, din['s5_c_im'][l].rearrange("(m g) j n -> (g j) m n", g=8))
        qb = lambda v: v.unsqueeze(2).broadcast_to([128, 16, 16])
        V('tensor_tensor', out=bb_re, in0=b_re, in1=qb(q_re), op=ALU.mult)
        V('tensor_tensor', out=tb, in0=b_im, in1=qb(q_im), op=ALU.mult)
        V('tensor_tensor', out=bb_re, in0=bb_re, in1=tb, op=ALU.subtract)
        V('tensor_tensor', out=bb_im, in0=b_im, in1=qb(q_re), op=ALU.mult)
        V('tensor_tensor', out=tb, in0=b_re, in1=qb(q_im), op=ALU.mult)
        V('tensor_tensor', out=bb_im, in0=bb_im, in1=tb, op=ALU.add)
        V('tensor_scalar', out=c_im, in0=c_im, scalar1=-1.0, scalar2=None, op0=ALU.mult)
        mskB = cst[:, C_MSKB:C_MSKB + 32]
        mskC = cst[:, C_MSKC:C_MSKC + 8]
        BL = {'bre': PG[3], 'bim': PG[4], 'cre': PG[5], 'cim': PG[6]}
        big = Ring([S5BIG0, S5BIG1])
        for nm, srcv in (('bre', bb_re), ('bim', bb_im)):
            for c0_ in range(0, 16, 4):
                ps = psr.next()
                for c in range(c0_, c0_ + 4):
                    yb = big.next()
                    V('tensor_tensor', out=yb[:, :].rearrange("p (g j) -> p g j", g=8),
                      in0=srcv[:, c, :].unsqueeze(1).broadcast_to([128, 8, 16]),
                      in1=mskB[:, (c % 4) * 8:(c % 4) * 8 + 8].unsqueeze(2).broadcast_to([128, 8, 16]), op=ALU.mult)
                    S.I('pe', 'transpose', ps[:, (c - c0_) * 128:(c - c0_ + 1) * 128], yb[:, :], ident, sig=True)
                A('activation', out=BL[nm][:, c0_ * 128:(c0_ + 4) * 128], in_=ps[:, :], func=AF.Copy)
        for nm, srcv in (('cre', c_re), ('cim', c_im)):
            for m in range(4):
                ps = psr.next()
                for i in range(4):
                    yb = big.next()
                    V('tensor_tensor', out=yb[:, :].rearrange("p (g n) -> p g n", g=2),
                      in0=srcv[:, m, :].unsqueeze(1).broadcast_to([128, 2, 64]),
                      in1=mskC[:, i * 2:i * 2 + 2].unsqueeze(2).broadcast_to([128, 2, 64]), op=ALU.mult)
                    S.I('pe', 'transpose', ps[:, i * 128:(i + 1) * 128], yb[:, :], ident, sig=True)
                A('activation', out=BL[nm][:, m * 512:(m + 1) * 512], in_=ps[:, :], func=AF.Copy)
        wv = loadw(din['w_in'][l], 2576, 512)
        for m in range(4):
            ps = dense_fm(wv, m, lambda k: xn[:, k, :ntk], KC, ntk)
            A('activation', out=ubuf[:, m, :ntk], in_=ps[:, :ntk], func=AF.Copy)
        if kind == 's':
            tok_to_fm_load(din['state_s5_re'][l].rearrange("b g n -> b (g n)"), 16, NSEQ, lambda c: hs_s[:, 0, c, :])
            tok_to_fm_load(din['state_s5_im'][l].rearrange("b g n -> b (g n)"), 16, NSEQ, lambda c: hs_s[:, 1, c, :])
        if kind == 's':
            NB = min(8, NSEQ)
            chunks = [(g * NB * TS, NB * TS) for g in range(NSEQ // NB)]
            Tt = TS
        else:
            NB = 1
            Tt = None
        for ci, (c0, T) in enumerate(chunks):
            n = 16 * T
            tt = Tt or T
            v4 = lambda ap: ap.rearrange("p c (b t) -> p c b t", t=tt)
            tA = PG[2][:, 0:n].rearrange("p (c t) -> p c t", c=16)
            tB = PG[2][:, 1024:1024 + n].rearrange("p (c t) -> p c t", c=16)
            W_re = PG[8][:, 0:n].rearrange("p (c t) -> p c t", c=16)
            W_im = PG[8][:, 1024:1024 + n].rearrange("p (c t) -> p c t", c=16)
            tb4 = lambda tab: tab[:, :, :tt].unsqueeze(2).broadcast_to([128, 16, T // tt, tt])
            tfr, tfi, tir, tii = tb4(TF_re), tb4(TF_im), tb4(TI_re), tb4(TI_im)
            if kind == 'p':
                hin_re, hin_im = hS[l][:, 0, :].unsqueeze(2), hS[l][:, 1, :].unsqueeze(2)
            else:
                hin_re, hin_im = hs_s[:, 0, :, ci * NB:(ci + 1) * NB], hs_s[:, 1, :, ci * NB:(ci + 1) * NB]
            nb = (n + 511) // 512
            pre = [psr.next() for _ in range(nb)]
            pim = [psr.next() for _ in range(nb)]
            for c in range(16):
                o = c * T
                MM(pre[o // 512][:, o % 512:o % 512 + T], BL['bre'][:, c * 128:(c + 1) * 128], ubuf[:, c // 4, c0:c0 + T],
                   sg=True)
                MM(pim[o // 512][:, o % 512:o % 512 + T], BL['bim'][:, c * 128:(c + 1) * 128], ubuf[:, c // 4, c0:c0 + T],
                   sg=True)
            cpb = 512 // T if n > 512 else 16
            for bb_ in range(nb):
                cs = slice(bb_ * cpb, (bb_ + 1) * cpb)
                w_ = min(512, n)
                pv = lambda p: p[:, 0:w_].rearrange("p (c b t) -> p c b t", b=T // tt, t=tt)
                V('tensor_tensor', out=v4(tA[:, cs, :]), in0=pv(pre[bb_]), in1=tir[:, cs], op=ALU.mult)
                V('tensor_tensor', out=v4(tB[:, cs, :]), in0=pv(pim[bb_]), in1=tii[:, cs], op=ALU.mult)
                V('tensor_tensor', out=W_re[:, cs, :], in0=tA[:, cs, :], in1=tB[:, cs, :], op=ALU.subtract)
                V('tensor_tensor', out=v4(tA[:, cs, :]), in0=pv(pim[bb_]), in1=tir[:, cs], op=ALU.mult)
                V('tensor_tensor', out=v4(tB[:, cs, :]), in0=pv(pre[bb_]), in1=tii[:, cs], op=ALU.mult)
                V('tensor_tensor', out=W_im[:, cs, :], in0=tA[:, cs, :], in1=tB[:, cs, :], op=ALU.add)
            rm = cst[:, C_R64:C_R64 + 512] if kind == 'p' else cst[:, C_R8L:C_R8L + 512]
            for (Wv, off) in ((PG[8], 0), (PG[8], 1024)):
                for b in range(nb):
                    w_ = min(512, n)
                    seg = Wv[:, off + b * 512:off + b * 512 + w_]
                    dst = PG[2][:, off + b * 512:off + b * 512 + w_]
                    V('tensor_tensor_scan', out=dst, data0=rm[:, :w_], data1=seg, initial=0.0, op0=ALU.mult, op1=ALU.add)
            G_re, G_im = tA, tB
            hb4 = lambda hh: hh.unsqueeze(3).broadcast_to([128, 16, T // tt, tt])
            V('tensor_tensor', out=v4(G_re), in0=v4(G_re), in1=hb4(hin_re), op=ALU.add)
            V('tensor_tensor', out=v4(G_im), in0=v4(G_im), in1=hb4(hin_im), op=ALU.add)
            H_re, H_im = W_re, W_im
            V('tensor_tensor', out=v4(H_re), in0=v4(G_re), in1=tfr, op=ALU.mult)
            V('tensor_tensor', out=v4(H_im), in0=v4(G_im), in1=tfi, op=ALU.mult)
            V('tensor_tensor', out=H_re, in0=H_re, in1=H_im, op=ALU.subtract)
            V('tensor_tensor', out=v4(H_im), in0=v4(G_im), in1=tfr, op=ALU.mult)
            V('tensor_tensor', out=v4(G_re), in0=v4(G_re), in1=tfi, op=ALU.mult)
            V('tensor_tensor', out=H_im, in0=H_im, in1=G_re, op=ALU.add)
            V('tensor_copy', out=hin_re, in_=v4(H_re)[:, :, :, tt - 1])
            V('tensor_copy', out=hin_im, in_=v4(H_im)[:, :, :, tt - 1])
            for m in range(4):
                psY = psr.next()
                for i in range(4):
                    c = 4 * m + i
                    MM(psY[:, :T], BL['cre'][:, c * 128:(c + 1) * 128], H_re[:, c, :], i == 0, False)
                    MM(psY[:, :T], BL['cim'][:, c * 128:(c + 1) * 128], H_im[:, c, :], False, i == 3)
                V('scalar_tensor_tensor', out=ubuf[:, m, c0:c0 + T], in0=ubuf[:, m, c0:c0 + T],
                  scalar=par[:, P_S5D + m:P_S5D + m + 1], in1=psY[:, :T], op0=ALU.mult, op1=ALU.add)
        if kind == 's':
            fm_to_tok_store(lambda c: hs_s[:, 0, c, :], 16, NSEQ, dout['s_s5_re'][l].rearrange("b g n -> b (g n)"))
            fm_to_tok_store(lambda c: hs_s[:, 1, c, :], 16, NSEQ, dout['s_s5_im'][l].rearrange("b g n -> b (g n)"))
        elif last:
            fm_to_tok_store(lambda c: hS[l][:, 0, c:c + 1], 16, 1, dout['p_s5_re'][l].rearrange("(o g) n -> o (g n)", o=1))
            fm_to_tok_store(lambda c: hS[l][:, 1, c:c + 1], 16, 1, dout['p_s5_im'][l].rearrange("(o g) n -> o (g n)", o=1))
        ygb = PG[2][:, :].bitcast(BF16)
        wg = loadw(din['s5_glu_w'][l], 0, 512)
        for m in range(4):
            u1 = t1.next()
            g = ubuf[:, m, :ntk]
            A('activation', out=u1[:, :ntk], in_=g, func=AF.Square)
            V('tensor_scalar', out=u1[:, :ntk], in0=u1[:, :ntk], scalar1=0.044715, scalar2=1.0, op0=ALU.mult, op1=ALU.add)
            V('tensor_tensor', out=u1[:, :ntk], in0=u1[:, :ntk], in1=g, op=ALU.mult)
            A('activation', out=u1[:, :ntk], in_=u1[:, :ntk], func=AF.Sigmoid, scale=1.5957691216)
            V('tensor_tensor', out=g, in0=u1[:, :ntk], in1=g, op=ALU.mult)
            V('tensor_copy', out=ygb[:, m * 512:m * 512 + ntk], in_=g)
        for m in range(4):
            ps = dense_fm(wg, m, lambda k: ygb[:, k * 512:k * 512 + ntk], 4, ntk)
            u1 = t1.next()
            A('activation', out=u1[:, :ntk], in_=ps[:, :ntk], func=AF.Sigmoid, bias=par[:, P_GLB + m:P_GLB + m + 1])
            V('tensor_tensor', out=ygb[:, 2048 + m * 512:2048 + m * 512 + ntk], in0=u1[:, :ntk], in1=ubuf[:, m, :ntk],
              op=ALU.mult)
        branch_out(l, 1, lambda k: ygb[:, 2048 + k * 512:2048 + k * 512 + ntk], 4, ntk, 'w_br_s5')

    ST_rw = [S.sb([128, 4, 64], F32, 'strw%d' % l) for l in range(L)]
    sh_rw = [S.sb([128, 14, 1], F32, 'shrw%d' % l) for l in range(L)]
    for l in range(L):
        V('memset', ST_rw[l][:], 0.0)
        V('memset', sh_rw[l][:], 0.0)
    rwp = S.sb([128, 1024], F32, 'rwp')
    ST_rwb = S.sb([128, 4, 64], BF16, 'strwb')
    P_MU, P_W0, P_A0, P_KK, P_KA, P_RK, P_LNW, P_LNB = 320, 334, 338, 342, 346, 350, 354, 358

    def rw_state_load(l, dram3):
        lt = t1.next()
        S.dma('sp', lt[:64, 0:512].rearrange("v (c hp k) -> v c hp k", c=4, hp=2), dram3.rearrange("(c hp) v k -> v c hp k", hp=2))
        ps = psr.next()
        for c4 in range(4):
            S.I('pe', 'transpose', ps[:, c4 * 64:(c4 + 1) * 64], lt[:64, c4 * 128:(c4 + 1) * 128], ident[:64, :64],
                sig=(c4 == 3))
        A('activation', out=ST_rw[l][:, :, :], in_=ps[:, 0:256].rearrange("p (c v) -> p c v", c=4), func=AF.Copy)

    def rw_state_store(l, dram3):
        ps = psr.next()
        for c4 in range(4):
            S.I('pe', 'transpose', ps[:64, c4 * 128:(c4 + 1) * 128], ST_rw[l][:, c4, :], ident, sig=(c4 == 3))
        so = t1.next()
        A('activation', out=so[:64, 0:512], in_=ps[:64, 0:512], func=AF.Copy)
        S.dma('sp', dram3.rearrange("(c hp) v k -> v c hp k", hp=2), so[:64, 0:512].rearrange("v (c hp k) -> v c hp k", c=4, hp=2))

    def rw_phase(l, kind, ntk, chunks, last):
        base = 5136
        RWSTOP = cfg.get('rwstop', 99)
        colload(par[:, P_MU:P_MU + 14], din['rw_mu'][l])
        colload(par[:, P_W0:P_W0 + 4], din['rw_w0'][l])
        colload(par[:, P_A0:P_A0 + 4], din['rw_a0'][l])
        for nm, pc in (('rw_k_k', P_KK), ('rw_k_a', P_KA), ('rw_r_k', P_RK), ('rw_ln_w', P_LNW), ('rw_ln_b', P_LNB)):
            colload(par[:, pc:pc + 4], din[nm][l].rearrange("h v -> (h v)"))
        S.dma('sp', rwp[0:64, 0:512], din['rw_w_up'][l])
        S.dma('sp', rwp[64:128, 0:512], din['rw_a_up'][l])
        S.dma('sp', rwp[:, 512:1024], din['rw_g_up'][l])
        if kind == 's':
            tok_to_fm_load(din['state_rwkv_shift'][l], 14, NSEQ, lambda c: shs[:, c, 0:NSEQ])
        ybv = PG[8][:, :].bitcast(BF16)
        ybf = [ybv[:, 2048 + h * 512:2048 + (h + 1) * 512] for h in range(4)]
        nhalf = (ntk + 255) // 256
        for hf in range(nhalf):
            h0 = hf * 256
            nt = min(256, ntk - h0)
            hv = lambda pg, half: PG[pg][:, half * 1024:(half + 1) * 1024].rearrange("p (c n) -> p c n", c=4)
            R, K, Vv, AT = hv(0, 0), hv(0, 1), hv(1, 0), hv(1, 1)
            BT, KK, EP, G = hv(2, 0), hv(2, 1), hv(3, 0), hv(3, 1)
            BON, Yfm = hv(4, 0), hv(4, 1)
            LOGW, ASIG = AT, BT
            for u in range(4):
                ncol = 512 if u < 3 else 256
                wv = loadw(din['w_in'][l], base + u * 512, ncol)
                for cb in range(ncol // 128):
                    c = u * 4 + cb
                    ps = dense_fm(wv, cb, lambda k: xn[:, k, h0:h0 + nt], KC, nt)
                    sg = stg.next()
                    xm = t1.next()
                    if kind == 'p':
                        A('activation', out=sg[:, 1:1 + nt], in_=ps[:, :nt], func=AF.Copy)
                        V('tensor_copy', out=sg[:, 0:1], in_=sh_rw[l][:, c, :])
                        V('tensor_copy', out=sh_rw[l][:, c, :], in_=sg[:, nt:nt + 1])
                        prev, cur, xo = sg[:, 0:nt], sg[:, 1:1 + nt], xm[:, :nt]
                    else:
                        v3 = sg[:, 0:NSEQ * 9].rearrange("p (b t) -> p b t", t=9)
                        A('activation', out=v3[:, :, 1:9], in_=ps[:, :nt].rearrange("p (b t) -> p b t", t=TS), func=AF.Copy)
                        V('tensor_copy', out=v3[:, :, 0:1], in_=shs[:, c, 0:NSEQ].unsqueeze(2))
                        V('tensor_copy', out=shs[:, c, 0:NSEQ].unsqueeze(2), in_=v3[:, :, 8:9])
                        prev, cur = v3[:, :, 0:8], v3[:, :, 1:9]
                        xo = xm[:, :nt].rearrange("p (b t) -> p b t", t=TS)
                    dd = t1.next()
                    ddv = dd[:, :nt] if kind == 'p' else dd[:, :nt].rearrange("p (b t) -> p b t", t=TS)
                    V('tensor_tensor', out=ddv, in0=prev, in1=cur, op=ALU.subtract)
                    V('scalar_tensor_tensor', out=xo, in0=ddv, scalar=par[:, P_MU + c:P_MU + c + 1], in1=cur,
                      op0=ALU.mult, op1=ALU.add)
                    xs_ = xm[:, :nt]
                    if c < 4:
                        V('tensor_copy', out=R[:, c, :nt], in_=xs_)
                    elif c < 8:
                        V('tensor_copy', out=K[:, c - 4, :nt], in_=xs_)
                    elif c < 12:
                        V('tensor_copy', out=Vv[:, c - 8, :nt], in_=xs_)
                    elif c == 12:
                        lowr = stg.next()
                        A('activation', out=lowr[0:64, :nt], in_=xm[0:64, :nt], func=AF.Tanh)
                        V('tensor_copy', out=lowr[64:128, :nt], in_=xm[64:128, :nt])
                        for cb2 in range(4):
                            psw = psr.next()
                            MM(psw[:, :nt], rwp[0:64, cb2 * 128:(cb2 + 1) * 128], lowr[0:64, :nt])
                            A('activation', out=LOGW[:, cb2, :nt], in_=psw[:, :nt], func=AF.Sigmoid,
                              bias=par[:, P_W0 + cb2:P_W0 + cb2 + 1])
                            V('tensor_scalar', out=LOGW[:, cb2, :nt], in0=LOGW[:, cb2, :nt], scalar1=-0.6065306597126334,
                              scalar2=None, op0=ALU.mult)
                            psa = psr.next()
                            MM(psa[:, :nt], rwp[64:128, cb2 * 128:(cb2 + 1) * 128], lowr[64:128, :nt])
                            A('activation', out=ASIG[:, cb2, :nt], in_=psa[:, :nt], func=AF.Sigmoid,
                              bias=par[:, P_A0 + cb2:P_A0 + cb2 + 1])
                    else:
                        gsg = stg.next()
                        A('activation', out=gsg[:, :nt], in_=xs_, func=AF.Sigmoid)
                        for cb2 in range(4):
                            psg = psr.next()
                            MM(psg[:, :nt], rwp[:, 512 + cb2 * 128:512 + (cb2 + 1) * 128], gsg[:, :nt])
                            A('activation', out=G[:, cb2, :nt], in_=psg[:, :nt], func=AF.Copy)
            if RWSTOP <= 1:
                continue
            rmask = cst[:, C_R64:C_R64 + 512] if kind == 'p' else cst[:, C_R8:C_R8 + 128]
            for c4 in range(4):
                ta = t1.next()
                V('tensor_scalar', out=KK[:, c4, :nt], in0=K[:, c4, :nt], scalar1=par[:, P_KK + c4:P_KK + c4 + 1],
                  scalar2=None, op0=ALU.mult)
                A('activation', out=ta[:, :nt], in_=KK[:, c4, :nt], func=AF.Square)
                ps = psr.next()
                MM(ps[:, :nt], bd64, ta[:, :nt])
                V('tensor_scalar', out=ta[:, :nt], in0=ps[:, :nt], scalar1=1e-24, scalar2=None, op0=ALU.max)
                A('sqrt', ta[:, :nt], ta[:, :nt])
                V('reciprocal', ta[:, :nt], ta[:, :nt])
                V('tensor_tensor', out=KK[:, c4, :nt], in0=KK[:, c4, :nt], in1=ta[:, :nt], op=ALU.mult)
                V('tensor_scalar', out=ta[:, :nt], in0=ASIG[:, c4, :nt], scalar1=-1.0, scalar2=par[:, P_KA + c4:P_KA + c4 + 1],
                  op0=ALU.add, op1=ALU.mult)
                V('tensor_scalar', out=ta[:, :nt], in0=ta[:, :nt], scalar1=1.0, scalar2=None, op0=ALU.add)
                V('tensor_tensor', out=K[:, c4, :nt], in0=K[:, c4, :nt], in1=ta[:, :nt], op=ALU.mult)
                V('tensor_tensor', out=ta[:, :nt], in0=R[:, c4, :nt], in1=K[:, c4, :nt], op=ALU.mult)
                V('tensor_scalar', out=ta[:, :nt], in0=ta[:, :nt], scalar1=par[:, P_RK + c4:P_RK + c4 + 1], scalar2=None,
                  op0=ALU.mult)
                ps = psr.next()
                MM(ps[:, :nt], bd64, ta[:, :nt])
                V('tensor_tensor', out=BON[:, c4, :nt], in0=ps[:, :nt], in1=Vv[:, c4, :nt], op=ALU.mult)
                V('tensor_tensor_scan', out=EP[:, c4, :nt], data0=rmask[:, :nt], data1=LOGW[:, c4, :nt], initial=0.0,
                  op0=ALU.mult, op1=ALU.add)
                em = t1.next()
                A('activation', out=em[:, :nt], in_=EP[:, c4, :nt], func=AF.Exp, scale=-1.0)
                A('activation', out=EP[:, c4, :nt], in_=EP[:, c4, :nt], func=AF.Exp)
                A('activation', out=ta[:, :nt], in_=LOGW[:, c4, :nt], func=AF.Exp, scale=-1.0)
                V('tensor_tensor', out=ta[:, :nt], in0=ta[:, :nt], in1=EP[:, c4, :nt], op=ALU.mult)
                V('scalar_tensor_tensor', out=AT[:, c4, :nt], in0=KK[:, c4, :nt], scalar=-1.0, in1=ta[:, :nt],
                  op0=ALU.mult, op1=ALU.mult)
                V('tensor_tensor', out=BT[:, c4, :nt], in0=ASIG[:, c4, :nt], in1=KK[:, c4, :nt], op=ALU.mult)
                V('tensor_tensor', out=BT[:, c4, :nt], in0=BT[:, c4, :nt], in1=em[:, :nt], op=ALU.mult)
                V('tensor_tensor', out=R[:, c4, :nt], in0=R[:, c4, :nt], in1=EP[:, c4, :nt], op=ALU.mult)
                V('tensor_tensor', out=K[:, c4, :nt], in0=K[:, c4, :nt], in1=em[:, :nt], op=ALU.mult)
            if RWSTOP <= 2:
                continue
            def bfv(pg, lo):
                return PG[pg][:, lo:lo + 512].bitcast(BF16).rearrange("p (c n) -> p c n", c=4)
            R_b, K_b = bfv(2, 1024), bfv(2, 1536)
            V('tensor_copy', out=R_b[:, :, :nt], in_=R[:, :, :nt])
            V('tensor_copy', out=K_b[:, :, :nt], in_=K[:, :, :nt])
            AT_b, BT_b = bfv(0, 0), bfv(0, 512)
            V('tensor_copy', out=AT_b[:, :, :nt], in_=AT[:, :, :nt])
            V('tensor_copy', out=BT_b[:, :, :nt], in_=BT[:, :, :nt])
            my = [(ci, c0 - h0, T) for ci, (c0, T) in enumerate(chunks) if h0 <= c0 < h0 + nt]
            for (ci, c0, T) in my:
                if kind == 's':
                    rw_state_load(l, din['state_rwkv'][l, ci])
                A('activation', out=ST_rwb[:, :, :], in_=ST_rw[l][:, :, :], func=AF.Copy)
                W8 = 8 * T
                blk = lambda pg, i: PG[pg][:T, i * 512:i * 512 + W8 // 2].bitcast(BF16).rearrange("s (h t) -> s h t", h=8)
                Q, QT_, P_, AkM = blk(5, 0), blk(5, 1), blk(5, 2), blk(5, 3)
                RbM, RkM, Q2, QT2 = blk(6, 0), blk(6, 1), blk(6, 2), blk(6, 3)
                Vtok, UT, Bgt, Kgt = (PG[7][:T, i * 512:i * 512 + 256].bitcast(BF16) for i in range(4))
                RH, ytok = PG[8][:T, 0:256].bitcast(BF16), PG[8][:T, 512:1024]
                RHf = PG[8][:T, 256:512].bitcast(BF16)
                fs = lambda X, h: X[(h % 2) * 64:(h % 2) * 64 + 64, h // 2, c0:c0 + T]
                pss = [psr.next() for _ in range(5)]
                for h in range(8):
                    o = slice(h * T, (h + 1) * T)
                    MM(pss[0][:T, o], fs(BT_b, h), fs(AT_b, h), sg=(h == 7))
                    MM(pss[1][:T, o], fs(K_b, h), fs(AT_b, h), sg=(h == 7))
                    MM(pss[2][:T, o], fs(BT_b, h), fs(R_b, h), sg=(h == 7))
                    MM(pss[3][:T, o], fs(K_b, h), fs(R_b, h), sg=(h == 7))
                    MM(pss[4][:T, o], fs(AT_b, h), fs(BT_b, h), sg=(h == 7))
                pv = lambda p: p[:T, 0:W8].rearrange("s (h t) -> s h t", h=8)
                mb = lambda m: m[:T, :T].unsqueeze(1).broadcast_to([T, 8, T])
                V('tensor_tensor', out=Q, in0=pv(pss[0]), in1=mb(mlt), op=ALU.mult)
                V('tensor_tensor', out=AkM, in0=pv(pss[1]), in1=mb(mlt), op=ALU.mult)
                V('tensor_tensor', out=RbM, in0=pv(pss[2]), in1=mb(mle), op=ALU.mult)
                V('tensor_tensor', out=RkM, in0=pv(pss[3]), in1=mb(mle), op=ALU.mult)
                V('tensor_tensor', out=QT_, in0=pv(pss[4]), in1=mb(mgt), op=ALU.mult)
                V('tensor_tensor', out=P_, in0=Q, in1=mb(ident), op=ALU.add)
                if RWSTOP <= 3:
                    continue
                psv = psr.next()
                for c4 in range(4):
                    S.I('pe', 'transpose', psv[:T, c4 * 128:(c4 + 1) * 128], Vv[:, c4, c0:c0 + T], ident, sig=(c4 == 3))
                A('activation', out=Vtok, in_=psv[:T, 0:512], func=AF.Copy)
                for (srcX, dstX) in ((BT, Bgt), (K, Kgt)):
                    pst = psr.next()
                    for c4 in range(4):
                        tg = stg.next()
                        V('tensor_scalar', out=tg[:, :T], in0=srcX[:, c4, c0:c0 + T],
                          scalar1=EP[:, c4, c0 + T - 1:c0 + T], scalar2=None, op0=ALU.mult)
                        S.I('pe', 'transpose', pst[:T, c4 * 128:(c4 + 1) * 128], tg[:, :T], ident, sig=True)
                    A('activation', out=dstX, in_=pst[:T, 0:512], func=AF.Copy)
                nsteps = max(1, int(math.ceil(math.log2(T))))
                cq, cqt, nq, nqt = Q, QT_, Q2, QT2
                for j in range(nsteps - 1):
                    lastj = (j == nsteps - 2)
                    psQT = psr.next()
                    if not lastj:
                        psQ = psr.next()
                    for h in range(8):
                        MM(psQT[:T, h * T:(h + 1) * T], cq[:, h, :], cqt[:, h, :], sg=(h == 7))
                        if not lastj:
                            MM(psQ[:T, h * T:(h + 1) * T], cqt[:, h, :], cq[:, h, :], sg=(h == 7))
                    A('activation', out=nqt, in_=pv(psQT), func=AF.Copy)
                    if not lastj:
                        V('tensor_copy', out=nq, in_=pv(psQ))
                    cq, cqt, nq, nqt = nq, nqt, cq, cqt
                    psP = psr.next()
                    for h in range(8):
                        MM(psP[:T, h * T:(h + 1) * T], cqt[:, h, :], P_[:, h, :], sg=(h == 7))
                    V('tensor_tensor', out=P_, in0=P_, in1=pv(psP), op=ALU.add)
                if RWSTOP <= 4:
                    continue
                if RWSTOP <= 5:
                    continue
                psRe, psRo, psR2 = psr.next(), psr.next(), psr.next()
                for h in range(8):
                    o = slice(h * 64, (h + 1) * 64)
                    o2 = slice((h // 2) * 64, (h // 2 + 1) * 64)
                    MM((psRe, psRo)[h % 2][:T, o2], fs(AT_b, h), ST_rwb[(h % 2) * 64:(h % 2) * 64 + 64, h // 2, :], sg=(h >= 6))
                    MM(psR2[:T, o], AkM[:, h, :], Vtok[:, o], sg=(h == 7))
                RH4 = RH.rearrange("t (c hp v) -> t c hp v", c=4, hp=2)
                A('activation', out=RH4[:, :, 0, :], in_=psRe[:T, 0:256].rearrange("t (c v) -> t c v", c=4), func=AF.Copy)
                A('activation', out=RH4[:, :, 1, :], in_=psRo[:T, 0:256].rearrange("t (c v) -> t c v", c=4), func=AF.Copy)
                V('tensor_tensor', out=RH, in0=RH, in1=psR2[:T, 0:512], op=ALU.add)
                if RWSTOP <= 5.2:
                    continue
                psU = psr.next()
                for h in range(8):
                    o = slice(h * 64, (h + 1) * 64)
                    MM(psU[:T, o], P_[:, h, :], RH[:, o], sg=(h == 7))
                A('activation', out=UT, in_=psU[:T, 0:512], func=AF.Copy)
                if RWSTOP <= 5.4:
                    continue
                psYe, psYo, psY2 = psr.next(), psr.next(), psr.next()
                for h in range(8):
                    o = slice(h * 64, (h + 1) * 64)
                    o2 = slice((h // 2) * 64, (h // 2 + 1) * 64)
                    MM((psYe, psYo)[h % 2][:T, o2], fs(R_b, h), ST_rwb[(h % 2) * 64:(h % 2) * 64 + 64, h // 2, :], sg=(h >= 6))
                    MM(psY2[:T, o], RbM[:, h, :], UT[:, o], True, False)
                    MM(psY2[:T, o], RkM[:, h, :], Vtok[:, o], False, True, sg=(h == 7))
                Y4 = ytok.rearrange("t (c hp v) -> t c hp v", c=4, hp=2)
                A('activation', out=Y4[:, :, 0, :], in_=psYe[:T, 0:256].rearrange("t (c v) -> t c v", c=4), func=AF.Copy)
                A('activation', out=Y4[:, :, 1, :], in_=psYo[:T, 0:256].rearrange("t (c v) -> t c v", c=4), func=AF.Copy)
                V('tensor_tensor', out=ytok, in0=ytok, in1=psY2[:T, 0:512], op=ALU.add)
                if RWSTOP <= 5.6:
                    continue
                psyt = psr.next()
                for c4 in range(4):
                    S.I('pe', 'transpose', psyt[:, c4 * T:(c4 + 1) * T], ytok[:, c4 * 128:(c4 + 1) * 128], ident[:T, :T],
                        sig=(c4 == 3))
                A('activation', out=Yfm[:, :, c0:c0 + T], in_=psyt[:, 0:4 * T].rearrange("p (c t) -> p c t", c=4), func=AF.Copy)
                if RWSTOP <= 6:
                    continue
                for h in range(8):
                    c4, hp = h // 2, h % 2
                    o = slice(h * 64, (h + 1) * 64)
                    psS = psr.next()
                    MM(psS[:, 0:64], Bgt[:, c4 * 128:(c4 + 1) * 128], UT[:, o], True, False)
                    MM(psS[:, 0:64], Kgt[:, c4 * 128:(c4 + 1) * 128], Vtok[:, o], False, True)
                    rows = slice(hp * 64, hp * 64 + 64)
                    V('scalar_tensor_tensor', out=ST_rw[l][rows, c4, :], in0=ST_rw[l][rows, c4, :],
                      scalar=EP[rows, c4, c0 + T - 1:c0 + T], in1=psS[rows, 0:64], op0=ALU.mult, op1=ALU.add)
                if kind == 's':
                    rw_state_store(l, dout['s_rwkv'][l, ci])
            if RWSTOP <= 7:
                continue
            for c4 in range(4):
                ps = psr.next()
                MM(ps[:, :nt], bd64, Yfm[:, c4, :nt])
                cen = t1.next()
                V('scalar_tensor_tensor', out=cen[:, :nt], in0=ps[:, :nt], scalar=-1.0 / 64, in1=Yfm[:, c4, :nt],
                  op0=ALU.mult, op1=ALU.add)
                sq = t1.next()
                A('activation', out=sq[:, :nt], in_=cen[:, :nt], func=AF.Square)
                ps2 = psr.next()
                MM(ps2[:, :nt], bd64, sq[:, :nt])
                V('tensor_scalar', out=sq[:, :nt], in0=ps2[:, :nt], scalar1=1.0 / 64, scalar2=64e-5, op0=ALU.mult, op1=ALU.add)
                A('sqrt', sq[:, :nt], sq[:, :nt])
                V('reciprocal', sq[:, :nt], sq[:, :nt])
                V('tensor_tensor', out=cen[:, :nt], in0=cen[:, :nt], in1=sq[:, :nt], op=ALU.mult)
                V('tensor_scalar', out=cen[:, :nt], in0=cen[:, :nt], scalar1=par[:, P_LNW + c4:P_LNW + c4 + 1],
                  scalar2=par[:, P_LNB + c4:P_LNB + c4 + 1], op0=ALU.mult, op1=ALU.add)
                V('tensor_tensor', out=cen[:, :nt], in0=cen[:, :nt], in1=BON[:, c4, :nt], op=ALU.add)
                V('tensor_tensor', out=ybf[c4][:, h0:h0 + nt], in0=cen[:, :nt], in1=G[:, c4, :nt], op=ALU.mult)
        if RWSTOP <= 8:
            return
        if kind == 's':
            fm_to_tok_store(lambda c: shs[:, c, 0:NSEQ], 14, NSEQ, dout['s_rwkv_shift'][l])
        elif last:
            fm_to_tok_store(lambda c: sh_rw[l][:, c, :], 14, 1, dout['p_rwkv_shift'][l].rearrange("(o c) -> o c", o=1))
            rw_state_store(l, dout['p_rwkv'][l])
        branch_out(l, 3, lambda k: ybf[k][:, :ntk], 4, ntk, 'w_br_rw')


    tiles = [('p', i * 512, min(512, SEQ - i * 512)) for i in range((SEQ + 511) // 512)]
    tiles.append(('s', 0, NSEQ * TS))
    nprompt = len(tiles) - 1

    for ti, (kind, t0, ntk) in enumerate(tiles):
        last = (ti == nprompt - 1)
        src = din['x_prompt'] if kind == 'p' else din['x_sample']
        dsty = dout['y_prompt'] if kind == 'p' else dout['y_sample']
        nsub = (ntk + 127) // 128
        if kind == 'p':
            chunks128 = [(j * 128, min(128, ntk - j * 128)) for j in range(nsub)]
            chunks64 = [(j * 64, min(64, ntk - j * 64)) for j in range((ntk + 63) // 64)]
        else:
            chunks128 = [(b * TS, TS) for b in range(NSEQ)]
            chunks64 = chunks128
        for j in range(nsub):
            n = min(128, ntk - j * 128)
            lt = PG[7 + j % 2][:, 0:1024]
            S.dma('sp', lt[:n, :], src[t0 + j * 128:t0 + j * 128 + n, :])
            for k0 in range(0, KC, 4):
                ps = psr.next()
                for k in range(k0, k0 + 4):
                    S.I('pe', 'transpose', ps[:, (k - k0) * 128:(k - k0) * 128 + n], lt[:n, k * 128:(k + 1) * 128],
                        ident[:n, :n], sig=(k == k0 + 3))
                A('activation', out=x[:, k0:k0 + 4, j * 128:j * 128 + n],
                  in_=ps[:, :].rearrange("p (k n) -> p k n", k=4)[:, :, :n], func=AF.Copy)

        for l in range(L):
            colload(par[:, P_N1:P_N1 + 8], din['norm1_w'][l])
            colload(par[:, P_N2:P_N2 + 8], din['norm2_w'][l])
            colload(par[:, P_FCB:P_FCB + 44], din['ffn_conv_b'][l])
            for j in range(3):
                colload(par[:, P_FCW + 44 * j:P_FCW + 44 * j + 44], din['ffn_conv_w'][l, j])
            colload(par[:, P_BM:P_BM + 32], din['b_merge'][l])
            rmsnorm_x(par[:, P_N1:P_N1 + 8], ntk, xn)
            S.mrg_first = True
            if 'ssd' in EN:
                ssd_phase(l, kind, ntk, chunks128, last)
            if 's5' in EN:
                s5_phase(l, kind, ntk, chunks64, last)
            if 'hg' in EN:
                hg_phase(l, kind, ntk, chunks64, last)
            if 'rw' in EN:
                rw_phase(l, kind, ntk, chunks64, last)
            if S.mrg_first:
                V('memset', mrg[:, :, :ntk], 0.0)
            A('activation', out=xn[:, :, :ntk], in_=mrg[:, :, :ntk], func=AF.Copy)
            for u in range(2):
                wv = loadw(din['w_out'][l], u * 512, 512)
                for cb in range(4):
                    ps = dense_fm(wv, cb, lambda k: xn[:, k, :ntk], KC, ntk)
                    kk = u * 4 + cb
                    V('tensor_tensor', out=x[:, kk, :ntk], in0=x[:, kk, :ntk], in1=ps[:, :ntk], op=ALU.add)
            rmsnorm_x(par[:, P_N2:P_N2 + 8], ntk, xn)
            ffh = [PG[c // 8][:, :].bitcast(BF16)[:, (c % 8) * 512:(c % 8) * 512 + 512] for c in range(22)]
            if kind == 's':
                tok_to_fm_load(din['state_ffn_conv'][l].rearrange("b j c -> (b j) c"), 44, NSEQ * 2,
                               lambda c: shf[:, c, :])
            for ub in range(11):
                wg = loadw(din['ffn_up'][l], ub * 256, 256)
                wvv = loadw(din['ffn_up'][l], D_FF + ub * 256, 256)
                for cb in range(2):
                    c = ub * 2 + cb
                    res = []
                    for (wsel, cc) in ((wg, c), (wvv, 22 + c)):
                        ps = dense_fm(wsel, cb, lambda k: xn[:, k, :ntk], KC, ntk)
                        cv = t1.next()
                        if kind == 'p':
                            hist = newst = cv_ffn[l][:, cc, :]
                        else:
                            hist = newst = shf[:, cc, :].rearrange("p (b j) -> p b j", j=2)
                        conv_block(ps, ntk, kind, hist, newst,
                                   [par[:, P_FCW + 44 * j + cc:P_FCW + 44 * j + cc + 1] for j in range(3)],
                                   par[:, P_FCB + cc:P_FCB + cc + 1], 3, cv[:, :ntk])
                        res.append(cv)
                    g, v = res
                    u1 = t1.next()
                    A('activation', out=u1[:, :ntk], in_=g[:, :ntk], func=AF.Square)
                    V('tensor_scalar', out=u1[:, :ntk], in0=u1[:, :ntk], scalar1=0.044715, scalar2=1.0,
                      op0=ALU.mult, op1=ALU.add)
                    V('tensor_tensor', out=u1[:, :ntk], in0=u1[:, :ntk], in1=g[:, :ntk], op=ALU.mult)
                    A('activation', out=u1[:, :ntk], in_=u1[:, :ntk], func=AF.Sigmoid, scale=1.5957691216)
                    V('tensor_tensor', out=u1[:, :ntk], in0=u1[:, :ntk], in1=g[:, :ntk], op=ALU.mult)
                    V('tensor_tensor', out=ffh[c][:, :ntk], in0=u1[:, :ntk], in1=v[:, :ntk], op=ALU.mult)
            if kind == 's':
                fm_to_tok_store(lambda c: shf[:, c, :], 44, NSEQ * 2,
                                dout['s_ffn_conv'][l].rearrange("b j c -> (b j) c"))
            elif last:
                fm_to_tok_store(lambda c: cv_ffn[l][:, c, :], 44, 2, dout['p_ffn_conv'][l])
            for cb in range(8):
                wv = loadw(din['ffn_down'][l], cb * 128, 128)
                ps = psr.next()
                for k in range(22):
                    MM(ps[:, :ntk], wv[:, k, :], ffh[k][:, :ntk], k == 0, k == 21)
                V('tensor_tensor', out=x[:, cb, :ntk], in0=x[:, cb, :ntk], in1=ps[:, :ntk], op=ALU.add)

        colload(par[:, 500:508], din['final_norm_w'])
        sumsq_rstd([x[:, k, :ntk] for k in range(KC)], ntk, D, EPS)
        yo = [PG[k // 4][:, (k % 4) * 512:(k % 4) * 512 + 512] for k in range(8)]
        for k in range(KC):
            V('scalar_tensor_tensor', out=yo[k][:, :ntk], in0=x[:, k, :ntk], scalar=par[:, 500 + k:501 + k],
              in1=rstd[:, :ntk], op0=ALU.mult, op1=ALU.mult)
        for j in range(nsub):
            n = min(128, ntk - j * 128)
            so = PG[7 + j % 2][:, 0:1024]
            for k0 in range(0, KC, 4):
                ps = psr.next()
                for k in range(k0, k0 + 4):
                    S.I('pe', 'transpose', ps[:n, (k - k0) * 128:(k - k0 + 1) * 128], yo[k][:, j * 128:j * 128 + n],
                        ident, sig=(k == k0 + 3))
                A('activation', out=so[:n, k0 * 128:(k0 + 4) * 128], in_=ps[:n, :], func=AF.Copy)
            S.dma('sp', dsty[t0 + j * 128:t0 + j * 128 + n, :], so[:n, :])

    S.finish()


OUT_ORDER = ['y_prompt', 'y_sample'] + ['p_' + n for n in STATE_NAMES] + ['s_' + n for n in STATE_NAMES]


def kernel(_cfg=None, **inputs):
    inputs = {k: np.asarray(v) for k, v in inputs.items()}
    x_prompt, x_sample = inputs['x_prompt'], inputs['x_sample']
    B, SEQ, _ = x_prompt.shape
    DB, TS, _ = x_sample.shape
    DEPTH = inputs['w_in'].shape[0]
    nseq = DB // NCORES
    cfg = dict(depth=DEPTH, seq=SEQ, nseq=nseq)
    if _cfg:
        cfg.update(_cfg)
    consts = make_consts()
    in_maps = []
    for c in range(NCORES):
        m = {}
        for k in IN_NAMES:
            a = inputs[k]
            if k == 'x_prompt':
                a = a[c]
            elif k == 'x_sample':
                a = a[c * nseq:(c + 1) * nseq].reshape(nseq * TS, D)
            elif k.startswith('state_'):
                a = a[:, c * nseq:(c + 1) * nseq]
            m[k] = np.ascontiguousarray(a, dtype=np.float32)
        m['consts'] = consts
        in_maps.append(m)
    shapes = {k: in_maps[0][k].shape for k in IN_NAMES}
    nc = build(cfg, shapes)
    res = run_bass_kernel_spmd(nc, in_maps, core_ids=list(range(NCORES)))
    r = res.results
    outs = []
    for name in OUT_ORDER:
        if name == 'y_prompt':
            outs.append(np.stack([r[c][name] for c in range(NCORES)], 0))
        elif name == 'y_sample':
            outs.append(np.concatenate([r[c][name].reshape(nseq, TS, D) for c in range(NCORES)], 0))
        elif name.startswith('p_'):
            outs.append(np.stack([r[c][name] for c in range(NCORES)], 1))
        else:
            outs.append(np.concatenate([r[c][name] for c in range(NCORES)], 1))
    return tuple(outs)
```

```python
import numpy as np
import concourse.bass as bass
import concourse.mybir as mybir
from concourse.bass_utils import run_bass_kernel_spmd

F32 = mybir.dt.float32
BF16 = mybir.dt.bfloat16
AF = mybir.ActivationFunctionType
ALU = mybir.AluOpType
AX = mybir.AxisListType

D = 1024
KC = D // 128
NCORES = 8
EPS = 1e-6
D_FF = 2816


class Buf:
    def __init__(self, t):
        self.t = t
        self.last_w = None
        self.readers = {}
        self.dsem = None
        self.dcount = 0

    def __getitem__(self, k):
        return self.t[k]


class Sched:
    def __init__(self, nc):
        self.nc = nc
        self.eng = {'pe': nc.tensor, 'dve': nc.vector, 'act': nc.scalar, 'pool': nc.gpsimd, 'sp': nc.sync}
        self.sem = {e: nc.alloc_semaphore(name='s_' + e) for e in ('pe', 'dve', 'act', 'pool')}
        self.cnt = {e: 0 for e in self.sem}
        self.epoch = {e: 0 for e in self.sem}
        self.all_dma = []
        self.LIMIT = 8000
        self.seen = {e: {} for e in self.eng}
        self.bufs = {}
        self.nbuf = 0
        self.pending_pe = []

    def sb(self, shape, dt=F32, name=None):
        self.nbuf += 1
        name = name or ('b%d' % self.nbuf)
        b = Buf(self.nc.alloc_sbuf_tensor(name, list(shape), dt))
        self.bufs[name] = b
        return b

    def ps(self, shape, dt=F32, name=None):
        self.nbuf += 1
        name = name or ('p%d' % self.nbuf)
        b = Buf(self.nc.alloc_psum_tensor(name, list(shape), dt))
        self.bufs[name] = b
        return b

    def _wait(self, e, tok):
        sem, val, key = tok
        if key.startswith('pe#') and e == 'pe':
            return
        if self.seen[e].get(key, 0) >= val:
            return
        self.eng[e].wait_ge(sem, val)
        self.seen[e][key] = val

    def _find(self, ap):
        try:
            return self.bufs.get(ap.tensor.name)
        except Exception:
            return None

    def _deps(self, e, kw, args):
        outs, ins = [], []
        items = list(kw.items()) + [('out' if i == 0 else 'in%d' % i, a) for i, a in enumerate(args)]
        for k, v in items:
            if isinstance(v, bass.AP):
                b = self._find(v)
                if b is None:
                    continue
                (outs if k in ('out', 'accum_out') else ins).append(b)
        for b in ins:
            if b.last_w:
                self._wait(e, b.last_w)
        for b in outs:
            if b.last_w:
                self._wait(e, b.last_w)
            for tok in b.readers.values():
                self._wait(e, tok)
        return outs, ins

    def I(self, e, fn, *args, sig=True, **kw):
        outs, ins = self._deps(e, kw, args)
        ins_obj = getattr(self.eng[e], fn)(*args, **kw)
        key = '%s#%d' % (e, self.epoch[e])
        if sig:
            self.cnt[e] += 1
            ins_obj.then_inc(self.sem[e], 1)
            tok = (self.sem[e], self.cnt[e], key)
        else:
            tok = (self.sem[e], self.cnt[e] + 1, key)
        for b in ins:
            b.readers[key] = tok
        for b in outs:
            b.last_w = tok
            b.readers = {}
        if sig and self.cnt[e] >= self.LIMIT:
            self.epoch[e] += 1
            self.sem[e] = self.nc.alloc_semaphore(name='s_%s_%d' % (e, self.epoch[e]))
            self.cnt[e] = 0
        return ins_obj

    def dma(self, q, out, in_):
        saved = []
        ob = self._find(out)
        if ob is not None and ob.last_w is not None and ob.last_w[2].startswith('d_' + ob.t.name + '_') \
                and ob.dcount < self.LIMIT:
            saved.append((ob, ob.last_w))
            ob.last_w = None
        outs, ins = self._deps(q, {'out': out, 'in_': in_}, ())
        for (bb, lw) in saved:
            bb.last_w = lw
        ins_obj = self.eng[q].dma_start(out=out, in_=in_)
        b = (outs + ins)[0]
        if b.dsem is None or b.dcount >= self.LIMIT:
            if b.dsem is not None:
                self.all_dma.append((b.dsem, b.dcount, 'd_%s_%d' % (b.t.name, b.depoch)))
            b.depoch = getattr(b, 'depoch', -1) + 1
            b.dsem = self.nc.alloc_semaphore(name='d_%s_%d' % (b.t.name, b.depoch))
            b.dcount = 0
        b.dcount += 16
        ins_obj.then_inc(b.dsem, 16)
        tok = (b.dsem, b.dcount, 'd_%s_%d' % (b.t.name, b.depoch))
        for x in ins:
            x.readers[tok[2]] = tok
        for x in outs:
            x.last_w = tok
            x.readers = {}

    def finish(self):
        for tok in self.all_dma:
            self._wait('sp', tok)
        for b in self.bufs.values():
            if b.dsem is not None:
                self._wait('sp', (b.dsem, b.dcount, 'd_%s_%d' % (b.t.name, b.depoch)))


class Ring:
    def __init__(self, bufs):
        self.bufs = bufs
        self.i = 0

    def next(self):
        b = self.bufs[self.i % len(self.bufs)]
        self.i += 1
        return b


NCONST = 2152
C_IDENT, C_ONES, C_MLE, C_MLT, C_MGT, C_NEG, C_BD64, C_POS, C_R64, C_R8, C_MSKB, C_MSKC, C_R8L = (
    0, 128, 256, 384, 512, 640, 768, 896, 960, 1472, 1600, 1632, 1640)


def make_consts():
    p = np.arange(128)[:, None]
    f = np.arange(128)[None, :]
    parts = [
        (p == f), np.ones((128, 128)), (p <= f), (p < f), (p > f), np.where(p > f, -30000.0, 0.0),
        (p // 64 == f // 64),
        np.broadcast_to(np.arange(1, 65)[None, :], (128, 64)),
        np.broadcast_to((np.arange(512) % 64 != 0)[None, :], (128, 512)),
        np.broadcast_to((np.arange(128) % 8 != 0)[None, :], (128, 128)),
        (np.arange(8)[None, None, :] == 2 * np.arange(4)[None, :, None] + (np.arange(128) // 64)[:, None, None]).reshape(128, 32),
        ((np.arange(128) // 16)[:, None, None] == 2 * np.arange(4)[None, :, None] + np.arange(2)[None, None, :]).reshape(128, 8),
        np.broadcast_to((np.arange(512) % 8 != 0)[None, :], (128, 512)),
    ]
    return np.ascontiguousarray(np.concatenate([np.asarray(a, np.float32) for a in parts], axis=1))


IN_NAMES = ['x_prompt', 'x_sample', 'state_ssd', 'state_ssd_conv', 'state_s5_re', 'state_s5_im', 'state_hgrn',
            'state_rwkv', 'state_rwkv_shift', 'state_ffn_conv',
            'norm1_w', 'w_in', 'ssd_conv_w', 'ssd_conv_b', 'ssd_dt_bias', 'ssd_a_log', 'ssd_d', 'ssd_norm_w',
            's5_a_re', 's5_a_im', 's5_log_dt', 's5_b_re', 's5_b_im', 's5_c_re', 's5_c_im', 's5_d', 's5_glu_w',
            's5_glu_b', 'hg_lb_raw', 'hg_norm_w',
            'rw_mu', 'rw_w0', 'rw_w_up', 'rw_a0', 'rw_a_up', 'rw_g_up', 'rw_k_k', 'rw_k_a', 'rw_r_k', 'rw_ln_w',
            'rw_ln_b', 'w_br_ssd', 'w_br_s5', 'w_br_hg', 'w_br_rw', 'w_merge', 'b_merge', 'w_out',
            'norm2_w', 'ffn_up', 'ffn_conv_w', 'ffn_conv_b', 'ffn_down', 'final_norm_w']
STATE_NAMES = ['ssd', 'ssd_conv', 's5_re', 's5_im', 'hgrn', 'rwkv', 'rwkv_shift', 'ffn_conv']


def build(cfg, shapes):
    nc = bass.Bass("TRN2", target_bir_lowering=False)
    with nc.allow_non_contiguous_dma(reason="small per-channel parameter loads"):
        _build(nc, cfg, shapes)
    return nc


def _build(nc, cfg, shapes):
    L, SEQ, NSEQ = cfg['depth'], cfg['seq'], cfg['nseq']
    EN = cfg.get('enable', ('ssd', 's5', 'hg', 'rw'))
    TS = 8
    S = Sched(nc)
    V = lambda fn, *a, **k: S.I('dve', fn, *a, **k)
    A = lambda fn, *a, **k: S.I('act', fn, *a, **k)

    def MM(out, lhsT, rhs, st=True, sp=True, sg=None):
        return S.I('pe', 'matmul', out, lhsT, rhs, start=st, stop=sp, sig=(sp if sg is None else sg))

    din = {}
    for n in IN_NAMES:
        din[n] = nc.dram_tensor(n, list(shapes[n]), F32, kind="ExternalInput").ap()
    din['consts'] = nc.dram_tensor('consts', [128, NCONST], F32, kind="ExternalInput").ap()
    st_shapes = {'ssd': [16, 64, 64], 'ssd_conv': [3, 1536], 's5_re': [32, 64], 's5_im': [32, 64],
                 'hgrn': [4, 128, 128], 'rwkv': [8, 64, 64], 'rwkv_shift': [1792], 'ffn_conv': [2, 5632]}
    dout = {}
    dout['y_prompt'] = nc.dram_tensor('y_prompt', [SEQ, D], F32, kind="ExternalOutput").ap()
    dout['y_sample'] = nc.dram_tensor('y_sample', [NSEQ * TS, D], F32, kind="ExternalOutput").ap()
    for n in STATE_NAMES:
        dout['p_' + n] = nc.dram_tensor('p_' + n, [L] + st_shapes[n], F32, kind="ExternalOutput").ap()
        dout['s_' + n] = nc.dram_tensor('s_' + n, [L, NSEQ] + st_shapes[n], F32, kind="ExternalOutput").ap()

    cst = S.sb([128, NCONST], F32, 'cst')
    S.dma('sp', cst[:, :], din['consts'][:, :])
    ident = cst[:, C_IDENT:C_IDENT + 128]
    ones = cst[:, C_ONES:C_ONES + 128]
    mle = cst[:, C_MLE:C_MLE + 128]
    mlt = cst[:, C_MLT:C_MLT + 128]
    mgt = cst[:, C_MGT:C_MGT + 128]
    negm = cst[:, C_NEG:C_NEG + 128]
    bd64 = cst[:, C_BD64:C_BD64 + 128]

    def TR(out, in_, n):
        return S.I('pe', 'transpose', out, in_, ident[:n, :n])

    psr = Ring([S.ps([128, 512], F32, 'psb%d' % i) for i in range(8)])
    wring = Ring([S.sb([128, 4096], BF16, 'wr%d' % i) for i in range(3)])
    NTKMAX = 512
    x = S.sb([128, KC, NTKMAX], F32, 'x')
    xn = S.sb([128, KC, NTKMAX], BF16, 'xn')
    mrg = S.sb([128, KC, NTKMAX], F32, 'mrg')
    rstd = S.sb([128, NTKMAX], F32, 'rstd')
    par = S.sb([128, 512], F32, 'par')
    parb = S.sb([128, 256], F32, 'parb')
    S5BIG0 = S.sb([128, 128], F32, 's5big0')
    S5BIG1 = S.sb([128, 128], F32, 's5big1')
    t1 = Ring([S.sb([128, NTKMAX], F32, 't1_%d' % i) for i in range(3)])
    stg = Ring([S.sb([128, NTKMAX + 16], F32, 'stg%d' % i) for i in range(2)])
    PG = [S.sb([128, 2048], F32, 'pg%d' % i) for i in range(9)]
    ST_ssd = [S.sb([128, 2, 256], F32, 'stssd%d' % l) for l in range(L)]
    cv_ssd = [S.sb([128, 12, 3], F32, 'cvssd%d' % l) for l in range(L)]
    cv_ffn = [S.sb([128, 44, 2], F32, 'cvffn%d' % l) for l in range(L)]
    for l in range(L):
        V('memset', ST_ssd[l][:], 0.0)
        V('memset', cv_ssd[l][:], 0.0)
        V('memset', cv_ffn[l][:], 0.0)
    shs = S.sb([128, 14, NSEQ * 3], F32, 'shs')
    shf = PG[3][:, 0:44 * NSEQ * 2].rearrange("p (c r) -> p c r", c=44)

    def loadw(dram2d, c0, ncols, r0=0, K=None):
        K = K or dram2d.shape[0]
        kc = K // 128
        b = wring.next()
        view = b.t[:, 0:kc * ncols].rearrange("p (k n) -> p k n", k=kc)
        S.dma('pool', view, dram2d[r0:r0 + K, :].rearrange("(k p) n -> p k n", p=128)[:, :, c0:c0 + ncols])
        return view

    def colload(dst, vec):
        S.dma('sp', dst, vec.rearrange("(k p) -> p k", p=128))

    def rowload(dst, vec, rows=128):
        S.dma('sp', dst, vec.partition_broadcast(rows))

    def sumsq_rstd(src_chunks, ntk, nfeat, eps):
        ps = psr.next()
        n = len(src_chunks)
        for k, sc in enumerate(src_chunks):
            tq = t1.next()
            A('activation', out=tq[:, :ntk], in_=sc, func=AF.Square)
            MM(ps[:, :ntk], ones, tq[:, :ntk], k == 0, k == n - 1, sg=True)
        V('tensor_scalar', out=rstd[:, :ntk], in0=ps[:, :ntk], scalar1=1.0 / nfeat, scalar2=eps,
          op0=ALU.mult, op1=ALU.add)
        A('sqrt', rstd[:, :ntk], rstd[:, :ntk])
        V('reciprocal', rstd[:, :ntk], rstd[:, :ntk])

    def rmsnorm_x(w_col, ntk, dst):
        sumsq_rstd([x[:, k, :ntk] for k in range(KC)], ntk, D, EPS)
        for k in range(KC):
            V('scalar_tensor_tensor', out=dst[:, k, :ntk], in0=x[:, k, :ntk], scalar=w_col[:, k:k + 1],
              in1=rstd[:, :ntk], op0=ALU.mult, op1=ALU.mult)

    def dense_fm(wv, cb, src, kc, ntk):
        ps = psr.next()
        for k in range(kc):
            MM(ps[:, :ntk], wv[:, k, cb * 128:(cb + 1) * 128], src(k), k == 0, k == kc - 1)
        return ps

    def fm_to_tok_store(src, nch, rows, dram2d):
        for g0 in range(0, nch, 4):
            ng = min(4, nch - g0)
            ps = psr.next()
            for i in range(ng):
                S.I('pe', 'transpose', ps[:rows, i * 128:(i + 1) * 128], src(g0 + i), ident, sig=(i == ng - 1))
            so = t1.next()
            A('activation', out=so[:rows, :ng * 128], in_=ps[:rows, :ng * 128], func=AF.Copy)
            S.dma('sp', dram2d[:, g0 * 128:(g0 + ng) * 128], so[:rows, :ng * 128])

    def tok_to_fm_load(dram2d, nch, rows, dst):
        for g0 in range(0, nch, 4):
            ng = min(4, nch - g0)
            lt = t1.next()
            S.dma('sp', lt[:rows, :ng * 128], dram2d[:, g0 * 128:(g0 + ng) * 128])
            ps = psr.next()
            for i in range(ng):
                S.I('pe', 'transpose', ps[:, i * rows:(i + 1) * rows], lt[:rows, i * 128:(i + 1) * 128],
                    ident[:rows, :rows], sig=(i == ng - 1))
            for i in range(ng):
                A('activation', out=dst(g0 + i), in_=ps[:, i * rows:(i + 1) * rows], func=AF.Copy)

    def conv_block(ps, ntk, kind, hist, newst, wcols, bcol, ntap, out_ap):
        h = ntap - 1
        sg = stg.next()
        if kind == 'p':
            A('activation', out=sg[:, h:h + ntk], in_=ps[:, :ntk], func=AF.Copy)
            V('tensor_copy', out=sg[:, 0:h], in_=hist)
            V('tensor_copy', out=newst, in_=sg[:, ntk:ntk + h])
            full = lambda j: sg[:, j:j + ntk]
            o = out_ap
        else:
            w = TS + h
            v3 = sg[:, 0:NSEQ * w].rearrange("p (b t) -> p b t", t=w)
            A('activation', out=v3[:, :, h:w], in_=ps[:, :ntk].rearrange("p (b t) -> p b t", t=TS), func=AF.Copy)
            V('tensor_copy', out=v3[:, :, 0:h], in_=hist)
            V('tensor_copy', out=newst, in_=v3[:, :, TS:w])
            full = lambda j: v3[:, :, j:j + TS]
            o = out_ap.rearrange("p (b t) -> p b t", t=TS)
        V('tensor_scalar', out=o, in0=full(0), scalar1=wcols[0], scalar2=bcol, op0=ALU.mult, op1=ALU.add)
        for j in range(1, ntap):
            V('scalar_tensor_tensor', out=o, in0=full(j), scalar=wcols[j], in1=o, op0=ALU.mult, op1=ALU.add)

    P_N1, P_N2, P_FCB, P_FCW, P_BM, P_SCW, P_SCB, P_SNW, P_SD = 0, 8, 16, 60, 192, 224, 272, 284, 292
    B_DTB, B_A = 0, 16

    def branch_out(l, bi, ysrc, kc, ntk, wname):
        for u in range(2):
            wb = loadw(din[wname][l], u * 512, 512)
            wm = loadw(din['w_merge'][l], bi * D + u * 512, 512)
            for cb in range(4):
                cc = u * 4 + cb
                psB = dense_fm(wb, cb, ysrc, kc, ntk)
                psG = dense_fm(wm, cb, lambda k: xn[:, k, :ntk], KC, ntk)
                tg = t1.next()
                A('activation', out=tg[:, :ntk], in_=psG[:, :ntk], func=AF.Sigmoid,
                  bias=par[:, P_BM + bi * 8 + cc:P_BM + bi * 8 + cc + 1])
                if S.mrg_first:
                    V('tensor_tensor', out=mrg[:, cc, :ntk], in0=tg[:, :ntk], in1=psB[:, :ntk], op=ALU.mult)
                else:
                    V('tensor_tensor', out=tg[:, :ntk], in0=tg[:, :ntk], in1=psB[:, :ntk], op=ALU.mult)
                    V('tensor_tensor', out=mrg[:, cc, :ntk], in0=mrg[:, cc, :ntk], in1=tg[:, :ntk], op=ALU.add)
        S.mrg_first = False

    def ssd_state_store(l, dram3):
        ps = psr.next()
        for j in range(2):
            for q in range(2):
                S.I('pe', 'transpose', ps[:, (j * 2 + q) * 128:(j * 2 + q + 1) * 128],
                    ST_ssd[l][:, j, q * 128:(q + 1) * 128], ident, sig=(j == 1 and q == 1))
        so = t1.next()
        A('activation', out=so[:, :512], in_=ps[:, :512], func=AF.Copy)
        for j in range(2):
            for g2 in range(2):
                for q in range(2):
                    h0 = 8 * j + 4 * g2 + 2 * q
                    S.dma('sp', dram3[h0:h0 + 2].rearrange("h p n -> (h p) n"),
                          so[:, (j * 2 + q) * 128 + g2 * 64:(j * 2 + q) * 128 + g2 * 64 + 64])

    def ssd_state_load(l, dram3):
        lt = t1.next()
        for j in range(2):
            for g2 in range(2):
                for q in range(2):
                    h0 = 8 * j + 4 * g2 + 2 * q
                    S.dma('sp', lt[:, (j * 2 + q) * 128 + g2 * 64:(j * 2 + q) * 128 + g2 * 64 + 64],
                          dram3[h0:h0 + 2].rearrange("h p n -> (h p) n"))
        ps = psr.next()
        for j in range(2):
            for q in range(2):
                S.I('pe', 'transpose', ps[:, (j * 2 + q) * 128:(j * 2 + q + 1) * 128],
                    lt[:, (j * 2 + q) * 128:(j * 2 + q + 1) * 128], ident, sig=(j == 1 and q == 1))
        A('activation', out=ST_ssd[l][:, :, :], in_=ps[:, :512].rearrange("p (j c) -> p j c", j=2), func=AF.Copy)

    ST_ssdb = S.sb([128, 2, 256], BF16, 'stssdb')

    def ssd_phase(l, kind, ntk, chunks, last):
        XCbv = PG[8][:, 1024:2048].bitcast(BF16)
        XCb = [XCbv[:, j * 512:(j + 1) * 512] for j in range(4)]
        XC = [PG[c // 4][:, (c % 4) * 512:(c % 4) * 512 + 512] for c in range(12)]
        szv = PG[3][:, :].bitcast(BF16)
        sz = [szv[:, k * 512:(k + 1) * 512] for k in range(8)]
        yss = [PG[4 + k // 4][:, (k % 4) * 512:(k % 4) * 512 + 512] for k in range(8)]
        for j in range(4):
            colload(par[:, P_SCW + 12 * j:P_SCW + 12 * j + 12], din['ssd_conv_w'][l, j])
        colload(par[:, P_SCB:P_SCB + 12], din['ssd_conv_b'][l])
        colload(par[:, P_SNW:P_SNW + 8], din['ssd_norm_w'][l])
        for h in range(16):
            S.dma('sp', par[(h % 2) * 64:(h % 2) * 64 + 64, P_SD + h // 2:P_SD + h // 2 + 1],
                  din['ssd_d'][l, h:h + 1].partition_broadcast(64))
        rowload(parb[:, B_DTB:B_DTB + 16], din['ssd_dt_bias'][l])
        rowload(parb[:, B_A:B_A + 16], din['ssd_a_log'][l])
        A('activation', out=parb[:, B_A:B_A + 16], in_=parb[:, B_A:B_A + 16], func=AF.Exp)
        V('tensor_scalar', out=parb[:, B_A:B_A + 16], in0=parb[:, B_A:B_A + 16], scalar1=-1.0, scalar2=None,
          op0=ALU.mult)
        for u in range(2):
            wv = loadw(din['w_in'][l], u * 512, 512)
            for cb in range(4):
                ps = dense_fm(wv, cb, lambda k: xn[:, k, :ntk], KC, ntk)
                tq = t1.next()
                A('activation', out=tq[:, :ntk], in_=ps[:, :ntk], func=AF.Sigmoid)
                V('tensor_tensor', out=sz[u * 4 + cb][:, :ntk], in0=tq[:, :ntk], in1=ps[:, :ntk], op=ALU.mult)
        if kind == 's':
            for b in range(NSEQ):
                pass
            tok_to_fm_load(din['state_ssd_conv'][l].rearrange("b j c -> (b j) c"), 12, NSEQ * 3,
                           lambda c: shs[:, c, :])
        for u in range(3):
            wv = loadw(din['w_in'][l], 1024 + u * 512, 512)
            for cb in range(4):
                c = u * 4 + cb
                ps = dense_fm(wv, cb, lambda k: xn[:, k, :ntk], KC, ntk)
                tq = t1.next()
                if kind == 'p':
                    hist, newst = cv_ssd[l][:, c, :], cv_ssd[l][:, c, :]
                else:
                    hist = newst = shs[:, c, :].rearrange("p (b j) -> p b j", j=3)
                conv_block(ps, ntk, kind, hist, newst, [par[:, P_SCW + 12 * j + c:P_SCW + 12 * j + c + 1] for j in range(4)],
                           par[:, P_SCB + c:P_SCB + c + 1], 4, tq[:, :ntk])
                tq2 = t1.next()
                A('activation', out=tq2[:, :ntk], in_=tq[:, :ntk], func=AF.Sigmoid)
                V('tensor_tensor', out=XC[c][:, :ntk], in0=tq2[:, :ntk], in1=tq[:, :ntk], op=ALU.mult)
                if c >= 8:
                    A('activation', out=XCb[c - 8][:, :ntk], in_=XC[c][:, :ntk], func=AF.Copy)
        if kind == 's':
            fm_to_tok_store(lambda c: shs[:, c, :], 12, NSEQ * 3,
                            dout['s_ssd_conv'][l].rearrange("b j c -> (b j) c"))
        elif last:
            fm_to_tok_store(lambda c: cv_ssd[l][:, c, :], 12, 3, dout['p_ssd_conv'][l])
        wdt = loadw(din['w_in'][l], 2560, 16)
        pg6, pg7, pg8 = PG[6], PG[7], PG[8]
        for ci, (c0, T) in enumerate(chunks):
            if kind == 's':
                ssd_state_load(l, din['state_ssd'][l, ci])
            dtv, dta, acum, wl = pg8[:T, 0:16], pg8[:T, 16:32], pg8[:T, 32:48], pg8[:T, 48:64]
            ETb = pg8[:, 64:80]
            Btok = pg8[:T, 128:256].bitcast(BF16)
            dtw = pg8[:T, 80:96]
            A('activation', out=ST_ssdb[:, :, :], in_=ST_ssd[l][:, :, :], func=AF.Copy)
            tY = pg8[:, 384:512]
            ps = psr.next()
            for k in range(KC):
                MM(ps[:T, 0:16], xn[:, k, c0:c0 + T], wdt[:, k, 0:16], k == 0, k == KC - 1)
            V('tensor_tensor', out=dtv, in0=ps[:T, 0:16], in1=parb[:T, B_DTB:B_DTB + 16], op=ALU.add)
            A('activation', out=dtv, in_=dtv, func=AF.Exp)
            V('tensor_scalar', out=dtv, in0=dtv, scalar1=1.0, scalar2=None, op0=ALU.add)
            A('activation', out=dtv, in_=dtv, func=AF.Ln)
            V('tensor_tensor', out=dta, in0=dtv, in1=parb[:T, B_A:B_A + 16], op=ALU.mult)
            ps = psr.next()
            MM(ps[:T, 0:16], mle[:T, :T], dta)
            MM(ps[:, 16:32], ones[:T, :], dta)
            A('activation', out=acum, in_=ps[:T, 0:16], func=AF.Copy)
            A('activation', out=ETb, in_=ps[:, 16:32], func=AF.Exp)
            V('tensor_tensor', out=wl, in0=ps[:T, 16:32], in1=acum, op=ALU.subtract)
            A('activation', out=wl, in_=wl, func=AF.Exp)
            XDT = pg7[:T, 0:512].bitcast(BF16)
            XDTW = pg7[:T, 512:1024].bitcast(BF16)
            V('tensor_tensor', out=dtw, in0=dtv, in1=wl, op=ALU.mult)
            for half in range(2):
                ps = psr.next()
                for i in range(4):
                    S.I('pe', 'transpose', ps[:T, i * 128:(i + 1) * 128], XC[half * 4 + i][:, c0:c0 + T], ident,
                        sig=(i == 3))
                V('tensor_tensor', out=XDT[:, half * 512:(half + 1) * 512].rearrange("t (h p) -> t h p", p=64),
                  in0=ps[:T, :512].rearrange("t (h p) -> t h p", p=64),
                  in1=dtv[:, half * 8:half * 8 + 8].unsqueeze(2).broadcast_to([T, 8, 64]), op=ALU.mult)
                V('tensor_tensor', out=XDTW[:, half * 512:(half + 1) * 512].rearrange("t (h p) -> t h p", p=64),
                  in0=ps[:T, :512].rearrange("t (h p) -> t h p", p=64),
                  in1=dtw[:, half * 8:half * 8 + 8].unsqueeze(2).broadcast_to([T, 8, 64]), op=ALU.mult)
            ps = psr.next()
            for i in range(2):
                S.I('pe', 'transpose', ps[:T, i * 128:(i + 1) * 128], XC[8 + i][:, c0:c0 + T], ident, sig=(i == 1))
            A('activation', out=Btok, in_=ps[:T, 0:256], func=AF.Copy)
            if T <= 32:
                HT = 16 * T
                D16 = pg6[:T, 0:HT].rearrange("s (h t) -> s h t", h=16)
                E16 = pg6[:T, 512:512 + HT].rearrange("s (h t) -> s h t", h=16)
                EB16 = pg6[:, 1024:1024 + HT]
                M16f = pg6[:T, 1536:1536 + HT // 2].bitcast(BF16)
                M16 = M16f.rearrange("s (h t) -> s h t", h=16)
                V('tensor_tensor', out=D16, in0=dta[:, 0:16].unsqueeze(2).broadcast_to([T, 16, T]),
                  in1=mle[:T, :T].unsqueeze(1).broadcast_to([T, 16, T]), op=ALU.mult)
                psAB = psr.next()
                MM(psAB[:, 0:HT], ones[:T, :], pg6[:T, 0:HT])
                A('activation', out=EB16, in_=psAB[:, 0:HT], func=AF.Exp)
                V('tensor_tensor', out=E16, in0=psAB[:T, 0:HT].rearrange("s (h t) -> s h t", h=16),
                  in1=acum[:, 0:16].unsqueeze(2).broadcast_to([T, 16, T]), op=ALU.subtract)
                V('tensor_tensor', out=E16, in0=E16, in1=negm[:T, :T].unsqueeze(1).broadcast_to([T, 16, T]), op=ALU.add)
                A('activation', out=E16, in_=E16, func=AF.Exp)
                psGp = [psr.next(), psr.next()]
                for g in range(4):
                    gp = slice((g % 2) * 64, (g % 2) * 64 + 64)
                    MM(psGp[g % 2][:T, (g // 2) * T:(g // 2 + 1) * T], XCb[g // 2][gp, c0:c0 + T], XCb[2 + g // 2][gp, c0:c0 + T],
                       sg=(g >= 2))
                E5 = pg6[:T, 512:512 + HT].rearrange("s (j q h t) -> s j q h t", j=2, q=2, h=4)
                M5 = M16f.rearrange("s (j q h t) -> s j q h t", j=2, q=2, h=4)
                for q in range(2):
                    V('tensor_tensor', out=M5[:, :, q, :, :], in0=E5[:, :, q, :, :],
                      in1=psGp[q][:T, 0:2 * T].rearrange("s (j t) -> s j t", j=2).unsqueeze(2).broadcast_to([T, 2, 4, T]),
                      op=ALU.mult)
                psOp = [psr.next(), psr.next()]
                psY = psr.next()
                for g in range(4):
                    gp = slice((g % 2) * 64, (g % 2) * 64 + 64)
                    for hp in range(2):
                        slot = (g // 2) * 2 + hp
                        MM(psOp[g % 2][:, slot * T:(slot + 1) * T], ST_ssdb[gp, g // 2, hp * 128:(hp + 1) * 128],
                           XCb[2 + g // 2][gp, c0:c0 + T], sg=(g >= 2 and hp == 1))
                for h in range(16):
                    MM(psY[:, h * T:(h + 1) * T], XDT[:, (h // 2) * 128:(h // 2 + 1) * 128], M16[:, h, :], sg=(h == 15))
                EB6 = EB16.rearrange("p (j q i h t) -> p j q i h t", j=2, q=2, i=2, h=2)
                Y6 = psY[:, 0:HT].rearrange("p (j q i h t) -> p j q i h t", j=2, q=2, i=2, h=2)
                for q in range(2):
                    O4 = psOp[q][:, 0:4 * T].rearrange("p (j i t) -> p j i t", j=2, i=2)
                    for h2 in range(2):
                        hv = slice(h2 * 64, h2 * 64 + 64)
                        for j in range(2):
                            tYv = tY[hv, 0:2 * T].rearrange("p (i t) -> p i t", i=2)
                            V('tensor_tensor', out=tYv, in0=O4[hv, j, :, :], in1=EB6[hv, j, q, :, h2, :], op=ALU.mult)
                            dstv = PG[4 + j][hv, :].rearrange("p (k n) -> p k n", k=4)[:, 2 * q:2 * q + 2, c0:c0 + T]
                            V('tensor_tensor', out=dstv, in0=tYv, in1=Y6[hv, j, q, :, h2, :], op=ALU.add)
                for j in range(2):
                    tq = t1.next()
                    tqv = tq[:, 0:4 * T].rearrange("p (k t) -> p k t", k=4)
                    xcv = PG[j][:, :].rearrange("p (k n) -> p k n", k=4)[:, :, c0:c0 + T]
                    ysv = PG[4 + j][:, :].rearrange("p (k n) -> p k n", k=4)[:, :, c0:c0 + T]
                    V('tensor_tensor', out=tqv, in0=xcv, in1=par[:, P_SD + 4 * j:P_SD + 4 * j + 4].unsqueeze(2).broadcast_to([128, 4, T]),
                      op=ALU.mult)
                    V('tensor_tensor', out=ysv, in0=ysv, in1=tqv, op=ALU.add)
                for gq in range(4):
                    gp = slice((gq % 2) * 64, (gq % 2) * 64 + 64)
                    psS = psr.next()
                    MM(psS[:, 0:256], Btok[:, (gq // 2) * 128:(gq // 2 + 1) * 128], XDTW[:, gq * 256:(gq + 1) * 256])
                    STv = ST_ssd[l][gp, gq // 2, :]
                    V('tensor_tensor', out=STv.rearrange("n (h p) -> n h p", p=64),
                      in0=STv.rearrange("n (h p) -> n h p", p=64),
                      in1=ETb[gp, 4 * gq:4 * gq + 4].unsqueeze(2).broadcast_to([64, 4, 64]), op=ALU.mult)
                    V('tensor_tensor', out=STv, in0=STv, in1=psS[gp, 0:256], op=ALU.add)
            else:
                for gq in range(4):
                    D4 = pg6[:T, 0:4 * T].rearrange("s (h t) -> s h t", h=4)
                    E4 = pg6[:T, 512:512 + 4 * T].rearrange("s (h t) -> s h t", h=4)
                    EB4 = pg6[:, 1024:1024 + 4 * T].rearrange("s (h t) -> s h t", h=4)
                    M4 = pg6[:T, 1536:1536 + 2 * T].bitcast(BF16).rearrange("s (h t) -> s h t", h=4)
                    V('tensor_tensor', out=D4, in0=dta[:, 4 * gq:4 * gq + 4].unsqueeze(2).broadcast_to([T, 4, T]),
                      in1=mle[:T, :T].unsqueeze(1).broadcast_to([T, 4, T]), op=ALU.mult)
                    psAB = psr.next()
                    MM(psAB[:, 0:4 * T], ones[:T, :], pg6[:T, 0:4 * T])
                    A('activation', out=EB4, in_=psAB[:, 0:4 * T].rearrange("s (h t) -> s h t", h=4), func=AF.Exp)
                    V('tensor_tensor', out=E4, in0=psAB[:T, 0:4 * T].rearrange("s (h t) -> s h t", h=4),
                      in1=acum[:, 4 * gq:4 * gq + 4].unsqueeze(2).broadcast_to([T, 4, T]), op=ALU.subtract)
                    V('tensor_tensor', out=E4, in0=E4, in1=negm[:T, :T].unsqueeze(1).broadcast_to([T, 4, T]), op=ALU.add)
                    A('activation', out=E4, in_=E4, func=AF.Exp)
                    gp = slice((gq % 2) * 64, (gq % 2) * 64 + 64)
                    Bfm = XCb[gq // 2][gp, c0:c0 + T]
                    Cfm = XCb[2 + gq // 2][gp, c0:c0 + T]
                    psG = psr.next()
                    MM(psG[:T, :T], Bfm, Cfm)
                    V('tensor_tensor', out=M4, in0=E4, in1=psG[:T, :T].unsqueeze(1).broadcast_to([T, 4, T]), op=ALU.mult)
                    for hp in range(2):
                        k = 2 * gq + hp
                        psO = psr.next()
                        MM(psO[:, :T], ST_ssdb[gp, gq // 2, hp * 128:(hp + 1) * 128], Cfm)
                        for h2 in range(2):
                            hl = 2 * hp + h2
                            psY = psr.next()
                            MM(psY[:, :T], XDT[:, k * 128:(k + 1) * 128], M4[:, hl, :])
                            hv = slice(h2 * 64, h2 * 64 + 64)
                            V('tensor_tensor', out=tY[hv, :T], in0=psO[hv, :T], in1=EB4[hv, hl, :], op=ALU.mult)
                            V('tensor_tensor', out=yss[k][hv, c0:c0 + T], in0=tY[hv, :T], in1=psY[hv, :T], op=ALU.add)
                        V('scalar_tensor_tensor', out=yss[k][:, c0:c0 + T], in0=XC[k][:, c0:c0 + T],
                          scalar=par[:, P_SD + k:P_SD + k + 1], in1=yss[k][:, c0:c0 + T], op0=ALU.mult, op1=ALU.add)
                    psS = psr.next()
                    MM(psS[:, 0:256], Btok[:, (gq // 2) * 128:(gq // 2 + 1) * 128], XDTW[:, gq * 256:(gq + 1) * 256])
                    STv = ST_ssd[l][gp, gq // 2, :]
                    V('tensor_tensor', out=STv.rearrange("n (h p) -> n h p", p=64),
                      in0=STv.rearrange("n (h p) -> n h p", p=64),
                      in1=ETb[gp, 4 * gq:4 * gq + 4].unsqueeze(2).broadcast_to([64, 4, 64]), op=ALU.mult)
                    V('tensor_tensor', out=STv, in0=STv, in1=psS[gp, 0:256], op=ALU.add)
            if kind == 's':
                ssd_state_store(l, dout['s_ssd'][l, ci])
        if kind == 'p' and last:
            ssd_state_store(l, dout['p_ssd'][l])
        for k in range(8):
            V('tensor_tensor', out=yss[k][:, :ntk], in0=yss[k][:, :ntk], in1=sz[k][:, :ntk], op=ALU.mult)
        sumsq_rstd([yss[k][:, :ntk] for k in range(8)], ntk, 1024, EPS)
        for k in range(8):
            V('scalar_tensor_tensor', out=sz[k][:, :ntk], in0=yss[k][:, :ntk], scalar=par[:, P_SNW + k:P_SNW + k + 1],
              in1=rstd[:, :ntk], op0=ALU.mult, op1=ALU.mult)
        branch_out(l, 0, lambda k: sz[k][:, :ntk], 8, ntk, 'w_br_ssd')

    ST_hg = [S.sb([128, 4, 128], F32, 'sthg%d' % l) for l in range(L)]
    LB = S.sb([128, L, 4], F32, 'lb')
    lbtmp = S.sb([128, L + 2, 4], F32, 'lbtmp')
    for l in range(L):
        V('memset', ST_hg[l][:], 0.0)
        colload(lbtmp[:, l, :], din['hg_lb_raw'][l])
    A('activation', out=lbtmp[:, 0:L, :], in_=lbtmp[:, 0:L, :], func=AF.Exp)
    V('tensor_copy', out=lbtmp[:, L, :], in_=lbtmp[:, 0, :])
    for l in range(1, L):
        V('tensor_tensor', out=lbtmp[:, L, :], in0=lbtmp[:, L, :], in1=lbtmp[:, l, :], op=ALU.add)
    V('reciprocal', lbtmp[:, L + 1, :], lbtmp[:, L, :])
    V('memset', LB[:, 0, :], 0.0)
    for l in range(1, L):
        V('tensor_tensor', out=lbtmp[:, l, :], in0=lbtmp[:, l, :], in1=lbtmp[:, L + 1, :], op=ALU.mult)
        V('tensor_tensor', out=LB[:, l, :], in0=LB[:, l - 1, :], in1=lbtmp[:, l, :], op=ALU.add)
    OML = S.sb([128, L, 4], F32, 'oml')
    V('tensor_scalar', out=OML[:], in0=LB[:], scalar1=-1.0, scalar2=1.0, op0=ALU.mult, op1=ALU.add)
    P_HNW = 304

    def hg_phase(l, kind, ntk, chunks, last):
        def pgv(i):
            return PG[i][:, :].rearrange("p (h n) -> p h n", h=4)
        QT, KT, BB, VV, OO, GS, EBt = (pgv(i) for i in range(7))
        pg7 = PG[7]
        ybv = PG[8][:, :].bitcast(BF16)
        ybf = [ybv[:, h * 512:(h + 1) * 512] for h in range(4)]
        rmask = cst[:, C_R64:C_R64 + 512] if kind == 'p' else cst[:, C_R8:C_R8 + 128]
        colload(par[:, P_HNW:P_HNW + 1], din['hg_norm_w'][l])
        base = 3088
        wv = loadw(din['w_in'][l], base, 512)
        for h in range(4):
            ps = dense_fm(wv, h, lambda k: xn[:, k, :ntk], KC, ntk)
            tq = t1.next()
            A('activation', out=tq[:, :ntk], in_=ps[:, :ntk], func=AF.Sigmoid)
            V('tensor_tensor', out=QT[:, h, :ntk], in0=tq[:, :ntk], in1=ps[:, :ntk], op=ALU.mult)
        wv = loadw(din['w_in'][l], base + 512, 512)
        for h in range(4):
            ps = dense_fm(wv, h, lambda k: xn[:, k, :ntk], KC, ntk)
            tq = t1.next()
            A('activation', out=tq[:, :ntk], in_=ps[:, :ntk], func=AF.Sigmoid)
            V('tensor_scalar', out=tq[:, :ntk], in0=tq[:, :ntk], scalar1=OML[:, l, h:h + 1], scalar2=LB[:, l, h:h + 1],
              op0=ALU.mult, op1=ALU.add)
            V('tensor_scalar', out=KT[:, h, :ntk], in0=tq[:, :ntk], scalar1=-1.0, scalar2=1.0, op0=ALU.mult, op1=ALU.add)
            A('activation', out=tq[:, :ntk], in_=tq[:, :ntk], func=AF.Ln)
            V('tensor_tensor_scan', out=BB[:, h, :ntk], data0=rmask[:, :ntk], data1=tq[:, :ntk], initial=0.0,
              op0=ALU.mult, op1=ALU.add)
            A('activation', out=EBt[:, h, :ntk], in_=BB[:, h, :ntk], func=AF.Exp)
            V('tensor_tensor', out=QT[:, h, :ntk], in0=QT[:, h, :ntk], in1=EBt[:, h, :ntk], op=ALU.mult)
            tq2 = t1.next()
            V('tensor_scalar', out=tq2[:, :ntk], in0=BB[:, h, :ntk], scalar1=-1.0, scalar2=80.0, op0=ALU.mult, op1=ALU.min)
            A('activation', out=tq2[:, :ntk], in_=tq2[:, :ntk], func=AF.Exp)
            V('tensor_tensor', out=KT[:, h, :ntk], in0=KT[:, h, :ntk], in1=tq2[:, :ntk], op=ALU.mult)
        wv = loadw(din['w_in'][l], base + 1024, 512)
        for h in range(4):
            ps = dense_fm(wv, h, lambda k: xn[:, k, :ntk], KC, ntk)
            A('activation', out=VV[:, h, :ntk], in_=ps[:, :ntk], func=AF.Copy)
        wv = loadw(din['w_in'][l], base + 1536, 512)
        for h in range(4):
            ps = dense_fm(wv, h, lambda k: xn[:, k, :ntk], KC, ntk)
            A('activation', out=GS[:, h, :ntk], in_=ps[:, :ntk], func=AF.Sigmoid)
        for ci, (c0, T) in enumerate(chunks):
            if kind == 's':
                S.dma('sp', ST_hg[l][:, :, :], din['state_hgrn'][l, ci].rearrange("h k v -> k h v"))
            SC = pg7[:T, 0:4 * T].rearrange("s (h t) -> s h t", h=4)
            Vtok = pg7[:T, 256:768]
            Ktok = pg7[:T, 768:1280]
            psS = psr.next()
            for h in range(4):
                MM(psS[:T, h * T:(h + 1) * T], KT[:, h, c0:c0 + T], QT[:, h, c0:c0 + T], sg=(h == 3))
            V('tensor_tensor', out=SC, in0=psS[:T, 0:4 * T].rearrange("s (h t) -> s h t", h=4),
              in1=mle[:T, :T].unsqueeze(1).broadcast_to([T, 4, T]), op=ALU.mult)
            psV = psr.next()
            for h in range(4):
                S.I('pe', 'transpose', psV[:T, h * 128:(h + 1) * 128], VV[:, h, c0:c0 + T], ident, sig=(h == 3))
            A('activation', out=Vtok, in_=psV[:T, :512], func=AF.Copy)
            psK = psr.next()
            for h in range(4):
                S.I('pe', 'transpose', psK[:T, h * 128:(h + 1) * 128], KT[:, h, c0:c0 + T], ident, sig=(h == 3))
            A('activation', out=Ktok, in_=psK[:T, :512], func=AF.Copy)
            for h in range(4):
                psO = psr.next()
                MM(psO[:, :T], Vtok[:, h * 128:(h + 1) * 128], SC[:, h, :], True, False)
                MM(psO[:, :T], ST_hg[l][:, h, :], QT[:, h, c0:c0 + T], False, True)
                A('activation', out=OO[:, h, c0:c0 + T], in_=psO[:, :T], func=AF.Copy)
                psU = psr.next()
                MM(psU[:, :128], Ktok[:, h * 128:(h + 1) * 128], Vtok[:, h * 128:(h + 1) * 128])
                V('tensor_tensor', out=ST_hg[l][:, h, :], in0=ST_hg[l][:, h, :], in1=psU[:, :128], op=ALU.add)
                V('tensor_scalar', out=ST_hg[l][:, h, :], in0=ST_hg[l][:, h, :],
                  scalar1=EBt[:, h, c0 + T - 1:c0 + T], scalar2=None, op0=ALU.mult)
            if kind == 's':
                S.dma('sp', dout['s_hgrn'][l, ci].rearrange("h k v -> k h v"), ST_hg[l][:, :, :])
        if kind == 'p' and last:
            S.dma('sp', dout['p_hgrn'][l].rearrange("h k v -> k h v"), ST_hg[l][:, :, :])
        for h in range(4):
            sumsq_rstd([OO[:, h, :ntk]], ntk, 128, EPS)
            tq = t1.next()
            V('scalar_tensor_tensor', out=tq[:, :ntk], in0=OO[:, h, :ntk], scalar=par[:, P_HNW:P_HNW + 1],
              in1=rstd[:, :ntk], op0=ALU.mult, op1=ALU.mult)
            V('tensor_tensor', out=ybf[h][:, :ntk], in0=tq[:, :ntk], in1=GS[:, h, :ntk], op=ALU.mult)
        branch_out(l, 2, lambda k: ybf[k][:, :ntk], 4, ntk, 'w_br_hg')

    import math
    PI = math.pi
    hS = [S.sb([128, 2, 16], F32, 'hs5_%d' % l) for l in range(L)]
    for l in range(L):
        V('memset', hS[l][:], 0.0)
    hs_s = S.sb([128, 2, 16, NSEQ], F32, 'hs5s')
    s5p = S.sb([128, 16, 16], F32, 's5p')
    ubuf = S.sb([128, 4, NTKMAX], F32, 'ubuf')
    P_S5D, P_GLB = 308, 312

    def s5_phase(l, kind, ntk, chunks, last):
        TC = 64
        pos = cst[:, C_POS:C_POS + 64]
        TF_re = PG[0][:, 0:1024].rearrange("p (c t) -> p c t", c=16)
        TF_im = PG[0][:, 1024:2048].rearrange("p (c t) -> p c t", c=16)
        TI_re = PG[1][:, 0:1024].rearrange("p (c t) -> p c t", c=16)
        TI_im = PG[1][:, 1024:2048].rearrange("p (c t) -> p c t", c=16)
        X1 = PG[2][:, 0:1024].rearrange("p (c t) -> p c t", c=16)
        X2 = PG[2][:, 1024:2048].rearrange("p (c t) -> p c t", c=16)
        a_re, a_im, dtc, da_re, da_im, den, q_re, q_im, tm1, tm2, tm3 = (s5p[:, i, :] for i in range(11))
        S.dma('sp', a_re, din['s5_a_re'][l].rearrange("(c g) n -> (g n) c", g=2))
        S.dma('sp', a_im, din['s5_a_im'][l].rearrange("(c g) n -> (g n) c", g=2))
        for g2 in range(2):
            S.dma('sp', s5p[g2 * 64:(g2 + 1) * 64, 2, :],
                  din['s5_log_dt'][l].rearrange("(c g) -> g c", g=2)[g2].partition_broadcast(64))
        colload(par[:, P_S5D:P_S5D + 4], din['s5_d'][l])
        colload(par[:, P_GLB:P_GLB + 4], din['s5_glu_b'][l])
        A('activation', out=dtc, in_=dtc, func=AF.Exp)
        V('tensor_tensor', out=da_re, in0=dtc, in1=a_re, op=ALU.mult)
        V('tensor_tensor', out=da_im, in0=dtc, in1=a_im, op=ALU.mult)
        bc = lambda v: v.unsqueeze(2).broadcast_to([128, 16, 64])
        posb = pos.unsqueeze(1).broadcast_to([128, 16, 64])
        V('tensor_tensor', out=X1, in0=bc(da_re), in1=posb, op=ALU.mult)
        A('activation', out=TF_re, in_=X1, func=AF.Exp)
        A('activation', out=TI_re, in_=X1, func=AF.Exp, scale=-1.0)
        V('tensor_tensor', out=X2, in0=bc(da_im), in1=posb, op=ALU.mult)
        I32 = mybir.dt.int32
        XI = PG[8][:, 0:1024].rearrange("p (c t) -> p c t", c=16).bitcast(I32)
        XF = PG[8][:, 1024:2048].rearrange("p (c t) -> p c t", c=16)

        def rred(dst, src, shift):
            V('tensor_scalar', out=dst, in0=src, scalar1=shift, scalar2=None, op0=ALU.add)
            V('tensor_scalar', out=XF, in0=dst, scalar1=1.0 / (2 * PI), scalar2=None, op0=ALU.mult)
            V('tensor_copy', out=XI, in_=XF)
            V('tensor_copy', out=XF, in_=XI)
            V('scalar_tensor_tensor', out=dst, in0=XF, scalar=-2 * PI, in1=dst, op0=ALU.mult, op1=ALU.add)
            V('tensor_scalar', out=XF, in0=dst, scalar1=PI, scalar2=2 * PI, op0=ALU.is_gt, op1=ALU.mult)
            V('tensor_tensor', out=dst, in0=dst, in1=XF, op=ALU.subtract)
            V('tensor_scalar', out=XF, in0=dst, scalar1=-PI, scalar2=2 * PI, op0=ALU.is_lt, op1=ALU.mult)
            V('tensor_tensor', out=dst, in0=dst, in1=XF, op=ALU.add)
        rred(X1, X2, 0.0)
        A('activation', out=X1, in_=X1, func=AF.Sin)
        rred(X2, X2, 0.5 * PI)
        A('activation', out=X2, in_=X2, func=AF.Sin)
        V('tensor_tensor', out=TF_im, in0=TF_re, in1=X1, op=ALU.mult)
        V('tensor_tensor', out=TF_re, in0=TF_re, in1=X2, op=ALU.mult)
        V('tensor_tensor', out=TI_im, in0=TI_re, in1=X1, op=ALU.mult)
        V('tensor_scalar', out=TI_im, in0=TI_im, scalar1=-1.0, scalar2=None, op0=ALU.mult)
        V('tensor_tensor', out=TI_re, in0=TI_re, in1=X2, op=ALU.mult)
        ab_re, ab_im = TF_re[:, :, 0], TF_im[:, :, 0]
        V('tensor_tensor', out=den, in0=a_re, in1=a_re, op=ALU.mult)
        V('tensor_tensor', out=tm1, in0=a_im, in1=a_im, op=ALU.mult)
        V('tensor_tensor', out=den, in0=den, in1=tm1, op=ALU.add)
        V('reciprocal', den, den)
        V('tensor_scalar', out=tm1, in0=ab_re, scalar1=-1.0, scalar2=None, op0=ALU.add)
        V('tensor_tensor', out=tm2, in0=tm1, in1=a_re, op=ALU.mult)
        V('tensor_tensor', out=tm3, in0=ab_im, in1=a_im, op=ALU.mult)
        V('tensor_tensor', out=tm2, in0=tm2, in1=tm3, op=ALU.add)
        V('tensor_tensor', out=q_re, in0=tm2, in1=den, op=ALU.mult)
        V('tensor_tensor', out=tm2, in0=ab_im, in1=a_re, op=ALU.mult)
        V('tensor_tensor', out=tm3, in0=tm1, in1=a_im, op=ALU.mult)
        V('tensor_tensor', out=tm2, in0=tm2, in1=tm3, op=ALU.subtract)
        V('tensor_tensor', out=q_im, in0=tm2, in1=den, op=ALU.mult)
        pg7 = PG[7]
        b_re = pg7[:, 0:256].rearrange("p (c j) -> p c j", c=16)
        b_im = pg7[:, 256:512].rearrange("p (c j) -> p c j", c=16)
        c_re = pg7[:, 512:768].rearrange("p (m n) -> p m n", m=4)
        c_im = pg7[:, 768:1024].rearrange("p (m n) -> p m n", m=4)
        bb_re = pg7[:, 1024:1280].rearrange("p (c j) -> p c j", c=16)
        bb_im = pg7[:, 1280:1536].rearrange("p (c j) -> p c j", c=16)
        tb = pg7[:, 1536:1792].rearrange("p (c j) -> p c j", c=16)
        S.dma('sp', b_re, din['s5_b_re'][l].rearrange("(c g) n j -> (g n) c j", g=2))
        S.dma('sp', b_im, din['s5_b_im'][l].rearrange("(c g) n j -> (g n) c j", g=2))
        S.dma('sp', c_re, din['s5_c_re'][l].rearrange("(m g) j n -> (g j) m n", g=8))
        S.dma('sp', c_im, din['s5_c_im'][l].rearrange("(m g) j n -> (g j) m n", g=8))
        qb = lambda v: v.unsqueeze(2).broadcast_to([128, 16, 16])
        V('tensor_tensor', out=bb_re, in0=b_re, in1=qb(q_re), op=ALU.mult)
        V('tensor_tensor', out=tb, in0=b_im, in1=qb(q_im), op=ALU.mult)
        V('tensor_tensor', out=bb_re, in0=bb_re, in1=tb, op=ALU.subtract)
        V('tensor_tensor', out=bb_im, in0=b_im, in1=qb(q_re), op=ALU.mult)
        V('tensor_tensor', out=tb, in0=b_re, in1=qb(q_im), op=ALU.mult)
        V('tensor_tensor', out=bb_im, in0=bb_im, in1=tb, op=ALU.add)
        V('tensor_scalar', out=c_im, in0=c_im, scalar1=-1.0, scalar2=None, op0=ALU.mult)
        mskB = cst[:, C_MSKB:C_MSKB + 32]
        mskC = cst[:, C_MSKC:C_MSKC + 8]
        BL = {'bre': PG[3], 'bim': PG[4], 'cre': PG[5], 'cim': PG[6]}
        big = Ring([S5BIG0, S5BIG1])
        for nm, srcv in (('bre', bb_re), ('bim', bb_im)):
            for c0_ in range(0, 16, 4):
                ps = psr.next()
                for c in range(c0_, c0_ + 4):
                    yb = big.next()
                    V('tensor_tensor', out=yb[:, :].rearrange("p (g j) -> p g j", g=8),
                      in0=srcv[:, c, :].unsqueeze(1).broadcast_to([128, 8, 16]),
                      in1=mskB[:, (c % 4) * 8:(c % 4) * 8 + 8].unsqueeze(2).broadcast_to([128, 8, 16]), op=ALU.mult)
                    S.I('pe', 'transpose', ps[:, (c - c0_) * 128:(c - c0_ + 1) * 128], yb[:, :], ident, sig=True)
                A('activation', out=BL[nm][:, c0_ * 128:(c0_ + 4) * 128], in_=ps[:, :], func=AF.Copy)
        for nm, srcv in (('cre', c_re), ('cim', c_im)):
            for m in range(4):
                ps = psr.next()
                for i in range(4):
                    yb = big.next()
                    V('tensor_tensor', out=yb[:, :].rearrange("p (g n) -> p g n", g=2),
                      in0=srcv[:, m, :].unsqueeze(1).broadcast_to([128, 2, 64]),
                      in1=mskC[:, i * 2:i * 2 + 2].unsqueeze(2).broadcast_to([128, 2, 64]), op=ALU.mult)
                    S.I('pe', 'transpose', ps[:, i * 128:(i + 1) * 128], yb[:, :], ident, sig=True)
                A('activation', out=BL[nm][:, m * 512:(m + 1) * 512], in_=ps[:, :], func=AF.Copy)
        wv = loadw(din['w_in'][l], 2576, 512)
        for m in range(4):
            ps = dense_fm(wv, m, lambda k: xn[:, k, :ntk], KC, ntk)
            A('activation', out=ubuf[:, m, :ntk], in_=ps[:, :ntk], func=AF.Copy)
        if kind == 's':
            tok_to_fm_load(din['state_s5_re'][l].rearrange("b g n -> b (g n)"), 16, NSEQ, lambda c: hs_s[:, 0, c, :])
            tok_to_fm_load(din['state_s5_im'][l].rearrange("b g n -> b (g n)"), 16, NSEQ, lambda c: hs_s[:, 1, c, :])
        if kind == 's':
            NB = min(8, NSEQ)
            chunks = [(g * NB * TS, NB * TS) for g in range(NSEQ // NB)]
            Tt = TS
        else:
            NB = 1
            Tt = None
        for ci, (c0, T) in enumerate(chunks):
            n = 16 * T
            tt = Tt or T
            v4 = lambda ap: ap.rearrange("p c (b t) -> p c b t", t=tt)
            tA = PG[2][:, 0:n].rearrange("p (c t) -> p c t", c=16)
            tB = PG[2][:, 1024:1024 + n].rearrange("p (c t) -> p c t", c=16)
            W_re = PG[8][:, 0:n].rearrange("p (c t) -> p c t", c=16)
            W_im = PG[8][:, 1024:1024 + n].rearrange("p (c t) -> p c t", c=16)
            tb4 = lambda tab: tab[:, :, :tt].unsqueeze(2).broadcast_to([128, 16, T // tt, tt])
            tfr, tfi, tir, tii = tb4(TF_re), tb4(TF_im), tb4(TI_re), tb4(TI_im)
            if kind == 'p':
                hin_re, hin_im = hS[l][:, 0, :].unsqueeze(2), hS[l][:, 1, :].unsqueeze(2)
            else:
                hin_re, hin_im = hs_s[:, 0, :, ci * NB:(ci + 1) * NB], hs_s[:, 1, :, ci * NB:(ci + 1) * NB]
            nb = (n + 511) // 512
            pre = [psr.next() for _ in range(nb)]
            pim = [psr.next() for _ in range(nb)]
            for c in range(16):
                o = c * T
                MM(pre[o // 512][:, o % 512:o % 512 + T], BL['bre'][:, c * 128:(c + 1) * 128], ubuf[:, c // 4, c0:c0 + T],
                   sg=True)
                MM(pim[o // 512][:, o % 512:o % 512 + T], BL['bim'][:, c * 128:(c + 1) * 128], ubuf[:, c // 4, c0:c0 + T],
                   sg=True)
            cpb = 512 // T if n > 512 else 16
            for bb_ in range(nb):
                cs = slice(bb_ * cpb, (bb_ + 1) * cpb)
                w_ = min(512, n)
                pv = lambda p: p[:, 0:w_].rearrange("p (c b t) -> p c b t", b=T // tt, t=tt)
                V('tensor_tensor', out=v4(tA[:, cs, :]), in0=pv(pre[bb_]), in1=tir[:, cs], op=ALU.mult)
                V('tensor_tensor', out=v4(tB[:, cs, :]), in0=pv(pim[bb_]), in1=tii[:, cs], op=ALU.mult)
                V('tensor_tensor', out=W_re[:, cs, :], in0=tA[:, cs, :], in1=tB[:, cs, :], op=ALU.subtract)
                V('tensor_tensor', out=v4(tA[:, cs, :]), in0=pv(pim[bb_]), in1=tir[:, cs], op=ALU.mult)
                V('tensor_tensor', out=v4(tB[:, cs, :]), in0=pv(pre[bb_]), in1=tii[:, cs], op=ALU.mult)
                V('tensor_tensor', out=W_im[:, cs, :], in0=tA[:, cs, :], in1=tB[:, cs, :], op=ALU.add)
            rm = cst[:, C_R64:C_R64 + 512] if kind == 'p' else cst[:, C_R8L:C_R8L + 512]
            for (Wv, off) in ((PG[8], 0), (PG[8], 1024)):
                for b in range(nb):
                    w_ = min(512, n)
                    seg = Wv[:, off + b * 512:off + b * 512 + w_]
                    dst = PG[2][:, off + b * 512:off + b * 512 + w_]
                    V('tensor_tensor_scan', out=dst, data0=rm[:, :w_], data1=seg, initial=0.0, op0=ALU.mult, op1=ALU.add)
            G_re, G_im = tA, tB
            hb4 = lambda hh: hh.unsqueeze(3).broadcast_to([128, 16, T // tt, tt])
            V('tensor_tensor', out=v4(G_re), in0=v4(G_re), in1=hb4(hin_re), op=ALU.add)
            V('tensor_tensor', out=v4(G_im), in0=v4(G_im), in1=hb4(hin_im), op=ALU.add)
            H_re, H_im = W_re, W_im
            V('tensor_tensor', out=v4(H_re), in0=v4(G_re), in1=tfr, op=ALU.mult)
            V('tensor_tensor', out=v4(H_im), in0=v4(G_im), in1=tfi, op=ALU.mult)
            V('tensor_tensor', out=H_re, in0=H_re, in1=H_im, op=ALU.subtract)
            V('tensor_tensor', out=v4(H_im), in0=v4(G_im), in1=tfr, op=ALU.mult)
            V('tensor_tensor', out=v4(G_re), in0=v4(G_re), in1=tfi, op=ALU.mult)
            V('tensor_tensor', out=H_im, in0=H_im, in1=G_re, op=ALU.add)
            V('tensor_copy', out=hin_re, in_=v4(H_re)[:, :, :, tt - 1])
            V('tensor_copy', out=hin_im, in_=v4(H_im)[:, :, :, tt - 1])
            for m in range(4):
                psY = psr.next()
                for i in range(4):
                    c = 4 * m + i
                    MM(psY[:, :T], BL['cre'][:, c * 128:(c + 1) * 128], H_re[:, c, :], i == 0, False)
                    MM(psY[:, :T], BL['cim'][:, c * 128:(c + 1) * 128], H_im[:, c, :], False, i == 3)
                V('scalar_tensor_tensor', out=ubuf[:, m, c0:c0 + T], in0=ubuf[:, m, c0:c0 + T],
                  scalar=par[:, P_S5D + m:P_S5D + m + 1], in1=psY[:, :T], op0=ALU.mult, op1=ALU.add)
        if kind == 's':
            fm_to_tok_store(lambda c: hs_s[:, 0, c, :], 16, NSEQ, dout['s_s5_re'][l].rearrange("b g n -> b (g n)"))
            fm_to_tok_store(lambda c: hs_s[:, 1, c, :], 16, NSEQ, dout['s_s5_im'][l].rearrange("b g n -> b (g n)"))
        elif last:
            fm_to_tok_store(lambda c: hS[l][:, 0, c:c + 1], 16, 1, dout['p_s5_re'][l].rearrange("(o g) n -> o (g n)", o=1))
            fm_to_tok_store(lambda c: hS[l][:, 1, c:c + 1], 16, 1, dout['p_s5_im'][l].rearrange("(o g) n -> o (g n)", o=1))
        ygb = PG[2][:, :].bitcast(BF16)
        wg = loadw(din['s5_glu_w'][l], 0, 512)
        for m in range(4):
            u1 = t1.next()
            g = ubuf[:, m, :ntk]
            A('activation', out=u1[:, :ntk], in_=g, func=AF.Square)
            V('tensor_scalar', out=u1[:, :ntk], in0=u1[:, :ntk], scalar1=0.044715, scalar2=1.0, op0=ALU.mult, op1=ALU.add)
            V('tensor_tensor', out=u1[:, :ntk], in0=u1[:, :ntk], in1=g, op=ALU.mult)
            A('activation', out=u1[:, :ntk], in_=u1[:, :ntk], func=AF.Sigmoid, scale=1.5957691216)
            V('tensor_tensor', out=g, in0=u1[:, :ntk], in1=g, op=ALU.mult)
            V('tensor_copy', out=ygb[:, m * 512:m * 512 + ntk], in_=g)
        for m in range(4):
            ps = dense_fm(wg, m, lambda k: ygb[:, k * 512:k * 512 + ntk], 4, ntk)
            u1 = t1.next()
            A('activation', out=u1[:, :ntk], in_=ps[:, :ntk], func=AF.Sigmoid, bias=par[:, P_GLB + m:P_GLB + m + 1])
            V('tensor_tensor', out=ygb[:, 2048 + m * 512:2048 + m * 512 + ntk], in0=u1[:, :ntk], in1=ubuf[:, m, :ntk],
              op=ALU.mult)
        branch_out(l, 1, lambda k: ygb[:, 2048 + k * 512:2048 + k * 512 + ntk], 4, ntk, 'w_br_s5')

    ST_rw = [S.sb([128, 4, 64], F32, 'strw%d' % l) for l in range(L)]
    sh_rw = [S.sb([128, 14, 1], F32, 'shrw%d' % l) for l in range(L)]
    for l in range(L):
        V('memset', ST_rw[l][:], 0.0)
        V('memset', sh_rw[l][:], 0.0)
    rwp = S.sb([128, 1024], F32, 'rwp')
    ST_rwb = S.sb([128, 4, 64], BF16, 'strwb')
    P_MU, P_W0, P_A0, P_KK, P_KA, P_RK, P_LNW, P_LNB = 320, 334, 338, 342, 346, 350, 354, 358

    def rw_state_load(l, dram3):
        lt = t1.next()
        S.dma('sp', lt[:64, 0:512].rearrange("v (c hp k) -> v c hp k", c=4, hp=2), dram3.rearrange("(c hp) v k -> v c hp k", hp=2))
        ps = psr.next()
        for c4 in range(4):
            S.I('pe', 'transpose', ps[:, c4 * 64:(c4 + 1) * 64], lt[:64, c4 * 128:(c4 + 1) * 128], ident[:64, :64],
                sig=(c4 == 3))
        A('activation', out=ST_rw[l][:, :, :], in_=ps[:, 0:256].rearrange("p (c v) -> p c v", c=4), func=AF.Copy)

    def rw_state_store(l, dram3):
        ps = psr.next()
        for c4 in range(4):
            S.I('pe', 'transpose', ps[:64, c4 * 128:(c4 + 1) * 128], ST_rw[l][:, c4, :], ident, sig=(c4 == 3))
        so = t1.next()
        A('activation', out=so[:64, 0:512], in_=ps[:64, 0:512], func=AF.Copy)
        S.dma('sp', dram3.rearrange("(c hp) v k -> v c hp k", hp=2), so[:64, 0:512].rearrange("v (c hp k) -> v c hp k", c=4, hp=2))

    def rw_phase(l, kind, ntk, chunks, last):
        base = 5136
        RWSTOP = cfg.get('rwstop', 99)
        colload(par[:, P_MU:P_MU + 14], din['rw_mu'][l])
        colload(par[:, P_W0:P_W0 + 4], din['rw_w0'][l])
        colload(par[:, P_A0:P_A0 + 4], din['rw_a0'][l])
        for nm, pc in (('rw_k_k', P_KK), ('rw_k_a', P_KA), ('rw_r_k', P_RK), ('rw_ln_w', P_LNW), ('rw_ln_b', P_LNB)):
            colload(par[:, pc:pc + 4], din[nm][l].rearrange("h v -> (h v)"))
        S.dma('sp', rwp[0:64, 0:512], din['rw_w_up'][l])
        S.dma('sp', rwp[64:128, 0:512], din['rw_a_up'][l])
        S.dma('sp', rwp[:, 512:1024], din['rw_g_up'][l])
        if kind == 's':
            tok_to_fm_load(din['state_rwkv_shift'][l], 14, NSEQ, lambda c: shs[:, c, 0:NSEQ])
        ybv = PG[8][:, :].bitcast(BF16)
        ybf = [ybv[:, 2048 + h * 512:2048 + (h + 1) * 512] for h in range(4)]
        nhalf = (ntk + 255) // 256
        for hf in range(nhalf):
            h0 = hf * 256
            nt = min(256, ntk - h0)
            hv = lambda pg, half: PG[pg][:, half * 1024:(half + 1) * 1024].rearrange("p (c n) -> p c n", c=4)
            R, K, Vv, AT = hv(0, 0), hv(0, 1), hv(1, 0), hv(1, 1)
            BT, KK, EP, G = hv(2, 0), hv(2, 1), hv(3, 0), hv(3, 1)
            BON, Yfm = hv(4, 0), hv(4, 1)
            LOGW, ASIG = AT, BT
            for u in range(4):
                ncol = 512 if u < 3 else 256
                wv = loadw(din['w_in'][l], base + u * 512, ncol)
                for cb in range(ncol // 128):
                    c = u * 4 + cb
                    ps = dense_fm(wv, cb, lambda k: xn[:, k, h0:h0 + nt], KC, nt)
                    sg = stg.next()
                    xm = t1.next()
                    if kind == 'p':
                        A('activation', out=sg[:, 1:1 + nt], in_=ps[:, :nt], func=AF.Copy)
                        V('tensor_copy', out=sg[:, 0:1], in_=sh_rw[l][:, c, :])
                        V('tensor_copy', out=sh_rw[l][:, c, :], in_=sg[:, nt:nt + 1])
                        prev, cur, xo = sg[:, 0:nt], sg[:, 1:1 + nt], xm[:, :nt]
                    else:
                        v3 = sg[:, 0:NSEQ * 9].rearrange("p (b t) -> p b t", t=9)
                        A('activation', out=v3[:, :, 1:9], in_=ps[:, :nt].rearrange("p (b t) -> p b t", t=TS), func=AF.Copy)
                        V('tensor_copy', out=v3[:, :, 0:1], in_=shs[:, c, 0:NSEQ].unsqueeze(2))
                        V('tensor_copy', out=shs[:, c, 0:NSEQ].unsqueeze(2), in_=v3[:, :, 8:9])
                        prev, cur = v3[:, :, 0:8], v3[:, :, 1:9]
                        xo = xm[:, :nt].rearrange("p (b t) -> p b t", t=TS)
                    dd = t1.next()
                    ddv = dd[:, :nt] if kind == 'p' else dd[:, :nt].rearrange("p (b t) -> p b t", t=TS)
                    V('tensor_tensor', out=ddv, in0=prev, in1=cur, op=ALU.subtract)
                    V('scalar_tensor_tensor', out=xo, in0=ddv, scalar=par[:, P_MU + c:P_MU + c + 1], in1=cur,
                      op0=ALU.mult, op1=ALU.add)
                    xs_ = xm[:, :nt]
                    if c < 4:
                        V('tensor_copy', out=R[:, c, :nt], in_=xs_)
                    elif c < 8:
                        V('tensor_copy', out=K[:, c - 4, :nt], in_=xs_)
                    elif c < 12:
                        V('tensor_copy', out=Vv[:, c - 8, :nt], in_=xs_)
                    elif c == 12:
                        lowr = stg.next()
                        A('activation', out=lowr[0:64, :nt], in_=xm[0:64, :nt], func=AF.Tanh)
                        V('tensor_copy', out=lowr[64:128, :nt], in_=xm[64:128, :nt])
                        for cb2 in range(4):
                            psw = psr.next()
                            MM(psw[:, :nt], rwp[0:64, cb2 * 128:(cb2 + 1) * 128], lowr[0:64, :nt])
                            A('activation', out=LOGW[:, cb2, :nt], in_=psw[:, :nt], func=AF.Sigmoid,
                              bias=par[:, P_W0 + cb2:P_W0 + cb2 + 1])
                            V('tensor_scalar', out=LOGW[:, cb2, :nt], in0=LOGW[:, cb2, :nt], scalar1=-0.6065306597126334,
                              scalar2=None, op0=ALU.mult)
                            psa = psr.next()
                            MM(psa[:, :nt], rwp[64:128, cb2 * 128:(cb2 + 1) * 128], lowr[64:128, :nt])
                            A('activation', out=ASIG[:, cb2, :nt], in_=psa[:, :nt], func=AF.Sigmoid,
                              bias=par[:, P_A0 + cb2:P_A0 + cb2 + 1])
                    else:
                        gsg = stg.next()
                        A('activation', out=gsg[:, :nt], in_=xs_, func=AF.Sigmoid)
                        for cb2 in range(4):
                            psg = psr.next()
                            MM(psg[:, :nt], rwp[:, 512 + cb2 * 128:512 + (cb2 + 1) * 128], gsg[:, :nt])
                            A('activation', out=G[:, cb2, :nt], in_=psg[:, :nt], func=AF.Copy)
            if RWSTOP <= 1:
                continue
            rmask = cst[:, C_R64:C_R64 + 512] if kind == 'p' else cst[:, C_R8:C_R8 + 128]
            for c4 in range(4):
                ta = t1.next()
                V('tensor_scalar', out=KK[:, c4, :nt], in0=K[:, c4, :nt], scalar1=par[:, P_KK + c4:P_KK + c4 + 1],
                  scalar2=None, op0=ALU.mult)
                A('activation', out=ta[:, :nt], in_=KK[:, c4, :nt], func=AF.Square)
                ps = psr.next()
                MM(ps[:, :nt], bd64, ta[:, :nt])
                V('tensor_scalar', out=ta[:, :nt], in0=ps[:, :nt], scalar1=1e-24, scalar2=None, op0=ALU.max)
                A('sqrt', ta[:, :nt], ta[:, :nt])
                V('reciprocal', ta[:, :nt], ta[:, :nt])
                V('tensor_tensor', out=KK[:, c4, :nt], in0=KK[:, c4, :nt], in1=ta[:, :nt], op=ALU.mult)
                V('tensor_scalar', out=ta[:, :nt], in0=ASIG[:, c4, :nt], scalar1=-1.0, scalar2=par[:, P_KA + c4:P_KA + c4 + 1],
                  op0=ALU.add, op1=ALU.mult)
                V('tensor_scalar', out=ta[:, :nt], in0=ta[:, :nt], scalar1=1.0, scalar2=None, op0=ALU.add)
                V('tensor_tensor', out=K[:, c4, :nt], in0=K[:, c4, :nt], in1=ta[:, :nt], op=ALU.mult)
                V('tensor_tensor', out=ta[:, :nt], in0=R[:, c4, :nt], in1=K[:, c4, :nt], op=ALU.mult)
                V('tensor_scalar', out=ta[:, :nt], in0=ta[:, :nt], scalar1=par[:, P_RK + c4:P_RK + c4 + 1], scalar2=None,
                  op0=ALU.mult)
                ps = psr.next()
                MM(ps[:, :nt], bd64, ta[:, :nt])
                V('tensor_tensor', out=BON[:, c4, :nt], in0=ps[:, :nt], in1=Vv[:, c4, :nt], op=ALU.mult)
                V('tensor_tensor_scan', out=EP[:, c4, :nt], data0=rmask[:, :nt], data1=LOGW[:, c4, :nt], initial=0.0,
                  op0=ALU.mult, op1=ALU.add)
                em = t1.next()
                A('activation', out=em[:, :nt], in_=EP[:, c4, :nt], func=AF.Exp, scale=-1.0)
                A('activation', out=EP[:, c4, :nt], in_=EP[:, c4, :nt], func=AF.Exp)
                A('activation', out=ta[:, :nt], in_=LOGW[:, c4, :nt], func=AF.Exp, scale=-1.0)
                V('tensor_tensor', out=ta[:, :nt], in0=ta[:, :nt], in1=EP[:, c4, :nt], op=ALU.mult)
                V('scalar_tensor_tensor', out=AT[:, c4, :nt], in0=KK[:, c4, :nt], scalar=-1.0, in1=ta[:, :nt],
                  op0=ALU.mult, op1=ALU.mult)
                V('tensor_tensor', out=BT[:, c4, :nt], in0=ASIG[:, c4, :nt], in1=KK[:, c4, :nt], op=ALU.mult)
                V('tensor_tensor', out=BT[:, c4, :nt], in0=BT[:, c4, :nt], in1=em[:, :nt], op=ALU.mult)
                V('tensor_tensor', out=R[:, c4, :nt], in0=R[:, c4, :nt], in1=EP[:, c4, :nt], op=ALU.mult)
                V('tensor_tensor', out=K[:, c4, :nt], in0=K[:, c4, :nt], in1=em[:, :nt], op=ALU.mult)
            if RWSTOP <= 2:
                continue
            def bfv(pg, lo):
                return PG[pg][:, lo:lo + 512].bitcast(BF16).rearrange("p (c n) -> p c n", c=4)
            R_b, K_b = bfv(2, 1024), bfv(2, 1536)
            V('tensor_copy', out=R_b[:, :, :nt], in_=R[:, :, :nt])
            V('tensor_copy', out=K_b[:, :, :nt], in_=K[:, :, :nt])
            AT_b, BT_b = bfv(0, 0), bfv(0, 512)
            V('tensor_copy', out=AT_b[:, :, :nt], in_=AT[:, :, :nt])
            V('tensor_copy', out=BT_b[:, :, :nt], in_=BT[:, :, :nt])
            my = [(ci, c0 - h0, T) for ci, (c0, T) in enumerate(chunks) if h0 <= c0 < h0 + nt]
            for (ci, c0, T) in my:
                if kind == 's':
                    rw_state_load(l, din['state_rwkv'][l, ci])
                A('activation', out=ST_rwb[:, :, :], in_=ST_rw[l][:, :, :], func=AF.Copy)
                W8 = 8 * T
                blk = lambda pg, i: PG[pg][:T, i * 512:i * 512 + W8 // 2].bitcast(BF16).rearrange("s (h t) -> s h t", h=8)
                Q, QT_, P_, AkM = blk(5, 0), blk(5, 1), blk(5, 2), blk(5, 3)
                RbM, RkM, Q2, QT2 = blk(6, 0), blk(6, 1), blk(6, 2), blk(6, 3)
                Vtok, UT, Bgt, Kgt = (PG[7][:T, i * 512:i * 512 + 256].bitcast(BF16) for i in range(4))
                RH, ytok = PG[8][:T, 0:256].bitcast(BF16), PG[8][:T, 512:1024]
                RHf = PG[8][:T, 256:512].bitcast(BF16)
                fs = lambda X, h: X[(h % 2) * 64:(h % 2) * 64 + 64, h // 2, c0:c0 + T]
                pss = [psr.next() for _ in range(5)]
                for h in range(8):
                    o = slice(h * T, (h + 1) * T)
                    MM(pss[0][:T, o], fs(BT_b, h), fs(AT_b, h), sg=(h == 7))
                    MM(pss[1][:T, o], fs(K_b, h), fs(AT_b, h), sg=(h == 7))
                    MM(pss[2][:T, o], fs(BT_b, h), fs(R_b, h), sg=(h == 7))
                    MM(pss[3][:T, o], fs(K_b, h), fs(R_b, h), sg=(h == 7))
                    MM(pss[4][:T, o], fs(AT_b, h), fs(BT_b, h), sg=(h == 7))
                pv = lambda p: p[:T, 0:W8].rearrange("s (h t) -> s h t", h=8)
                mb = lambda m: m[:T, :T].unsqueeze(1).broadcast_to([T, 8, T])
                V('tensor_tensor', out=Q, in0=pv(pss[0]), in1=mb(mlt), op=ALU.mult)
                V('tensor_tensor', out=AkM, in0=pv(pss[1]), in1=mb(mlt), op=ALU.mult)
                V('tensor_tensor', out=RbM, in0=pv(pss[2]), in1=mb(mle), op=ALU.mult)
                V('tensor_tensor', out=RkM, in0=pv(pss[3]), in1=mb(mle), op=ALU.mult)
                V('tensor_tensor', out=QT_, in0=pv(pss[4]), in1=mb(mgt), op=ALU.mult)
                V('tensor_tensor', out=P_, in0=Q, in1=mb(ident), op=ALU.add)
                if RWSTOP <= 3:
                    continue
                psv = psr.next()
                for c4 in range(4):
                    S.I('pe', 'transpose', psv[:T, c4 * 128:(c4 + 1) * 128], Vv[:, c4, c0:c0 + T], ident, sig=(c4 == 3))
                A('activation', out=Vtok, in_=psv[:T, 0:512], func=AF.Copy)
                for (srcX, dstX) in ((BT, Bgt), (K, Kgt)):
                    pst = psr.next()
                    for c4 in range(4):
                        tg = stg.next()
                        V('tensor_scalar', out=tg[:, :T], in0=srcX[:, c4, c0:c0 + T],
                          scalar1=EP[:, c4, c0 + T - 1:c0 + T], scalar2=None, op0=ALU.mult)
                        S.I('pe', 'transpose', pst[:T, c4 * 128:(c4 + 1) * 128], tg[:, :T], ident, sig=True)
                    A('activation', out=dstX, in_=pst[:T, 0:512], func=AF.Copy)
                nsteps = max(1, int(math.ceil(math.log2(T))))
                cq, cqt, nq, nqt = Q, QT_, Q2, QT2
                for j in range(nsteps - 1):
                    lastj = (j == nsteps - 2)
                    psQT = psr.next()
                    if not lastj:
                        psQ = psr.next()
                    for h in range(8):
                        MM(psQT[:T, h * T:(h + 1) * T], cq[:, h, :], cqt[:, h, :], sg=(h == 7))
                        if not lastj:
                            MM(psQ[:T, h * T:(h + 1) * T], cqt[:, h, :], cq[:, h, :], sg=(h == 7))
                    A('activation', out=nqt, in_=pv(psQT), func=AF.Copy)
                    if not lastj:
                        V('tensor_copy', out=nq, in_=pv(psQ))
                    cq, cqt, nq, nqt = nq, nqt, cq, cqt
                    psP = psr.next()
                    for h in range(8):
                        MM(psP[:T, h * T:(h + 1) * T], cqt[:, h, :], P_[:, h, :], sg=(h == 7))
                    V('tensor_tensor', out=P_, in0=P_, in1=pv(psP), op=ALU.add)
                if RWSTOP <= 4:
                    continue
                if RWSTOP <= 5:
                    continue
                psRe, psRo, psR2 = psr.next(), psr.next(), psr.next()
                for h in range(8):
                    o = slice(h * 64, (h + 1) * 64)
                    o2 = slice((h // 2) * 64, (h // 2 + 1) * 64)
                    MM((psRe, psRo)[h % 2][:T, o2], fs(AT_b, h), ST_rwb[(h % 2) * 64:(h % 2) * 64 + 64, h // 2, :], sg=(h >= 6))
                    MM(psR2[:T, o], AkM[:, h, :], Vtok[:, o], sg=(h == 7))
                RH4 = RH.rearrange("t (c hp v) -> t c hp v", c=4, hp=2)
                A('activation', out=RH4[:, :, 0, :], in_=psRe[:T, 0:256].rearrange("t (c v) -> t c v", c=4), func=AF.Copy)
                A('activation', out=RH4[:, :, 1, :], in_=psRo[:T, 0:256].rearrange("t (c v) -> t c v", c=4), func=AF.Copy)
                V('tensor_tensor', out=RH, in0=RH, in1=psR2[:T, 0:512], op=ALU.add)
                if RWSTOP <= 5.2:
                    continue
                psU = psr.next()
                for h in range(8):
                    o = slice(h * 64, (h + 1) * 64)
                    MM(psU[:T, o], P_[:, h, :], RH[:, o], sg=(h == 7))
                A('activation', out=UT, in_=psU[:T, 0:512], func=AF.Copy)
                if RWSTOP <= 5.4:
                    continue
                psYe, psYo, psY2 = psr.next(), psr.next(), psr.next()
                for h in range(8):
                    o = slice(h * 64, (h + 1) * 64)
                    o2 = slice((h // 2) * 64, (h // 2 + 1) * 64)
                    MM((psYe, psYo)[h % 2][:T, o2], fs(R_b, h), ST_rwb[(h % 2) * 64:(h % 2) * 64 + 64, h // 2, :], sg=(h >= 6))
                    MM(psY2[:T, o], RbM[:, h, :], UT[:, o], True, False)
                    MM(psY2[:T, o], RkM[:, h, :], Vtok[:, o], False, True, sg=(h == 7))
                Y4 = ytok.rearrange("t (c hp v) -> t c hp v", c=4, hp=2)
                A('activation', out=Y4[:, :, 0, :], in_=psYe[:T, 0:256].rearrange("t (c v) -> t c v", c=4), func=AF.Copy)
                A('activation', out=Y4[:, :, 1, :], in_=psYo[:T, 0:256].rearrange("t (c v) -> t c v", c=4), func=AF.Copy)
                V('tensor_tensor', out=ytok, in0=ytok, in1=psY2[:T, 0:512], op=ALU.add)
                if RWSTOP <= 5.6:
                    continue
                psyt = psr.next()
                for c4 in range(4):
                    S.I('pe', 'transpose', psyt[:, c4 * T:(c4 + 1) * T], ytok[:, c4 * 128:(c4 + 1) * 128], ident[:T, :T],
                        sig=(c4 == 3))
                A('activation', out=Yfm[:, :, c0:c0 + T], in_=psyt[:, 0:4 * T].rearrange("p (c t) -> p c t", c=4), func=AF.Copy)
                if RWSTOP <= 6:
                    continue
                for h in range(8):
                    c4, hp = h // 2, h % 2
                    o = slice(h * 64, (h + 1) * 64)
                    psS = psr.next()
                    MM(psS[:, 0:64], Bgt[:, c4 * 128:(c4 + 1) * 128], UT[:, o], True, False)
                    MM(psS[:, 0:64], Kgt[:, c4 * 128:(c4 + 1) * 128], Vtok[:, o], False, True)
                    rows = slice(hp * 64, hp * 64 + 64)
                    V('scalar_tensor_tensor', out=ST_rw[l][rows, c4, :], in0=ST_rw[l][rows, c4, :],
                      scalar=EP[rows, c4, c0 + T - 1:c0 + T], in1=psS[rows, 0:64], op0=ALU.mult, op1=ALU.add)
                if kind == 's':
                    rw_state_store(l, dout['s_rwkv'][l, ci])
            if RWSTOP <= 7:
                continue
            for c4 in range(4):
                ps = psr.next()
                MM(ps[:, :nt], bd64, Yfm[:, c4, :nt])
                cen = t1.next()
                V('scalar_tensor_tensor', out=cen[:, :nt], in0=ps[:, :nt], scalar=-1.0 / 64, in1=Yfm[:, c4, :nt],
                  op0=ALU.mult, op1=ALU.add)
                sq = t1.next()
                A('activation', out=sq[:, :nt], in_=cen[:, :nt], func=AF.Square)
                ps2 = psr.next()
                MM(ps2[:, :nt], bd64, sq[:, :nt])
                V('tensor_scalar', out=sq[:, :nt], in0=ps2[:, :nt], scalar1=1.0 / 64, scalar2=64e-5, op0=ALU.mult, op1=ALU.add)
                A('sqrt', sq[:, :nt], sq[:, :nt])
                V('reciprocal', sq[:, :nt], sq[:, :nt])
                V('tensor_tensor', out=cen[:, :nt], in0=cen[:, :nt], in1=sq[:, :nt], op=ALU.mult)
                V('tensor_scalar', out=cen[:, :nt], in0=cen[:, :nt], scalar1=par[:, P_LNW + c4:P_LNW + c4 + 1],
                  scalar2=par[:, P_LNB + c4:P_LNB + c4 + 1], op0=ALU.mult, op1=ALU.add)
                V('tensor_tensor', out=cen[:, :nt], in0=cen[:, :nt], in1=BON[:, c4, :nt], op=ALU.add)
                V('tensor_tensor', out=ybf[c4][:, h0:h0 + nt], in0=cen[:, :nt], in1=G[:, c4, :nt], op=ALU.mult)
        if RWSTOP <= 8:
            return
        if kind == 's':
            fm_to_tok_store(lambda c: shs[:, c, 0:NSEQ], 14, NSEQ, dout['s_rwkv_shift'][l])
        elif last:
            fm_to_tok_store(lambda c: sh_rw[l][:, c, :], 14, 1, dout['p_rwkv_shift'][l].rearrange("(o c) -> o c", o=1))
            rw_state_store(l, dout['p_rwkv'][l])
        branch_out(l, 3, lambda k: ybf[k][:, :ntk], 4, ntk, 'w_br_rw')


    tiles = [('p', i * 512, min(512, SEQ - i * 512)) for i in range((SEQ + 511) // 512)]
    tiles.append(('s', 0, NSEQ * TS))
    nprompt = len(tiles) - 1

    for ti, (kind, t0, ntk) in enumerate(tiles):
        last = (ti == nprompt - 1)
        src = din['x_prompt'] if kind == 'p' else din['x_sample']
        dsty = dout['y_prompt'] if kind == 'p' else dout['y_sample']
        nsub = (ntk + 127) // 128
        if kind == 'p':
            chunks128 = [(j * 128, min(128, ntk - j * 128)) for j in range(nsub)]
            chunks64 = [(j * 64, min(64, ntk - j * 64)) for j in range((ntk + 63) // 64)]
        else:
            chunks128 = [(b * TS, TS) for b in range(NSEQ)]
            chunks64 = chunks128
        for j in range(nsub):
            n = min(128, ntk - j * 128)
            lt = PG[7 + j % 2][:, 0:1024]
            S.dma('sp', lt[:n, :], src[t0 + j * 128:t0 + j * 128 + n, :])
            for k0 in range(0, KC, 4):
                ps = psr.next()
                for k in range(k0, k0 + 4):
                    S.I('pe', 'transpose', ps[:, (k - k0) * 128:(k - k0) * 128 + n], lt[:n, k * 128:(k + 1) * 128],
                        ident[:n, :n], sig=(k == k0 + 3))
                A('activation', out=x[:, k0:k0 + 4, j * 128:j * 128 + n],
                  in_=ps[:, :].rearrange("p (k n) -> p k n", k=4)[:, :, :n], func=AF.Copy)

        for l in range(L):
            colload(par[:, P_N1:P_N1 + 8], din['norm1_w'][l])
            colload(par[:, P_N2:P_N2 + 8], din['norm2_w'][l])
            colload(par[:, P_FCB:P_FCB + 44], din['ffn_conv_b'][l])
            for j in range(3):
                colload(par[:, P_FCW + 44 * j:P_FCW + 44 * j + 44], din['ffn_conv_w'][l, j])
            colload(par[:, P_BM:P_BM + 32], din['b_merge'][l])
            rmsnorm_x(par[:, P_N1:P_N1 + 8], ntk, xn)
            S.mrg_first = True
            if 'ssd' in EN:
                ssd_phase(l, kind, ntk, chunks128, last)
            if 's5' in EN:
                s5_phase(l, kind, ntk, chunks64, last)
            if 'hg' in EN:
                hg_phase(l, kind, ntk, chunks64, last)
            if 'rw' in EN:
                rw_phase(l, kind, ntk, chunks64, last)
            if S.mrg_first:
                V('memset', mrg[:, :, :ntk], 0.0)
            A('activation', out=xn[:, :, :ntk], in_=mrg[:, :, :ntk], func=AF.Copy)
            for u in range(2):
                wv = loadw(din['w_out'][l], u * 512, 512)
                for cb in range(4):
                    ps = dense_fm(wv, cb, lambda k: xn[:, k, :ntk], KC, ntk)
                    kk = u * 4 + cb
                    V('tensor_tensor', out=x[:, kk, :ntk], in0=x[:, kk, :ntk], in1=ps[:, :ntk], op=ALU.add)
            rmsnorm_x(par[:, P_N2:P_N2 + 8], ntk, xn)
            ffh = [PG[c // 8][:, :].bitcast(BF16)[:, (c % 8) * 512:(c % 8) * 512 + 512] for c in range(22)]
            if kind == 's':
                tok_to_fm_load(din['state_ffn_conv'][l].rearrange("b j c -> (b j) c"), 44, NSEQ * 2,
                               lambda c: shf[:, c, :])
            for ub in range(11):
                wg = loadw(din['ffn_up'][l], ub * 256, 256)
                wvv = loadw(din['ffn_up'][l], D_FF + ub * 256, 256)
                for cb in range(2):
                    c = ub * 2 + cb
                    res = []
                    for (wsel, cc) in ((wg, c), (wvv, 22 + c)):
                        ps = dense_fm(wsel, cb, lambda k: xn[:, k, :ntk], KC, ntk)
                        cv = t1.next()
                        if kind == 'p':
                            hist = newst = cv_ffn[l][:, cc, :]
                        else:
                            hist = newst = shf[:, cc, :].rearrange("p (b j) -> p b j", j=2)
                        conv_block(ps, ntk, kind, hist, newst,
                                   [par[:, P_FCW + 44 * j + cc:P_FCW + 44 * j + cc + 1] for j in range(3)],
                                   par[:, P_FCB + cc:P_FCB + cc + 1], 3, cv[:, :ntk])
                        res.append(cv)
                    g, v = res
                    u1 = t1.next()
                    A('activation', out=u1[:, :ntk], in_=g[:, :ntk], func=AF.Square)
                    V('tensor_scalar', out=u1[:, :ntk], in0=u1[:, :ntk], scalar1=0.044715, scalar2=1.0,
                      op0=ALU.mult, op1=ALU.add)
                    V('tensor_tensor', out=u1[:, :ntk], in0=u1[:, :ntk], in1=g[:, :ntk], op=ALU.mult)
                    A('activation', out=u1[:, :ntk], in_=u1[:, :ntk], func=AF.Sigmoid, scale=1.5957691216)
                    V('tensor_tensor', out=u1[:, :ntk], in0=u1[:, :ntk], in1=g[:, :ntk], op=ALU.mult)
                    V('tensor_tensor', out=ffh[c][:, :ntk], in0=u1[:, :ntk], in1=v[:, :ntk], op=ALU.mult)
            if kind == 's':
                fm_to_tok_store(lambda c: shf[:, c, :], 44, NSEQ * 2,
                                dout['s_ffn_conv'][l].rearrange("b j c -> (b j) c"))
            elif last:
                fm_to_tok_store(lambda c: cv_ffn[l][:, c, :], 44, 2, dout['p_ffn_conv'][l])
            for cb in range(8):
                wv = loadw(din['ffn_down'][l], cb * 128, 128)
                ps = psr.next()
                for k in range(22):
                    MM(ps[:, :ntk], wv[:, k, :], ffh[k][:, :ntk], k == 0, k == 21)
                V('tensor_tensor', out=x[:, cb, :ntk], in0=x[:, cb, :ntk], in1=ps[:, :ntk], op=ALU.add)

        colload(par[:, 500:508], din['final_norm_w'])
        sumsq_rstd([x[:, k, :ntk] for k in range(KC)], ntk, D, EPS)
        yo = [PG[k // 4][:, (k % 4) * 512:(k % 4) * 512 + 512] for k in range(8)]
        for k in range(KC):
            V('scalar_tensor_tensor', out=yo[k][:, :ntk], in0=x[:, k, :ntk], scalar=par[:, 500 + k:501 + k],
              in1=rstd[:, :ntk], op0=ALU.mult, op1=ALU.mult)
        for j in range(nsub):
            n = min(128, ntk - j * 128)
            so = PG[7 + j % 2][:, 0:1024]
            for k0 in range(0, KC, 4):
                ps = psr.next()
                for k in range(k0, k0 + 4):
                    S.I('pe', 'transpose', ps[:n, (k - k0) * 128:(k - k0 + 1) * 128], yo[k][:, j * 128:j * 128 + n],
                        ident, sig=(k == k0 + 3))
                A('activation', out=so[:n, k0 * 128:(k0 + 4) * 128], in_=ps[:n, :], func=AF.Copy)
            S.dma('sp', dsty[t0 + j * 128:t0 + j * 128 + n, :], so[:n, :])

    S.finish()


OUT_ORDER = ['y_prompt', 'y_sample'] + ['p_' + n for n in STATE_NAMES] + ['s_' + n for n in STATE_NAMES]


def kernel(_cfg=None, **inputs):
    inputs = {k: np.asarray(v) for k, v in inputs.items()}
    x_prompt, x_sample = inputs['x_prompt'], inputs['x_sample']
    B, SEQ, _ = x_prompt.shape
    DB, TS, _ = x_sample.shape
    DEPTH = inputs['w_in'].shape[0]
    nseq = DB // NCORES
    cfg = dict(depth=DEPTH, seq=SEQ, nseq=nseq)
    if _cfg:
        cfg.update(_cfg)
    consts = make_consts()
    in_maps = []
    for c in range(NCORES):
        m = {}
        for k in IN_NAMES:
            a = inputs[k]
            if k == 'x_prompt':
                a = a[c]
            elif k == 'x_sample':
                a = a[c * nseq:(c + 1) * nseq].reshape(nseq * TS, D)
            elif k.startswith('state_'):
                a = a[:, c * nseq:(c + 1) * nseq]
            m[k] = np.ascontiguousarray(a, dtype=np.float32)
        m['consts'] = consts
        in_maps.append(m)
    shapes = {k: in_maps[0][k].shape for k in IN_NAMES}
    nc = build(cfg, shapes)
    res = run_bass_kernel_spmd(nc, in_maps, core_ids=list(range(NCORES)))
    r = res.results
    outs = []
    for name in OUT_ORDER:
        if name == 'y_prompt':
            outs.append(np.stack([r[c][name] for c in range(NCORES)], 0))
        elif name == 'y_sample':
            outs.append(np.concatenate([r[c][name].reshape(nseq, TS, D) for c in range(NCORES)], 0))
        elif name.startswith('p_'):
            outs.append(np.stack([r[c][name] for c in range(NCORES)], 1))
        else:
            outs.append(np.concatenate([r[c][name] for c in range(NCORES)], 1))
    return tuple(outs)
```

```python
import numpy as np
import concourse.bass as bass
import concourse.mybir as mybir
from concourse.bass_utils import run_bass_kernel_spmd

F32 = mybir.dt.float32
BF16 = mybir.dt.bfloat16
AF = mybir.ActivationFunctionType
ALU = mybir.AluOpType
AX = mybir.AxisListType

D = 1024
KC = D // 128
NCORES = 8
EPS = 1e-6
D_FF = 2816


class Buf:
    def __init__(self, t):
        self.t = t
        self.last_w = None
        self.readers = {}
        self.dsem = None
        self.dcount = 0

    def __getitem__(self, k):
        return self.t[k]


class Sched:
    def __init__(self, nc):
        self.nc = nc
        self.eng = {'pe': nc.tensor, 'dve': nc.vector, 'act': nc.scalar, 'pool': nc.gpsimd, 'sp': nc.sync}
        self.sem = {e: nc.alloc_semaphore(name='s_' + e) for e in ('pe', 'dve', 'act', 'pool')}
        self.cnt = {e: 0 for e in self.sem}
        self.epoch = {e: 0 for e in self.sem}
        self.all_dma = []
        self.LIMIT = 8000
        self.seen = {e: {} for e in self.eng}
        self.bufs = {}
        self.nbuf = 0
        self.pending_pe = []

    def sb(self, shape, dt=F32, name=None):
        self.nbuf += 1
        name = name or ('b%d' % self.nbuf)
        b = Buf(self.nc.alloc_sbuf_tensor(name, list(shape), dt))
        self.bufs[name] = b
        return b

    def ps(self, shape, dt=F32, name=None):
        self.nbuf += 1
        name = name or ('p%d' % self.nbuf)
        b = Buf(self.nc.alloc_psum_tensor(name, list(shape), dt))
        self.bufs[name] = b
        return b

    def _wait(self, e, tok):
        sem, val, key = tok
        if key.startswith('pe#') and e == 'pe':
            return
        if self.seen[e].get(key, 0) >= val:
            return
        self.eng[e].wait_ge(sem, val)
        self.seen[e][key] = val

    def _find(self, ap):
        try:
            return self.bufs.get(ap.tensor.name)
        except Exception:
            return None

    def _deps(self, e, kw, args):
        outs, ins = [], []
        items = list(kw.items()) + [('out' if i == 0 else 'in%d' % i, a) for i, a in enumerate(args)]
        for k, v in items:
            if isinstance(v, bass.AP):
                b = self._find(v)
                if b is None:
                    continue
                (outs if k in ('out', 'accum_out') else ins).append(b)
        for b in ins:
            if b.last_w:
                self._wait(e, b.last_w)
        for b in outs:
            if b.last_w:
                self._wait(e, b.last_w)
            for tok in b.readers.values():
                self._wait(e, tok)
        return outs, ins

    def I(self, e, fn, *args, sig=True, **kw):
        outs, ins = self._deps(e, kw, args)
        ins_obj = getattr(self.eng[e], fn)(*args, **kw)
        key = '%s#%d' % (e, self.epoch[e])
        if sig:
            self.cnt[e] += 1
            ins_obj.then_inc(self.sem[e], 1)
            tok = (self.sem[e], self.cnt[e], key)
        else:
            tok = (self.sem[e], self.cnt[e] + 1, key)
        for b in ins:
            b.readers[key] = tok
        for b in outs:
            b.last_w = tok
            b.readers = {}
        if sig and self.cnt[e] >= self.LIMIT:
            self.epoch[e] += 1
            self.sem[e] = self.nc.alloc_semaphore(name='s_%s_%d' % (e, self.epoch[e]))
            self.cnt[e] = 0
        return ins_obj

    def dma(self, q, out, in_):
        saved = []
        ob = self._find(out)
        if ob is not None and ob.last_w is not None and ob.last_w[2].startswith('d_' + ob.t.name + '_') \
                and ob.dcount < self.LIMIT:
            saved.append((ob, ob.last_w))
            ob.last_w = None
        outs, ins = self._deps(q, {'out': out, 'in_': in_}, ())
        for (bb, lw) in saved:
            bb.last_w = lw
        ins_obj = self.eng[q].dma_start(out=out, in_=in_)
        b = (outs + ins)[0]
        if b.dsem is None or b.dcount >= self.LIMIT:
            if b.dsem is not None:
                self.all_dma.append((b.dsem, b.dcount, 'd_%s_%d' % (b.t.name, b.depoch)))
            b.depoch = getattr(b, 'depoch', -1) + 1
            b.dsem = self.nc.alloc_semaphore(name='d_%s_%d' % (b.t.name, b.depoch))
            b.dcount = 0
        b.dcount += 16
        ins_obj.then_inc(b.dsem, 16)
        tok = (b.dsem, b.dcount, 'd_%s_%d' % (b.t.name, b.depoch))
        for x in ins:
            x.readers[tok[2]] = tok
        for x in outs:
            x.last_w = tok
            x.readers = {}

    def finish(self):
        for tok in self.all_dma:
            self._wait('sp', tok)
        for b in self.bufs.values():
            if b.dsem is not None:
                self._wait('sp', (b.dsem, b.dcount, 'd_%s_%d' % (b.t.name, b.depoch)))


class Ring:
    def __init__(self, bufs):
        self.bufs = bufs
        self.i = 0

    def next(self):
        b = self.bufs[self.i % len(self.bufs)]
        self.i += 1
        return b


NCONST = 2152
C_IDENT, C_ONES, C_MLE, C_MLT, C_MGT, C_NEG, C_BD64, C_POS, C_R64, C_R8, C_MSKB, C_MSKC, C_R8L = (
    0, 128, 256, 384, 512, 640, 768, 896, 960, 1472, 1600, 1632, 1640)


def make_consts():
    p = np.arange(128)[:, None]
    f = np.arange(128)[None, :]
    parts = [
        (p == f), np.ones((128, 128)), (p <= f), (p < f), (p > f), np.where(p > f, -30000.0, 0.0),
        (p // 64 == f // 64),
        np.broadcast_to(np.arange(1, 65)[None, :], (128, 64)),
        np.broadcast_to((np.arange(512) % 64 != 0)[None, :], (128, 512)),
        np.broadcast_to((np.arange(128) % 8 != 0)[None, :], (128, 128)),
        (np.arange(8)[None, None, :] == 2 * np.arange(4)[None, :, None] + (np.arange(128) // 64)[:, None, None]).reshape(128, 32),
        ((np.arange(128) // 16)[:, None, None] == 2 * np.arange(4)[None, :, None] + np.arange(2)[None, None, :]).reshape(128, 8),
        np.broadcast_to((np.arange(512) % 8 != 0)[None, :], (128, 512)),
    ]
    return np.ascontiguousarray(np.concatenate([np.asarray(a, np.float32) for a in parts], axis=1))


IN_NAMES = ['x_prompt', 'x_sample', 'state_ssd', 'state_ssd_conv', 'state_s5_re', 'state_s5_im', 'state_hgrn',
            'state_rwkv', 'state_rwkv_shift', 'state_ffn_conv',
            'norm1_w', 'w_in', 'ssd_conv_w', 'ssd_conv_b', 'ssd_dt_bias', 'ssd_a_log', 'ssd_d', 'ssd_norm_w',
            's5_a_re', 's5_a_im', 's5_log_dt', 's5_b_re', 's5_b_im', 's5_c_re', 's5_c_im', 's5_d', 's5_glu_w',
            's5_glu_b', 'hg_lb_raw', 'hg_norm_w',
            'rw_mu', 'rw_w0', 'rw_w_up', 'rw_a0', 'rw_a_up', 'rw_g_up', 'rw_k_k', 'rw_k_a', 'rw_r_k', 'rw_ln_w',
            'rw_ln_b', 'w_br_ssd', 'w_br_s5', 'w_br_hg', 'w_br_rw', 'w_merge', 'b_merge', 'w_out',
            'norm2_w', 'ffn_up', 'ffn_conv_w', 'ffn_conv_b', 'ffn_down', 'final_norm_w']
STATE_NAMES = ['ssd', 'ssd_conv', 's5_re', 's5_im', 'hgrn', 'rwkv', 'rwkv_shift', 'ffn_conv']


def build(cfg, shapes):
    nc = bass.Bass("TRN2", target_bir_lowering=False)
    with nc.allow_non_contiguous_dma(reason="small per-channel parameter loads"):
        _build(nc, cfg, shapes)
    return nc


def _build(nc, cfg, shapes):
    L, SEQ, NSEQ = cfg['depth'], cfg['seq'], cfg['nseq']
    EN = cfg.get('enable', ('ssd', 's5', 'hg', 'rw'))
    TS = 8
    S = Sched(nc)
    V = lambda fn, *a, **k: S.I('dve', fn, *a, **k)
    A = lambda fn, *a, **k: S.I('act', fn, *a, **k)

    def MM(out, lhsT, rhs, st=True, sp=True, sg=None):
        return S.I('pe', 'matmul', out, lhsT, rhs, start=st, stop=sp, sig=(sp if sg is None else sg))

    din = {}
    for n in IN_NAMES:
        din[n] = nc.dram_tensor(n, list(shapes[n]), F32, kind="ExternalInput").ap()
    din['consts'] = nc.dram_tensor('consts', [128, NCONST], F32, kind="ExternalInput").ap()
    st_shapes = {'ssd': [16, 64, 64], 'ssd_conv': [3, 1536], 's5_re': [32, 64], 's5_im': [32, 64],
                 'hgrn': [4, 128, 128], 'rwkv': [8, 64, 64], 'rwkv_shift': [1792], 'ffn_conv': [2, 5632]}
    dout = {}
    dout['y_prompt'] = nc.dram_tensor('y_prompt', [SEQ, D], F32, kind="ExternalOutput").ap()
    dout['y_sample'] = nc.dram_tensor('y_sample', [NSEQ * TS, D], F32, kind="ExternalOutput").ap()
    for n in STATE_NAMES:
        dout['p_' + n] = nc.dram_tensor('p_' + n, [L] + st_shapes[n], F32, kind="ExternalOutput").ap()
        dout['s_' + n] = nc.dram_tensor('s_' + n, [L, NSEQ] + st_shapes[n], F32, kind="ExternalOutput").ap()

    cst = S.sb([128, NCONST], F32, 'cst')
    S.dma('sp', cst[:, :], din['consts'][:, :])
    ident = cst[:, C_IDENT:C_IDENT + 128]
    ones = cst[:, C_ONES:C_ONES + 128]
    mle = cst[:, C_MLE:C_MLE + 128]
    mlt = cst[:, C_MLT:C_MLT + 128]
    mgt = cst[:, C_MGT:C_MGT + 128]
    negm = cst[:, C_NEG:C_NEG + 128]
    bd64 = cst[:, C_BD64:C_BD64 + 128]

    def TR(out, in_, n):
        return S.I('pe', 'transpose', out, in_, ident[:n, :n])

    psr = Ring([S.ps([128, 512], F32, 'psb%d' % i) for i in range(8)])
    wring = Ring([S.sb([128, 4096], BF16, 'wr%d' % i) for i in range(3)])
    NTKMAX = 512
    x = S.sb([128, KC, NTKMAX], F32, 'x')
    xn = S.sb([128, KC, NTKMAX], BF16, 'xn')
    mrg = S.sb([128, KC, NTKMAX], F32, 'mrg')
    rstd = S.sb([128, NTKMAX], F32, 'rstd')
    par = S.sb([128, 512], F32, 'par')
    parb = S.sb([128, 256], F32, 'parb')
    S5BIG0 = S.sb([128, 128], F32, 's5big0')
    S5BIG1 = S.sb([128, 128], F32, 's5big1')
    t1 = Ring([S.sb([128, NTKMAX], F32, 't1_%d' % i) for i in range(3)])
    stg = Ring([S.sb([128, NTKMAX + 16], F32, 'stg%d' % i) for i in range(2)])
    PG = [S.sb([128, 2048], F32, 'pg%d' % i) for i in range(9)]
    ST_ssd = [S.sb([128, 2, 256], F32, 'stssd%d' % l) for l in range(L)]
    cv_ssd = [S.sb([128, 12, 3], F32, 'cvssd%d' % l) for l in range(L)]
    cv_ffn = [S.sb([128, 44, 2], F32, 'cvffn%d' % l) for l in range(L)]
    for l in range(L):
        V('memset', ST_ssd[l][:], 0.0)
        V('memset', cv_ssd[l][:], 0.0)
        V('memset', cv_ffn[l][:], 0.0)
    shs = S.sb([128, 14, NSEQ * 3], F32, 'shs')
    shf = PG[3][:, 0:44 * NSEQ * 2].rearrange("p (c r) -> p c r", c=44)

    def loadw(dram2d, c0, ncols, r0=0, K=None):
        K = K or dram2d.shape[0]
        kc = K // 128
        b = wring.next()
        view = b.t[:, 0:kc * ncols].rearrange("p (k n) -> p k n", k=kc)
        S.dma('pool', view, dram2d[r0:r0 + K, :].rearrange("(k p) n -> p k n", p=128)[:, :, c0:c0 + ncols])
        return view

    def colload(dst, vec):
        S.dma('sp', dst, vec.rearrange("(k p) -> p k", p=128))

    def rowload(dst, vec, rows=128):
        S.dma('sp', dst, vec.partition_broadcast(rows))

    def sumsq_rstd(src_chunks, ntk, nfeat, eps):
        ps = psr.next()
        n = len(src_chunks)
        for k, sc in enumerate(src_chunks):
            tq = t1.next()
            A('activation', out=tq[:, :ntk], in_=sc, func=AF.Square)
            MM(ps[:, :ntk], ones, tq[:, :ntk], k == 0, k == n - 1, sg=True)
        V('tensor_scalar', out=rstd[:, :ntk], in0=ps[:, :ntk], scalar1=1.0 / nfeat, scalar2=eps,
          op0=ALU.mult, op1=ALU.add)
        A('sqrt', rstd[:, :ntk], rstd[:, :ntk])
        V('reciprocal', rstd[:, :ntk], rstd[:, :ntk])

    def rmsnorm_x(w_col, ntk, dst):
        sumsq_rstd([x[:, k, :ntk] for k in range(KC)], ntk, D, EPS)
        for k in range(KC):
            V('scalar_tensor_tensor', out=dst[:, k, :ntk], in0=x[:, k, :ntk], scalar=w_col[:, k:k + 1],
              in1=rstd[:, :ntk], op0=ALU.mult, op1=ALU.mult)

    def dense_fm(wv, cb, src, kc, ntk):
        ps = psr.next()
        for k in range(kc):
            MM(ps[:, :ntk], wv[:, k, cb * 128:(cb + 1) * 128], src(k), k == 0, k == kc - 1)
        return ps

    def fm_to_tok_store(src, nch, rows, dram2d):
        for g0 in range(0, nch, 4):
            ng = min(4, nch - g0)
            ps = psr.next()
            for i in range(ng):
                S.I('pe', 'transpose', ps[:rows, i * 128:(i + 1) * 128], src(g0 + i), ident, sig=(i == ng - 1))
            so = t1.next()
            A('activation', out=so[:rows, :ng * 128], in_=ps[:rows, :ng * 128], func=AF.Copy)
            S.dma('sp', dram2d[:, g0 * 128:(g0 + ng) * 128], so[:rows, :ng * 128])

    def tok_to_fm_load(dram2d, nch, rows, dst):
        for g0 in range(0, nch, 4):
            ng = min(4, nch - g0)
            lt = t1.next()
            S.dma('sp', lt[:rows, :ng * 128], dram2d[:, g0 * 128:(g0 + ng) * 128])
            ps = psr.next()
            for i in range(ng):
                S.I('pe', 'transpose', ps[:, i * rows:(i + 1) * rows], lt[:rows, i * 128:(i + 1) * 128],
                    ident[:rows, :rows], sig=(i == ng - 1))
            for i in range(ng):
                A('activation', out=dst(g0 + i), in_=ps[:, i * rows:(i + 1) * rows], func=AF.Copy)

    def conv_block(ps, ntk, kind, hist, newst, wcols, bcol, ntap, out_ap):
        h = ntap - 1
        sg = stg.next()
        if kind == 'p':
            A('activation', out=sg[:, h:h + ntk], in_=ps[:, :ntk], func=AF.Copy)
            V('tensor_copy', out=sg[:, 0:h], in_=hist)
            V('tensor_copy', out=newst, in_=sg[:, ntk:ntk + h])
            full = lambda j: sg[:, j:j + ntk]
            o = out_ap
        else:
            w = TS + h
            v3 = sg[:, 0:NSEQ * w].rearrange("p (b t) -> p b t", t=w)
            A('activation', out=v3[:, :, h:w], in_=ps[:, :ntk].rearrange("p (b t) -> p b t", t=TS), func=AF.Copy)
            V('tensor_copy', out=v3[:, :, 0:h], in_=hist)
            V('tensor_copy', out=newst, in_=v3[:, :, TS:w])
            full = lambda j: v3[:, :, j:j + TS]
            o = out_ap.rearrange("p (b t) -> p b t", t=TS)
        V('tensor_scalar', out=o, in0=full(0), scalar1=wcols[0], scalar2=bcol, op0=ALU.mult, op1=ALU.add)
        for j in range(1, ntap):
            V('scalar_tensor_tensor', out=o, in0=full(j), scalar=wcols[j], in1=o, op0=ALU.mult, op1=ALU.add)

    P_N1, P_N2, P_FCB, P_FCW, P_BM, P_SCW, P_SCB, P_SNW, P_SD = 0, 8, 16, 60, 192, 224, 272, 284, 292
    B_DTB, B_A = 0, 16

    def branch_out(l, bi, ysrc, kc, ntk, wname):
        for u in range(2):
            wb = loadw(din[wname][l], u * 512, 512)
            wm = loadw(din['w_merge'][l], bi * D + u * 512, 512)
            for cb in range(4):
                cc = u * 4 + cb
                psB = dense_fm(wb, cb, ysrc, kc, ntk)
                psG = dense_fm(wm, cb, lambda k: xn[:, k, :ntk], KC, ntk)
                tg = t1.next()
                A('activation', out=tg[:, :ntk], in_=psG[:, :ntk], func=AF.Sigmoid,
                  bias=par[:, P_BM + bi * 8 + cc:P_BM + bi * 8 + cc + 1])
                if S.mrg_first:
                    V('tensor_tensor', out=mrg[:, cc, :ntk], in0=tg[:, :ntk], in1=psB[:, :ntk], op=ALU.mult)
                else:
                    V('tensor_tensor', out=tg[:, :ntk], in0=tg[:, :ntk], in1=psB[:, :ntk], op=ALU.mult)
                    V('tensor_tensor', out=mrg[:, cc, :ntk], in0=mrg[:, cc, :ntk], in1=tg[:, :ntk], op=ALU.add)
        S.mrg_first = False

    def ssd_state_store(l, dram3):
        ps = psr.next()
        for j in range(2):
            for q in range(2):
                S.I('pe', 'transpose', ps[:, (j * 2 + q) * 128:(j * 2 + q + 1) * 128],
                    ST_ssd[l][:, j, q * 128:(q + 1) * 128], ident, sig=(j == 1 and q == 1))
        so = t1.next()
        A('activation', out=so[:, :512], in_=ps[:, :512], func=AF.Copy)
        for j in range(2):
            for g2 in range(2):
                for q in range(2):
                    h0 = 8 * j + 4 * g2 + 2 * q
                    S.dma('sp', dram3[h0:h0 + 2].rearrange("h p n -> (h p) n"),
                          so[:, (j * 2 + q) * 128 + g2 * 64:(j * 2 + q) * 128 + g2 * 64 + 64])

    def ssd_state_load(l, dram3):
        lt = t1.next()
        for j in range(2):
            for g2 in range(2):
                for q in range(2):
                    h0 = 8 * j + 4 * g2 + 2 * q
                    S.dma('sp', lt[:, (j * 2 + q) * 128 + g2 * 64:(j * 2 + q) * 128 + g2 * 64 + 64],
                          dram3[h0:h0 + 2].rearrange("h p n -> (h p) n"))
        ps = psr.next()
        for j in range(2):
            for q in range(2):
                S.I('pe', 'transpose', ps[:, (j * 2 + q) * 128:(j * 2 + q + 1) * 128],
                    lt[:, (j * 2 + q) * 128:(j * 2 + q + 1) * 128], ident, sig=(j == 1 and q == 1))
        A('activation', out=ST_ssd[l][:, :, :], in_=ps[:, :512].rearrange("p (j c) -> p j c", j=2), func=AF.Copy)

    ST_ssdb = S.sb([128, 2, 256], BF16, 'stssdb')

    def ssd_phase(l, kind, ntk, chunks, last):
        XCbv = PG[8][:, 1024:2048].bitcast(BF16)
        XCb = [XCbv[:, j * 512:(j + 1) * 512] for j in range(4)]
        XC = [PG[c // 4][:, (c % 4) * 512:(c % 4) * 512 + 512] for c in range(12)]
        szv = PG[3][:, :].bitcast(BF16)
        sz = [szv[:, k * 512:(k + 1) * 512] for k in range(8)]
        yss = [PG[4 + k // 4][:, (k % 4) * 512:(k % 4) * 512 + 512] for k in range(8)]
        for j in range(4):
            colload(par[:, P_SCW + 12 * j:P_SCW + 12 * j + 12], din['ssd_conv_w'][l, j])
        colload(par[:, P_SCB:P_SCB + 12], din['ssd_conv_b'][l])
        colload(par[:, P_SNW:P_SNW + 8], din['ssd_norm_w'][l])
        for h in range(16):
            S.dma('sp', par[(h % 2) * 64:(h % 2) * 64 + 64, P_SD + h // 2:P_SD + h // 2 + 1],
                  din['ssd_d'][l, h:h + 1].partition_broadcast(64))
        rowload(parb[:, B_DTB:B_DTB + 16], din['ssd_dt_bias'][l])
        rowload(parb[:, B_A:B_A + 16], din['ssd_a_log'][l])
        A('activation', out=parb[:, B_A:B_A + 16], in_=parb[:, B_A:B_A + 16], func=AF.Exp)
        V('tensor_scalar', out=parb[:, B_A:B_A + 16], in0=parb[:, B_A:B_A + 16], scalar1=-1.0, scalar2=None,
          op0=ALU.mult)
        for u in range(2):
            wv = loadw(din['w_in'][l], u * 512, 512)
            for cb in range(4):
                ps = dense_fm(wv, cb, lambda k: xn[:, k, :ntk], KC, ntk)
                tq = t1.next()
                A('activation', out=tq[:, :ntk], in_=ps[:, :ntk], func=AF.Sigmoid)
                V('tensor_tensor', out=sz[u * 4 + cb][:, :ntk], in0=tq[:, :ntk], in1=ps[:, :ntk], op=ALU.mult)
        if kind == 's':
            for b in range(NSEQ):
                pass
            tok_to_fm_load(din['state_ssd_conv'][l].rearrange("b j c -> (b j) c"), 12, NSEQ * 3,
                           lambda c: shs[:, c, :])
        for u in range(3):
            wv = loadw(din['w_in'][l], 1024 + u * 512, 512)
            for cb in range(4):
                c = u * 4 + cb
                ps = dense_fm(wv, cb, lambda k: xn[:, k, :ntk], KC, ntk)
                tq = t1.next()
                if kind == 'p':
                    hist, newst = cv_ssd[l][:, c, :], cv_ssd[l][:, c, :]
                else:
                    hist = newst = shs[:, c, :].rearrange("p (b j) -> p b j", j=3)
                conv_block(ps, ntk, kind, hist, newst, [par[:, P_SCW + 12 * j + c:P_SCW + 12 * j + c + 1] for j in range(4)],
                           par[:, P_SCB + c:P_SCB + c + 1], 4, tq[:, :ntk])
                tq2 = t1.next()
                A('activation', out=tq2[:, :ntk], in_=tq[:, :ntk], func=AF.Sigmoid)
                V('tensor_tensor', out=XC[c][:, :ntk], in0=tq2[:, :ntk], in1=tq[:, :ntk], op=ALU.mult)
                if c >= 8:
                    A('activation', out=XCb[c - 8][:, :ntk], in_=XC[c][:, :ntk], func=AF.Copy)
        if kind == 's':
            fm_to_tok_store(lambda c: shs[:, c, :], 12, NSEQ * 3,
                            dout['s_ssd_conv'][l].rearrange("b j c -> (b j) c"))
        elif last:
            fm_to_tok_store(lambda c: cv_ssd[l][:, c, :], 12, 3, dout['p_ssd_conv'][l])
        wdt = loadw(din['w_in'][l], 2560, 16)
        pg6, pg7, pg8 = PG[6], PG[7], PG[8]
        for ci, (c0, T) in enumerate(chunks):
            ls = l
            if kind == 's':
                ls = ci % L
                if ci == 0:
                    ssd_state_load(0, din['state_ssd'][l, 0])
                if ci + 1 < len(chunks):
                    ssd_state_load((ci + 1) % L, din['state_ssd'][l, ci + 1])
            dtv, dta, acum, wl = pg8[:T, 0:16], pg8[:T, 16:32], pg8[:T, 32:48], pg8[:T, 48:64]
            ETb = pg8[:, 64:80]
            Btok = pg8[:T, 128:256].bitcast(BF16)
            dtw = pg8[:T, 80:96]
            A('activation', out=ST_ssdb[:, :, :], in_=ST_ssd[ls][:, :, :], func=AF.Copy)
            tY = pg8[:, 384:512]
            ps = psr.next()
            for k in range(KC):
                MM(ps[:T, 0:16], xn[:, k, c0:c0 + T], wdt[:, k, 0:16], k == 0, k == KC - 1)
            V('tensor_tensor', out=dtv, in0=ps[:T, 0:16], in1=parb[:T, B_DTB:B_DTB + 16], op=ALU.add)
            A('activation', out=dtv, in_=dtv, func=AF.Exp)
            V('tensor_scalar', out=dtv, in0=dtv, scalar1=1.0, scalar2=None, op0=ALU.add)
            A('activation', out=dtv, in_=dtv, func=AF.Ln)
            V('tensor_tensor', out=dta, in0=dtv, in1=parb[:T, B_A:B_A + 16], op=ALU.mult)
            ps = psr.next()
            MM(ps[:T, 0:16], mle[:T, :T], dta)
            MM(ps[:, 16:32], ones[:T, :], dta)
            A('activation', out=acum, in_=ps[:T, 0:16], func=AF.Copy)
            A('activation', out=ETb, in_=ps[:, 16:32], func=AF.Exp)
            V('tensor_tensor', out=wl, in0=ps[:T, 16:32], in1=acum, op=ALU.subtract)
            A('activation', out=wl, in_=wl, func=AF.Exp)
            XDT = pg7[:T, 0:512].bitcast(BF16)
            XDTW = pg7[:T, 512:1024].bitcast(BF16)
            V('tensor_tensor', out=dtw, in0=dtv, in1=wl, op=ALU.mult)
            for half in range(2):
                ps = psr.next()
                for i in range(4):
                    S.I('pe', 'transpose', ps[:T, i * 128:(i + 1) * 128], XC[half * 4 + i][:, c0:c0 + T], ident,
                        sig=(i == 3))
                V('tensor_tensor', out=XDT[:, half * 512:(half + 1) * 512].rearrange("t (h p) -> t h p", p=64),
                  in0=ps[:T, :512].rearrange("t (h p) -> t h p", p=64),
                  in1=dtv[:, half * 8:half * 8 + 8].unsqueeze(2).broadcast_to([T, 8, 64]), op=ALU.mult)
                V('tensor_tensor', out=XDTW[:, half * 512:(half + 1) * 512].rearrange("t (h p) -> t h p", p=64),
                  in0=ps[:T, :512].rearrange("t (h p) -> t h p", p=64),
                  in1=dtw[:, half * 8:half * 8 + 8].unsqueeze(2).broadcast_to([T, 8, 64]), op=ALU.mult)
            ps = psr.next()
            for i in range(2):
                S.I('pe', 'transpose', ps[:T, i * 128:(i + 1) * 128], XC[8 + i][:, c0:c0 + T], ident, sig=(i == 1))
            A('activation', out=Btok, in_=ps[:T, 0:256], func=AF.Copy)
            if T <= 32:
                HT = 16 * T
                D16 = pg6[:T, 0:HT].rearrange("s (h t) -> s h t", h=16)
                E16 = pg6[:T, 512:512 + HT].rearrange("s (h t) -> s h t", h=16)
                EB16 = pg6[:, 1024:1024 + HT]
                M16f = pg6[:T, 1536:1536 + HT // 2].bitcast(BF16)
                M16 = M16f.rearrange("s (h t) -> s h t", h=16)
                V('tensor_tensor', out=D16, in0=dta[:, 0:16].unsqueeze(2).broadcast_to([T, 16, T]),
                  in1=mle[:T, :T].unsqueeze(1).broadcast_to([T, 16, T]), op=ALU.mult)
                psAB = psr.next()
                MM(psAB[:, 0:HT], ones[:T, :], pg6[:T, 0:HT])
                A('activation', out=EB16, in_=psAB[:, 0:HT], func=AF.Exp)
                V('tensor_tensor', out=E16, in0=psAB[:T, 0:HT].rearrange("s (h t) -> s h t", h=16),
                  in1=acum[:, 0:16].unsqueeze(2).broadcast_to([T, 16, T]), op=ALU.subtract)
                V('tensor_tensor', out=E16, in0=E16, in1=negm[:T, :T].unsqueeze(1).broadcast_to([T, 16, T]), op=ALU.add)
                A('activation', out=E16, in_=E16, func=AF.Exp)
                psGp = [psr.next(), psr.next()]
                for g in range(4):
                    gp = slice((g % 2) * 64, (g % 2) * 64 + 64)
                    MM(psGp[g % 2][:T, (g // 2) * T:(g // 2 + 1) * T], XCb[g // 2][gp, c0:c0 + T], XCb[2 + g // 2][gp, c0:c0 + T],
                       sg=(g >= 2))
                E5 = pg6[:T, 512:512 + HT].rearrange("s (j q h t) -> s j q h t", j=2, q=2, h=4)
                M5 = M16f.rearrange("s (j q h t) -> s j q h t", j=2, q=2, h=4)
                for q in range(2):
                    V('tensor_tensor', out=M5[:, :, q, :, :], in0=E5[:, :, q, :, :],
                      in1=psGp[q][:T, 0:2 * T].rearrange("s (j t) -> s j t", j=2).unsqueeze(2).broadcast_to([T, 2, 4, T]),
                      op=ALU.mult)
                psOp = [psr.next(), psr.next()]
                psY = psr.next()
                for g in range(4):
                    gp = slice((g % 2) * 64, (g % 2) * 64 + 64)
                    for hp in range(2):
                        slot = (g // 2) * 2 + hp
                        MM(psOp[g % 2][:, slot * T:(slot + 1) * T], ST_ssdb[gp, g // 2, hp * 128:(hp + 1) * 128],
                           XCb[2 + g // 2][gp, c0:c0 + T], sg=(g >= 2 and hp == 1))
                for h in range(16):
                    MM(psY[:, h * T:(h + 1) * T], XDT[:, (h // 2) * 128:(h // 2 + 1) * 128], M16[:, h, :], sg=(h == 15))
                EB6 = EB16.rearrange("p (j q i h t) -> p j q i h t", j=2, q=2, i=2, h=2)
                Y6 = psY[:, 0:HT].rearrange("p (j q i h t) -> p j q i h t", j=2, q=2, i=2, h=2)
                for q in range(2):
                    O4 = psOp[q][:, 0:4 * T].rearrange("p (j i t) -> p j i t", j=2, i=2)
                    for h2 in range(2):
                        hv = slice(h2 * 64, h2 * 64 + 64)
                        for j in range(2):
                            tYv = tY[hv, 0:2 * T].rearrange("p (i t) -> p i t", i=2)
                            V('tensor_tensor', out=tYv, in0=O4[hv, j, :, :], in1=EB6[hv, j, q, :, h2, :], op=ALU.mult)
                            dstv = PG[4 + j][hv, :].rearrange("p (k n) -> p k n", k=4)[:, 2 * q:2 * q + 2, c0:c0 + T]
                            V('tensor_tensor', out=dstv, in0=tYv, in1=Y6[hv, j, q, :, h2, :], op=ALU.add)
                for j in range(2):
                    tq = t1.next()
                    tqv = tq[:, 0:4 * T].rearrange("p (k t) -> p k t", k=4)
                    xcv = PG[j][:, :].rearrange("p (k n) -> p k n", k=4)[:, :, c0:c0 + T]
                    ysv = PG[4 + j][:, :].rearrange("p (k n) -> p k n", k=4)[:, :, c0:c0 + T]
                    V('tensor_tensor', out=tqv, in0=xcv, in1=par[:, P_SD + 4 * j:P_SD + 4 * j + 4].unsqueeze(2).broadcast_to([128, 4, T]),
                      op=ALU.mult)
                    V('tensor_tensor', out=ysv, in0=ysv, in1=tqv, op=ALU.add)
                for gq in range(4):
                    gp = slice((gq % 2) * 64, (gq % 2) * 64 + 64)
                    psS = psr.next()
                    MM(psS[:, 0:256], Btok[:, (gq // 2) * 128:(gq // 2 + 1) * 128], XDTW[:, gq * 256:(gq + 1) * 256])
                    STv = ST_ssd[ls][gp, gq // 2, :]
                    V('tensor_tensor', out=STv.rearrange("n (h p) -> n h p", p=64),
                      in0=STv.rearrange("n (h p) -> n h p", p=64),
                      in1=ETb[gp, 4 * gq:4 * gq + 4].unsqueeze(2).broadcast_to([64, 4, 64]), op=ALU.mult)
                    V('tensor_tensor', out=STv, in0=STv, in1=psS[gp, 0:256], op=ALU.add)
            else:
                for gq in range(4):
                    D4 = pg6[:T, 0:4 * T].rearrange("s (h t) -> s h t", h=4)
                    E4 = pg6[:T, 512:512 + 4 * T].rearrange("s (h t) -> s h t", h=4)
                    EB4 = pg6[:, 1024:1024 + 4 * T].rearrange("s (h t) -> s h t", h=4)
                    M4 = pg6[:T, 1536:1536 + 2 * T].bitcast(BF16).rearrange("s (h t) -> s h t", h=4)
                    V('tensor_tensor', out=D4, in0=dta[:, 4 * gq:4 * gq + 4].unsqueeze(2).broadcast_to([T, 4, T]),
                      in1=mle[:T, :T].unsqueeze(1).broadcast_to([T, 4, T]), op=ALU.mult)
                    psAB = psr.next()
                    MM(psAB[:, 0:4 * T], ones[:T, :], pg6[:T, 0:4 * T])
                    A('activation', out=EB4, in_=psAB[:, 0:4 * T].rearrange("s (h t) -> s h t", h=4), func=AF.Exp)
                    V('tensor_tensor', out=E4, in0=psAB[:T, 0:4 * T].rearrange("s (h t) -> s h t", h=4),
                      in1=acum[:, 4 * gq:4 * gq + 4].unsqueeze(2).broadcast_to([T, 4, T]), op=ALU.subtract)
                    V('tensor_tensor', out=E4, in0=E4, in1=negm[:T, :T].unsqueeze(1).broadcast_to([T, 4, T]), op=ALU.add)
                    A('activation', out=E4, in_=E4, func=AF.Exp)
                    gp = slice((gq % 2) * 64, (gq % 2) * 64 + 64)
                    Bfm = XCb[gq // 2][gp, c0:c0 + T]
                    Cfm = XCb[2 + gq // 2][gp, c0:c0 + T]
                    psG = psr.next()
                    MM(psG[:T, :T], Bfm, Cfm)
                    V('tensor_tensor', out=M4, in0=E4, in1=psG[:T, :T].unsqueeze(1).broadcast_to([T, 4, T]), op=ALU.mult)
                    for hp in range(2):
                        k = 2 * gq + hp
                        psO = psr.next()
                        MM(psO[:, :T], ST_ssdb[gp, gq // 2, hp * 128:(hp + 1) * 128], Cfm)
                        for h2 in range(2):
                            hl = 2 * hp + h2
                            psY = psr.next()
                            MM(psY[:, :T], XDT[:, k * 128:(k + 1) * 128], M4[:, hl, :])
                            hv = slice(h2 * 64, h2 * 64 + 64)
                            V('tensor_tensor', out=tY[hv, :T], in0=psO[hv, :T], in1=EB4[hv, hl, :], op=ALU.mult)
                            V('tensor_tensor', out=yss[k][hv, c0:c0 + T], in0=tY[hv, :T], in1=psY[hv, :T], op=ALU.add)
                        V('scalar_tensor_tensor', out=yss[k][:, c0:c0 + T], in0=XC[k][:, c0:c0 + T],
                          scalar=par[:, P_SD + k:P_SD + k + 1], in1=yss[k][:, c0:c0 + T], op0=ALU.mult, op1=ALU.add)
                    psS = psr.next()
                    MM(psS[:, 0:256], Btok[:, (gq // 2) * 128:(gq // 2 + 1) * 128], XDTW[:, gq * 256:(gq + 1) * 256])
                    STv = ST_ssd[ls][gp, gq // 2, :]
                    V('tensor_tensor', out=STv.rearrange("n (h p) -> n h p", p=64),
                      in0=STv.rearrange("n (h p) -> n h p", p=64),
                      in1=ETb[gp, 4 * gq:4 * gq + 4].unsqueeze(2).broadcast_to([64, 4, 64]), op=ALU.mult)
                    V('tensor_tensor', out=STv, in0=STv, in1=psS[gp, 0:256], op=ALU.add)
            if kind == 's':
                ssd_state_store(ls, dout['s_ssd'][l, ci])
        if kind == 'p' and last:
            ssd_state_store(l, dout['p_ssd'][l])
        for k in range(8):
            V('tensor_tensor', out=yss[k][:, :ntk], in0=yss[k][:, :ntk], in1=sz[k][:, :ntk], op=ALU.mult)
        sumsq_rstd([yss[k][:, :ntk] for k in range(8)], ntk, 1024, EPS)
        for k in range(8):
            V('scalar_tensor_tensor', out=sz[k][:, :ntk], in0=yss[k][:, :ntk], scalar=par[:, P_SNW + k:P_SNW + k + 1],
              in1=rstd[:, :ntk], op0=ALU.mult, op1=ALU.mult)
        branch_out(l, 0, lambda k: sz[k][:, :ntk], 8, ntk, 'w_br_ssd')

    ST_hg = [S.sb([128, 4, 128], F32, 'sthg%d' % l) for l in range(L)]
    LB = S.sb([128, L, 4], F32, 'lb')
    lbtmp = S.sb([128, L + 2, 4], F32, 'lbtmp')
    for l in range(L):
        V('memset', ST_hg[l][:], 0.0)
        colload(lbtmp[:, l, :], din['hg_lb_raw'][l])
    A('activation', out=lbtmp[:, 0:L, :], in_=lbtmp[:, 0:L, :], func=AF.Exp)
    V('tensor_copy', out=lbtmp[:, L, :], in_=lbtmp[:, 0, :])
    for l in range(1, L):
        V('tensor_tensor', out=lbtmp[:, L, :], in0=lbtmp[:, L, :], in1=lbtmp[:, l, :], op=ALU.add)
    V('reciprocal', lbtmp[:, L + 1, :], lbtmp[:, L, :])
    V('memset', LB[:, 0, :], 0.0)
    for l in range(1, L):
        V('tensor_tensor', out=lbtmp[:, l, :], in0=lbtmp[:, l, :], in1=lbtmp[:, L + 1, :], op=ALU.mult)
        V('tensor_tensor', out=LB[:, l, :], in0=LB[:, l - 1, :], in1=lbtmp[:, l, :], op=ALU.add)
    OML = S.sb([128, L, 4], F32, 'oml')
    V('tensor_scalar', out=OML[:], in0=LB[:], scalar1=-1.0, scalar2=1.0, op0=ALU.mult, op1=ALU.add)
    P_HNW = 304

    def hg_phase(l, kind, ntk, chunks, last):
        def pgv(i):
            return PG[i][:, :].rearrange("p (h n) -> p h n", h=4)
        QT, KT, BB, VV, OO, GS, EBt = (pgv(i) for i in range(7))
        pg7 = PG[7]
        ybv = PG[8][:, :].bitcast(BF16)
        ybf = [ybv[:, h * 512:(h + 1) * 512] for h in range(4)]
        rmask = cst[:, C_R64:C_R64 + 512] if kind == 'p' else cst[:, C_R8:C_R8 + 128]
        colload(par[:, P_HNW:P_HNW + 1], din['hg_norm_w'][l])
        base = 3088
        wv = loadw(din['w_in'][l], base, 512)
        for h in range(4):
            ps = dense_fm(wv, h, lambda k: xn[:, k, :ntk], KC, ntk)
            tq = t1.next()
            A('activation', out=tq[:, :ntk], in_=ps[:, :ntk], func=AF.Sigmoid)
            V('tensor_tensor', out=QT[:, h, :ntk], in0=tq[:, :ntk], in1=ps[:, :ntk], op=ALU.mult)
        wv = loadw(din['w_in'][l], base + 512, 512)
        for h in range(4):
            ps = dense_fm(wv, h, lambda k: xn[:, k, :ntk], KC, ntk)
            tq = t1.next()
            A('activation', out=tq[:, :ntk], in_=ps[:, :ntk], func=AF.Sigmoid)
            V('tensor_scalar', out=tq[:, :ntk], in0=tq[:, :ntk], scalar1=OML[:, l, h:h + 1], scalar2=LB[:, l, h:h + 1],
              op0=ALU.mult, op1=ALU.add)
            V('tensor_scalar', out=KT[:, h, :ntk], in0=tq[:, :ntk], scalar1=-1.0, scalar2=1.0, op0=ALU.mult, op1=ALU.add)
            A('activation', out=tq[:, :ntk], in_=tq[:, :ntk], func=AF.Ln)
            V('tensor_tensor_scan', out=BB[:, h, :ntk], data0=rmask[:, :ntk], data1=tq[:, :ntk], initial=0.0,
              op0=ALU.mult, op1=ALU.add)
            A('activation', out=EBt[:, h, :ntk], in_=BB[:, h, :ntk], func=AF.Exp)
            V('tensor_tensor', out=QT[:, h, :ntk], in0=QT[:, h, :ntk], in1=EBt[:, h, :ntk], op=ALU.mult)
            tq2 = t1.next()
            V('tensor_scalar', out=tq2[:, :ntk], in0=BB[:, h, :ntk], scalar1=-1.0, scalar2=80.0, op0=ALU.mult, op1=ALU.min)
            A('activation', out=tq2[:, :ntk], in_=tq2[:, :ntk], func=AF.Exp)
            V('tensor_tensor', out=KT[:, h, :ntk], in0=KT[:, h, :ntk], in1=tq2[:, :ntk], op=ALU.mult)
        wv = loadw(din['w_in'][l], base + 1024, 512)
        for h in range(4):
            ps = dense_fm(wv, h, lambda k: xn[:, k, :ntk], KC, ntk)
            A('activation', out=VV[:, h, :ntk], in_=ps[:, :ntk], func=AF.Copy)
        wv = loadw(din['w_in'][l], base + 1536, 512)
        for h in range(4):
            ps = dense_fm(wv, h, lambda k: xn[:, k, :ntk], KC, ntk)
            A('activation', out=GS[:, h, :ntk], in_=ps[:, :ntk], func=AF.Sigmoid)
        for ci, (c0, T) in enumerate(chunks):
            ls = l
            if kind == 's':
                ls = ci % L
                if ci == 0:
                    S.dma('sp', ST_hg[0][:, :, :], din['state_hgrn'][l, 0].rearrange("h k v -> k h v"))
                if ci + 1 < len(chunks):
                    S.dma('sp', ST_hg[(ci + 1) % L][:, :, :], din['state_hgrn'][l, ci + 1].rearrange("h k v -> k h v"))
            SC = pg7[:T, 0:4 * T].rearrange("s (h t) -> s h t", h=4)
            Vtok = pg7[:T, 256:768]
            Ktok = pg7[:T, 768:1280]
            psS = psr.next()
            for h in range(4):
                MM(psS[:T, h * T:(h + 1) * T], KT[:, h, c0:c0 + T], QT[:, h, c0:c0 + T], sg=(h == 3))
            V('tensor_tensor', out=SC, in0=psS[:T, 0:4 * T].rearrange("s (h t) -> s h t", h=4),
              in1=mle[:T, :T].unsqueeze(1).broadcast_to([T, 4, T]), op=ALU.mult)
            psV = psr.next()
            for h in range(4):
                S.I('pe', 'transpose', psV[:T, h * 128:(h + 1) * 128], VV[:, h, c0:c0 + T], ident, sig=(h == 3))
            A('activation', out=Vtok, in_=psV[:T, :512], func=AF.Copy)
            psK = psr.next()
            for h in range(4):
                S.I('pe', 'transpose', psK[:T, h * 128:(h + 1) * 128], KT[:, h, c0:c0 + T], ident, sig=(h == 3))
            A('activation', out=Ktok, in_=psK[:T, :512], func=AF.Copy)
            for h in range(4):
                psO = psr.next()
                MM(psO[:, :T], Vtok[:, h * 128:(h + 1) * 128], SC[:, h, :], True, False)
                MM(psO[:, :T], ST_hg[ls][:, h, :], QT[:, h, c0:c0 + T], False, True)
                A('activation', out=OO[:, h, c0:c0 + T], in_=psO[:, :T], func=AF.Copy)
                psU = psr.next()
                MM(psU[:, :128], Ktok[:, h * 128:(h + 1) * 128], Vtok[:, h * 128:(h + 1) * 128])
                V('tensor_tensor', out=ST_hg[ls][:, h, :], in0=ST_hg[ls][:, h, :], in1=psU[:, :128], op=ALU.add)
                V('tensor_scalar', out=ST_hg[ls][:, h, :], in0=ST_hg[ls][:, h, :],
                  scalar1=EBt[:, h, c0 + T - 1:c0 + T], scalar2=None, op0=ALU.mult)
            if kind == 's':
                S.dma('sp', dout['s_hgrn'][l, ci].rearrange("h k v -> k h v"), ST_hg[ls][:, :, :])
        if kind == 'p' and last:
            S.dma('sp', dout['p_hgrn'][l].rearrange("h k v -> k h v"), ST_hg[l][:, :, :])
        for h in range(4):
            sumsq_rstd([OO[:, h, :ntk]], ntk, 128, EPS)
            tq = t1.next()
            V('scalar_tensor_tensor', out=tq[:, :ntk], in0=OO[:, h, :ntk], scalar=par[:, P_HNW:P_HNW + 1],
              in1=rstd[:, :ntk], op0=ALU.mult, op1=ALU.mult)
            V('tensor_tensor', out=ybf[h][:, :ntk], in0=tq[:, :ntk], in1=GS[:, h, :ntk], op=ALU.mult)
        branch_out(l, 2, lambda k: ybf[k][:, :ntk], 4, ntk, 'w_br_hg')

    import math
    PI = math.pi
    hS = [S.sb([128, 2, 16], F32, 'hs5_%d' % l) for l in range(L)]
    for l in range(L):
        V('memset', hS[l][:], 0.0)
    hs_s = S.sb([128, 2, 16, NSEQ], F32, 'hs5s')
    s5p = S.sb([128, 16, 16], F32, 's5p')
    ubuf = S.sb([128, 4, NTKMAX], F32, 'ubuf')
    P_S5D, P_GLB = 308, 312

    def s5_phase(l, kind, ntk, chunks, last):
        TC = 64
        pos = cst[:, C_POS:C_POS + 64]
        TF_re = PG[0][:, 0:1024].rearrange("p (c t) -> p c t", c=16)
        TF_im = PG[0][:, 1024:2048].rearrange("p (c t) -> p c t", c=16)
        TI_re = PG[1][:, 0:1024].rearrange("p (c t) -> p c t", c=16)
        TI_im = PG[1][:, 1024:2048].rearrange("p (c t) -> p c t", c=16)
        X1 = PG[2][:, 0:1024].rearrange("p (c t) -> p c t", c=16)
        X2 = PG[2][:, 1024:2048].rearrange("p (c t) -> p c t", c=16)
        a_re, a_im, dtc, da_re, da_im, den, q_re, q_im, tm1, tm2, tm3 = (s5p[:, i, :] for i in range(11))
        S.dma('sp', a_re, din['s5_a_re'][l].rearrange("(c g) n -> (g n) c", g=2))
        S.dma('sp', a_im, din['s5_a_im'][l].rearrange("(c g) n -> (g n) c", g=2))
        for g2 in range(2):
            S.dma('sp', s5p[g2 * 64:(g2 + 1) * 64, 2, :],
                  din['s5_log_dt'][l].rearrange("(c g) -> g c", g=2)[g2].partition_broadcast(64))
        colload(par[:, P_S5D:P_S5D + 4], din['s5_d'][l])
        colload(par[:, P_GLB:P_GLB + 4], din['s5_glu_b'][l])
        A('activation', out=dtc, in_=dtc, func=AF.Exp)
        V('tensor_tensor', out=da_re, in0=dtc, in1=a_re, op=ALU.mult)
        V('tensor_tensor', out=da_im, in0=dtc, in1=a_im, op=ALU.mult)
        bc = lambda v: v.unsqueeze(2).broadcast_to([128, 16, 64])
        posb = pos.unsqueeze(1).broadcast_to([128, 16, 64])
        V('tensor_tensor', out=X1, in0=bc(da_re), in1=posb, op=ALU.mult)
        A('activation', out=TF_re, in_=X1, func=AF.Exp)
        A('activation', out=TI_re, in_=X1, func=AF.Exp, scale=-1.0)
        V('tensor_tensor', out=X2, in0=bc(da_im), in1=posb, op=ALU.mult)
        I32 = mybir.dt.int32
        XI = PG[8][:, 0:1024].rearrange("p (c t) -> p c t", c=16).bitcast(I32)
        XF = PG[8][:, 1024:2048].rearrange("p (c t) -> p c t", c=16)

        def rred(dst, src, shift):
            V('tensor_scalar', out=dst, in0=src, scalar1=shift, scalar2=None, op0=ALU.add)
            V('tensor_scalar', out=XF, in0=dst, scalar1=1.0 / (2 * PI), scalar2=None, op0=ALU.mult)
            V('tensor_copy', out=XI, in_=XF)
            V('tensor_copy', out=XF, in_=XI)
            V('scalar_tensor_tensor', out=dst, in0=XF, scalar=-2 * PI, in1=dst, op0=ALU.mult, op1=ALU.add)
            V('tensor_scalar', out=XF, in0=dst, scalar1=PI, scalar2=2 * PI, op0=ALU.is_gt, op1=ALU.mult)
            V('tensor_tensor', out=dst, in0=dst, in1=XF, op=ALU.subtract)
            V('tensor_scalar', out=XF, in0=dst, scalar1=-PI, scalar2=2 * PI, op0=ALU.is_lt, op1=ALU.mult)
            V('tensor_tensor', out=dst, in0=dst, in1=XF, op=ALU.add)
        rred(X1, X2, 0.0)
        A('activation', out=X1, in_=X1, func=AF.Sin)
        rred(X2, X2, 0.5 * PI)
        A('activation', out=X2, in_=X2, func=AF.Sin)
        V('tensor_tensor', out=TF_im, in0=TF_re, in1=X1, op=ALU.mult)
        V('tensor_tensor', out=TF_re, in0=TF_re, in1=X2, op=ALU.mult)
        V('tensor_tensor', out=TI_im, in0=TI_re, in1=X1, op=ALU.mult)
        V('tensor_scalar', out=TI_im, in0=TI_im, scalar1=-1.0, scalar2=None, op0=ALU.mult)
        V('tensor_tensor', out=TI_re, in0=TI_re, in1=X2, op=ALU.mult)
        ab_re, ab_im = TF_re[:, :, 0], TF_im[:, :, 0]
        V('tensor_tensor', out=den, in0=a_re, in1=a_re, op=ALU.mult)
        V('tensor_tensor', out=tm1, in0=a_im, in1=a_im, op=ALU.mult)
        V('tensor_tensor', out=den, in0=den, in1=tm1, op=ALU.add)
        V('reciprocal', den, den)
        V('tensor_scalar', out=tm1, in0=ab_re, scalar1=-1.0, scalar2=None, op0=ALU.add)
        V('tensor_tensor', out=tm2, in0=tm1, in1=a_re, op=ALU.mult)
        V('tensor_tensor', out=tm3, in0=ab_im, in1=a_im, op=ALU.mult)
        V('tensor_tensor', out=tm2, in0=tm2, in1=tm3, op=ALU.add)
        V('tensor_tensor', out=q_re, in0=tm2, in1=den, op=ALU.mult)
        V('tensor_tensor', out=tm2, in0=ab_im, in1=a_re, op=ALU.mult)
        V('tensor_tensor', out=tm3, in0=tm1, in1=a_im, op=ALU.mult)
        V('tensor_tensor', out=tm2, in0=tm2, in1=tm3, op=ALU.subtract)
        V('tensor_tensor', out=q_im, in0=tm2, in1=den, op=ALU.mult)
        pg7 = PG[7]
        b_re = pg7[:, 0:256].rearrange("p (c j) -> p c j", c=16)
        b_im = pg7[:, 256:512].rearrange("p (c j) -> p c j", c=16)
        c_re = pg7[:, 512:768].rearrange("p (m n) -> p m n", m=4)
        c_im = pg7[:, 768:1024].rearrange("p (m n) -> p m n", m=4)
        bb_re = pg7[:, 1024:1280].rearrange("p (c j) -> p c j", c=16)
        bb_im = pg7[:, 1280:1536].rearrange("p (c j) -> p c j", c=16)
        tb = pg7[:, 1536:1792].rearrange("p (c j) -> p c j", c=16)
        S.dma('sp', b_re, din['s5_b_re'][l].rearrange("(c g) n j -> (g n) c j", g=2))
        S.dma('sp', b_im, din['s5_b_im'][l].rearrange("(c g) n j -> (g n) c j", g=2))
        S.dma('sp', c_re, din['s5_c_re'][l].rearrange("(m g) j n -> (g j) m n", g=8))
        S.dma('sp', c_im, din['s5_c_im'][l].rearrange("(m g) j n -> (g j) m n", g=8))
        qb = lambda v: v.unsqueeze(2).broadcast_to([128, 16, 16])
        V('tensor_tensor', out=bb_re, in0=b_re, in1=qb(q_re), op=ALU.mult)
        V('tensor_tensor', out=tb, in0=b_im, in1=qb(q_im), op=ALU.mult)
        V('tensor_tensor', out=bb_re, in0=bb_re, in1=tb, op=ALU.subtract)
        V('tensor_tensor', out=bb_im, in0=b_im, in1=qb(q_re), op=ALU.mult)
        V('tensor_tensor', out=tb, in0=b_re, in1=qb(q_im), op=ALU.mult)
        V('tensor_tensor', out=bb_im, in0=bb_im, in1=tb, op=ALU.add)
        V('tensor_scalar', out=c_im, in0=c_im, scalar1=-1.0, scalar2=None, op0=ALU.mult)
        mskB = cst[:, C_MSKB:C_MSKB + 32]
        mskC = cst[:, C_MSKC:C_MSKC + 8]
        BL = {'bre': PG[3], 'bim': PG[4], 'cre': PG[5], 'cim': PG[6]}
        big = Ring([S5BIG0, S5BIG1])
        for nm, srcv in (('bre', bb_re), ('bim', bb_im)):
            for c0_ in range(0, 16, 4):
                ps = psr.next()
                for c in range(c0_, c0_ + 4):
                    yb = big.next()
                    V('tensor_tensor', out=yb[:, :].rearrange("p (g j) -> p g j", g=8),
                      in0=srcv[:, c, :].unsqueeze(1).broadcast_to([128, 8, 16]),
                      in1=mskB[:, (c % 4) * 8:(c % 4) * 8 + 8].unsqueeze(2).broadcast_to([128, 8, 16]), op=ALU.mult)
                    S.I('pe', 'transpose', ps[:, (c - c0_) * 128:(c - c0_ + 1) * 128], yb[:, :], ident, sig=True)
                A('activation', out=BL[nm][:, c0_ * 128:(c0_ + 4) * 128], in_=ps[:, :], func=AF.Copy)
        for nm, srcv in (('cre', c_re), ('cim', c_im)):
            for m in range(4):
                ps = psr.next()
                for i in range(4):
                    yb = big.next()
                    V('tensor_tensor', out=yb[:, :].rearrange("p (g n) -> p g n", g=2),
                      in0=srcv[:, m, :].unsqueeze(1).broadcast_to([128, 2, 64]),
                      in1=mskC[:, i * 2:i * 2 + 2].unsqueeze(2).broadcast_to([128, 2, 64]), op=ALU.mult)
                    S.I('pe', 'transpose', ps[:, i * 128:(i + 1) * 128], yb[:, :], ident, sig=True)
                A('activation', out=BL[nm][:, m * 512:(m + 1) * 512], in_=ps[:, :], func=AF.Copy)
        wv = loadw(din['w_in'][l], 2576, 512)
        for m in range(4):
            ps = dense_fm(wv, m, lambda k: xn[:, k, :ntk], KC, ntk)
            A('activation', out=ubuf[:, m, :ntk], in_=ps[:, :ntk], func=AF.Copy)
        if kind == 's':
            tok_to_fm_load(din['state_s5_re'][l].rearrange("b g n -> b (g n)"), 16, NSEQ, lambda c: hs_s[:, 0, c, :])
            tok_to_fm_load(din['state_s5_im'][l].rearrange("b g n -> b (g n)"), 16, NSEQ, lambda c: hs_s[:, 1, c, :])
        if kind == 's':
            NB = min(8, NSEQ)
            chunks = [(g * NB * TS, NB * TS) for g in range(NSEQ // NB)]
            Tt = TS
        else:
            NB = 1
            Tt = None
        for ci, (c0, T) in enumerate(chunks):
            n = 16 * T
            tt = Tt or T
            v4 = lambda ap: ap.rearrange("p c (b t) -> p c b t", t=tt)
            tA = PG[2][:, 0:n].rearrange("p (c t) -> p c t", c=16)
            tB = PG[2][:, 1024:1024 + n].rearrange("p (c t) -> p c t", c=16)
            W_re = PG[8][:, 0:n].rearrange("p (c t) -> p c t", c=16)
            W_im = PG[8][:, 1024:1024 + n].rearrange("p (c t) -> p c t", c=16)
            tb4 = lambda tab: tab[:, :, :tt].unsqueeze(2).broadcast_to([128, 16, T // tt, tt])
            tfr, tfi, tir, tii = tb4(TF_re), tb4(TF_im), tb4(TI_re), tb4(TI_im)
            if kind == 'p':
                hin_re, hin_im = hS[l][:, 0, :].unsqueeze(2), hS[l][:, 1, :].unsqueeze(2)
            else:
                hin_re, hin_im = hs_s[:, 0, :, ci * NB:(ci + 1) * NB], hs_s[:, 1, :, ci * NB:(ci + 1) * NB]
            nb = (n + 511) // 512
            pre = [psr.next() for _ in range(nb)]
            pim = [psr.next() for _ in range(nb)]
            for c in range(16):
                o = c * T
                MM(pre[o // 512][:, o % 512:o % 512 + T], BL['bre'][:, c * 128:(c + 1) * 128], ubuf[:, c // 4, c0:c0 + T],
                   sg=True)
                MM(pim[o // 512][:, o % 512:o % 512 + T], BL['bim'][:, c * 128:(c + 1) * 128], ubuf[:, c // 4, c0:c0 + T],
                   sg=True)
            cpb = 512 // T if n > 512 else 16
            for bb_ in range(nb):
                cs = slice(bb_ * cpb, (bb_ + 1) * cpb)
                w_ = min(512, n)
                pv = lambda p: p[:, 0:w_].rearrange("p (c b t) -> p c b t", b=T // tt, t=tt)
                V('tensor_tensor', out=v4(tA[:, cs, :]), in0=pv(pre[bb_]), in1=tir[:, cs], op=ALU.mult)
                V('tensor_tensor', out=v4(tB[:, cs, :]), in0=pv(pim[bb_]), in1=tii[:, cs], op=ALU.mult)
                V('tensor_tensor', out=W_re[:, cs, :], in0=tA[:, cs, :], in1=tB[:, cs, :], op=ALU.subtract)
                V('tensor_tensor', out=v4(tA[:, cs, :]), in0=pv(pim[bb_]), in1=tir[:, cs], op=ALU.mult)
                V('tensor_tensor', out=v4(tB[:, cs, :]), in0=pv(pre[bb_]), in1=tii[:, cs], op=ALU.mult)
                V('tensor_tensor', out=W_im[:, cs, :], in0=tA[:, cs, :], in1=tB[:, cs, :], op=ALU.add)
            rm = cst[:, C_R64:C_R64 + 512] if kind == 'p' else cst[:, C_R8L:C_R8L + 512]
            for (Wv, off) in ((PG[8], 0), (PG[8], 1024)):
                for b in range(nb):
                    w_ = min(512, n)
                    seg = Wv[:, off + b * 512:off + b * 512 + w_]
                    dst = PG[2][:, off + b * 512:off + b * 512 + w_]
                    V('tensor_tensor_scan', out=dst, data0=rm[:, :w_], data1=seg, initial=0.0, op0=ALU.mult, op1=ALU.add)
            G_re, G_im = tA, tB
            hb4 = lambda hh: hh.unsqueeze(3).broadcast_to([128, 16, T // tt, tt])
            V('tensor_tensor', out=v4(G_re), in0=v4(G_re), in1=hb4(hin_re), op=ALU.add)
            V('tensor_tensor', out=v4(G_im), in0=v4(G_im), in1=hb4(hin_im), op=ALU.add)
            H_re, H_im = W_re, W_im
            V('tensor_tensor', out=v4(H_re), in0=v4(G_re), in1=tfr, op=ALU.mult)
            V('tensor_tensor', out=v4(H_im), in0=v4(G_im), in1=tfi, op=ALU.mult)
            V('tensor_tensor', out=H_re, in0=H_re, in1=H_im, op=ALU.subtract)
            V('tensor_tensor', out=v4(H_im), in0=v4(G_im), in1=tfr, op=ALU.mult)
            V('tensor_tensor', out=v4(G_re), in0=v4(G_re), in1=tfi, op=ALU.mult)
            V('tensor_tensor', out=H_im, in0=H_im, in1=G_re, op=ALU.add)
            V('tensor_copy', out=hin_re, in_=v4(H_re)[:, :, :, tt - 1])
            V('tensor_copy', out=hin_im, in_=v4(H_im)[:, :, :, tt - 1])
            for m in range(4):
                psY = psr.next()
                for i in range(4):
                    c = 4 * m + i
                    MM(psY[:, :T], BL['cre'][:, c * 128:(c + 1) * 128], H_re[:, c, :], i == 0, False)
                    MM(psY[:, :T], BL['cim'][:, c * 128:(c + 1) * 128], H_im[:, c, :], False, i == 3)
                V('scalar_tensor_tensor', out=ubuf[:, m, c0:c0 + T], in0=ubuf[:, m, c0:c0 + T],
                  scalar=par[:, P_S5D + m:P_S5D + m + 1], in1=psY[:, :T], op0=ALU.mult, op1=ALU.add)
        if kind == 's':
            fm_to_tok_store(lambda c: hs_s[:, 0, c, :], 16, NSEQ, dout['s_s5_re'][l].rearrange("b g n -> b (g n)"))
            fm_to_tok_store(lambda c: hs_s[:, 1, c, :], 16, NSEQ, dout['s_s5_im'][l].rearrange("b g n -> b (g n)"))
        elif last:
            fm_to_tok_store(lambda c: hS[l][:, 0, c:c + 1], 16, 1, dout['p_s5_re'][l].rearrange("(o g) n -> o (g n)", o=1))
            fm_to_tok_store(lambda c: hS[l][:, 1, c:c + 1], 16, 1, dout['p_s5_im'][l].rearrange("(o g) n -> o (g n)", o=1))
        ygb = PG[2][:, :].bitcast(BF16)
        wg = loadw(din['s5_glu_w'][l], 0, 512)
        for m in range(4):
            u1 = t1.next()
            g = ubuf[:, m, :ntk]
            A('activation', out=u1[:, :ntk], in_=g, func=AF.Square)
            V('tensor_scalar', out=u1[:, :ntk], in0=u1[:, :ntk], scalar1=0.044715, scalar2=1.0, op0=ALU.mult, op1=ALU.add)
            V('tensor_tensor', out=u1[:, :ntk], in0=u1[:, :ntk], in1=g, op=ALU.mult)
            A('activation', out=u1[:, :ntk], in_=u1[:, :ntk], func=AF.Sigmoid, scale=1.5957691216)
            V('tensor_tensor', out=g, in0=u1[:, :ntk], in1=g, op=ALU.mult)
            V('tensor_copy', out=ygb[:, m * 512:m * 512 + ntk], in_=g)
        for m in range(4):
            ps = dense_fm(wg, m, lambda k: ygb[:, k * 512:k * 512 + ntk], 4, ntk)
            u1 = t1.next()
            A('activation', out=u1[:, :ntk], in_=ps[:, :ntk], func=AF.Sigmoid, bias=par[:, P_GLB + m:P_GLB + m + 1])
            V('tensor_tensor', out=ygb[:, 2048 + m * 512:2048 + m * 512 + ntk], in0=u1[:, :ntk], in1=ubuf[:, m, :ntk],
              op=ALU.mult)
        branch_out(l, 1, lambda k: ygb[:, 2048 + k * 512:2048 + k * 512 + ntk], 4, ntk, 'w_br_s5')

    ST_rw = [S.sb([128, 4, 64], F32, 'strw%d' % l) for l in range(L)]
    sh_rw = [S.sb([128, 14, 1], F32, 'shrw%d' % l) for l in range(L)]
    for l in range(L):
        V('memset', ST_rw[l][:], 0.0)
        V('memset', sh_rw[l][:], 0.0)
    rwp = S.sb([128, 1024], F32, 'rwp')
    ST_rwb = S.sb([128, 4, 64], BF16, 'strwb')
    P_MU, P_W0, P_A0, P_KK, P_KA, P_RK, P_LNW, P_LNB = 320, 334, 338, 342, 346, 350, 354, 358

    def rw_state_load(l, dram3):
        lt = t1.next()
        S.dma('sp', lt[:64, 0:512].rearrange("v (c hp k) -> v c hp k", c=4, hp=2), dram3.rearrange("(c hp) v k -> v c hp k", hp=2))
        ps = psr.next()
        for c4 in range(4):
            S.I('pe', 'transpose', ps[:, c4 * 64:(c4 + 1) * 64], lt[:64, c4 * 128:(c4 + 1) * 128], ident[:64, :64],
                sig=(c4 == 3))
        A('activation', out=ST_rw[l][:, :, :], in_=ps[:, 0:256].rearrange("p (c v) -> p c v", c=4), func=AF.Copy)

    def rw_state_store(l, dram3):
        ps = psr.next()
        for c4 in range(4):
            S.I('pe', 'transpose', ps[:64, c4 * 128:(c4 + 1) * 128], ST_rw[l][:, c4, :], ident, sig=(c4 == 3))
        so = t1.next()
        A('activation', out=so[:64, 0:512], in_=ps[:64, 0:512], func=AF.Copy)
        S.dma('sp', dram3.rearrange("(c hp) v k -> v c hp k", hp=2), so[:64, 0:512].rearrange("v (c hp k) -> v c hp k", c=4, hp=2))

    def rw_phase(l, kind, ntk, chunks, last):
        base = 5136
        RWSTOP = cfg.get('rwstop', 99)
        colload(par[:, P_MU:P_MU + 14], din['rw_mu'][l])
        colload(par[:, P_W0:P_W0 + 4], din['rw_w0'][l])
        colload(par[:, P_A0:P_A0 + 4], din['rw_a0'][l])
        for nm, pc in (('rw_k_k', P_KK), ('rw_k_a', P_KA), ('rw_r_k', P_RK), ('rw_ln_w', P_LNW), ('rw_ln_b', P_LNB)):
            colload(par[:, pc:pc + 4], din[nm][l].rearrange("h v -> (h v)"))
        S.dma('sp', rwp[0:64, 0:512], din['rw_w_up'][l])
        S.dma('sp', rwp[64:128, 0:512], din['rw_a_up'][l])
        S.dma('sp', rwp[:, 512:1024], din['rw_g_up'][l])
        if kind == 's':
            tok_to_fm_load(din['state_rwkv_shift'][l], 14, NSEQ, lambda c: shs[:, c, 0:NSEQ])
        ybv = PG[8][:, :].bitcast(BF16)
        ybf = [ybv[:, 2048 + h * 512:2048 + (h + 1) * 512] for h in range(4)]
        nhalf = (ntk + 255) // 256
        for hf in range(nhalf):
            h0 = hf * 256
            nt = min(256, ntk - h0)
            hv = lambda pg, half: PG[pg][:, half * 1024:(half + 1) * 1024].rearrange("p (c n) -> p c n", c=4)
            R, K, Vv, AT = hv(0, 0), hv(0, 1), hv(1, 0), hv(1, 1)
            BT, KK, EP, G = hv(2, 0), hv(2, 1), hv(3, 0), hv(3, 1)
            BON, Yfm = hv(4, 0), hv(4, 1)
            LOGW, ASIG = AT, BT
            for u in range(4):
                ncol = 512 if u < 3 else 256
                wv = loadw(din['w_in'][l], base + u * 512, ncol)
                for cb in range(ncol // 128):
                    c = u * 4 + cb
                    ps = dense_fm(wv, cb, lambda k: xn[:, k, h0:h0 + nt], KC, nt)
                    sg = stg.next()
                    xm = t1.next()
                    if kind == 'p':
                        A('activation', out=sg[:, 1:1 + nt], in_=ps[:, :nt], func=AF.Copy)
                        V('tensor_copy', out=sg[:, 0:1], in_=sh_rw[l][:, c, :])
                        V('tensor_copy', out=sh_rw[l][:, c, :], in_=sg[:, nt:nt + 1])
                        prev, cur, xo = sg[:, 0:nt], sg[:, 1:1 + nt], xm[:, :nt]
                    else:
                        v3 = sg[:, 0:NSEQ * 9].rearrange("p (b t) -> p b t", t=9)
                        A('activation', out=v3[:, :, 1:9], in_=ps[:, :nt].rearrange("p (b t) -> p b t", t=TS), func=AF.Copy)
                        V('tensor_copy', out=v3[:, :, 0:1], in_=shs[:, c, 0:NSEQ].unsqueeze(2))
                        V('tensor_copy', out=shs[:, c, 0:NSEQ].unsqueeze(2), in_=v3[:, :, 8:9])
                        prev, cur = v3[:, :, 0:8], v3[:, :, 1:9]
                        xo = xm[:, :nt].rearrange("p (b t) -> p b t", t=TS)
                    dd = t1.next()
                    ddv = dd[:, :nt] if kind == 'p' else dd[:, :nt].rearrange("p (b t) -> p b t", t=TS)
                    V('tensor_tensor', out=ddv, in0=prev, in1=cur, op=ALU.subtract)
                    V('scalar_tensor_tensor', out=xo, in0=ddv, scalar=par[:, P_MU + c:P_MU + c + 1], in1=cur,
                      op0=ALU.mult, op1=ALU.add)
                    xs_ = xm[:, :nt]
                    if c < 4:
                        V('tensor_copy', out=R[:, c, :nt], in_=xs_)
                    elif c < 8:
                        V('tensor_copy', out=K[:, c - 4, :nt], in_=xs_)
                    elif c < 12:
                        V('tensor_copy', out=Vv[:, c - 8, :nt], in_=xs_)
                    elif c == 12:
                        lowr = stg.next()
                        A('activation', out=lowr[0:64, :nt], in_=xm[0:64, :nt], func=AF.Tanh)
                        V('tensor_copy', out=lowr[64:128, :nt], in_=xm[64:128, :nt])
                        for cb2 in range(4):
                            psw = psr.next()
                            MM(psw[:, :nt], rwp[0:64, cb2 * 128:(cb2 + 1) * 128], lowr[0:64, :nt])
                            A('activation', out=LOGW[:, cb2, :nt], in_=psw[:, :nt], func=AF.Sigmoid,
                              bias=par[:, P_W0 + cb2:P_W0 + cb2 + 1])
                            V('tensor_scalar', out=LOGW[:, cb2, :nt], in0=LOGW[:, cb2, :nt], scalar1=-0.6065306597126334,
                              scalar2=None, op0=ALU.mult)
                            psa = psr.next()
                            MM(psa[:, :nt], rwp[64:128, cb2 * 128:(cb2 + 1) * 128], lowr[64:128, :nt])
                            A('activation', out=ASIG[:, cb2, :nt], in_=psa[:, :nt], func=AF.Sigmoid,
                              bias=par[:, P_A0 + cb2:P_A0 + cb2 + 1])
                    else:
                        gsg = stg.next()
                        A('activation', out=gsg[:, :nt], in_=xs_, func=AF.Sigmoid)
                        for cb2 in range(4):
                            psg = psr.next()
                            MM(psg[:, :nt], rwp[:, 512 + cb2 * 128:512 + (cb2 + 1) * 128], gsg[:, :nt])
                            A('activation', out=G[:, cb2, :nt], in_=psg[:, :nt], func=AF.Copy)
            if RWSTOP <= 1:
                continue
            rmask = cst[:, C_R64:C_R64 + 512] if kind == 'p' else cst[:, C_R8:C_R8 + 128]
            for c4 in range(4):
                ta = t1.next()
                V('tensor_scalar', out=KK[:, c4, :nt], in0=K[:, c4, :nt], scalar1=par[:, P_KK + c4:P_KK + c4 + 1],
                  scalar2=None, op0=ALU.mult)
                A('activation', out=ta[:, :nt], in_=KK[:, c4, :nt], func=AF.Square)
                ps = psr.next()
                MM(ps[:, :nt], bd64, ta[:, :nt])
                V('tensor_scalar', out=ta[:, :nt], in0=ps[:, :nt], scalar1=1e-24, scalar2=None, op0=ALU.max)
                A('sqrt', ta[:, :nt], ta[:, :nt])
                V('reciprocal', ta[:, :nt], ta[:, :nt])
                V('tensor_tensor', out=KK[:, c4, :nt], in0=KK[:, c4, :nt], in1=ta[:, :nt], op=ALU.mult)
                V('tensor_scalar', out=ta[:, :nt], in0=ASIG[:, c4, :nt], scalar1=-1.0, scalar2=par[:, P_KA + c4:P_KA + c4 + 1],
                  op0=ALU.add, op1=ALU.mult)
                V('tensor_scalar', out=ta[:, :nt], in0=ta[:, :nt], scalar1=1.0, scalar2=None, op0=ALU.add)
                V('tensor_tensor', out=K[:, c4, :nt], in0=K[:, c4, :nt], in1=ta[:, :nt], op=ALU.mult)
                V('tensor_tensor', out=ta[:, :nt], in0=R[:, c4, :nt], in1=K[:, c4, :nt], op=ALU.mult)
                V('tensor_scalar', out=ta[:, :nt], in0=ta[:, :nt], scalar1=par[:, P_RK + c4:P_RK + c4 + 1], scalar2=None,
                  op0=ALU.mult)
                ps = psr.next()
                MM(ps[:, :nt], bd64, ta[:, :nt])
                V('tensor_tensor', out=BON[:, c4, :nt], in0=ps[:, :nt], in1=Vv[:, c4, :nt], op=ALU.mult)
                V('tensor_tensor_scan', out=EP[:, c4, :nt], data0=rmask[:, :nt], data1=LOGW[:, c4, :nt], initial=0.0,
                  op0=ALU.mult, op1=ALU.add)
                em = t1.next()
                A('activation', out=em[:, :nt], in_=EP[:, c4, :nt], func=AF.Exp, scale=-1.0)
                A('activation', out=EP[:, c4, :nt], in_=EP[:, c4, :nt], func=AF.Exp)
                A('activation', out=ta[:, :nt], in_=LOGW[:, c4, :nt], func=AF.Exp, scale=-1.0)
                V('tensor_tensor', out=ta[:, :nt], in0=ta[:, :nt], in1=EP[:, c4, :nt], op=ALU.mult)
                V('scalar_tensor_tensor', out=AT[:, c4, :nt], in0=KK[:, c4, :nt], scalar=-1.0, in1=ta[:, :nt],
                  op0=ALU.mult, op1=ALU.mult)
                V('tensor_tensor', out=BT[:, c4, :nt], in0=ASIG[:, c4, :nt], in1=KK[:, c4, :nt], op=ALU.mult)
                V('tensor_tensor', out=BT[:, c4, :nt], in0=BT[:, c4, :nt], in1=em[:, :nt], op=ALU.mult)
                V('tensor_tensor', out=R[:, c4, :nt], in0=R[:, c4, :nt], in1=EP[:, c4, :nt], op=ALU.mult)
                V('tensor_tensor', out=K[:, c4, :nt], in0=K[:, c4, :nt], in1=em[:, :nt], op=ALU.mult)
            if RWSTOP <= 2:
                continue
            def bfv(pg, lo):
                return PG[pg][:, lo:lo + 512].bitcast(BF16).rearrange("p (c n) -> p c n", c=4)
            R_b, K_b = bfv(2, 1024), bfv(2, 1536)
            V('tensor_copy', out=R_b[:, :, :nt], in_=R[:, :, :nt])
            V('tensor_copy', out=K_b[:, :, :nt], in_=K[:, :, :nt])
            AT_b, BT_b = bfv(0, 0), bfv(0, 512)
            V('tensor_copy', out=AT_b[:, :, :nt], in_=AT[:, :, :nt])
            V('tensor_copy', out=BT_b[:, :, :nt], in_=BT[:, :, :nt])
            my = [(ci, c0 - h0, T) for ci, (c0, T) in enumerate(chunks) if h0 <= c0 < h0 + nt]
            for (ci, c0, T) in my:
                ls = l
                if kind == 's':
                    ls = ci % L
                    if ci == 0:
                        rw_state_load(0, din['state_rwkv'][l, 0])
                    if ci + 1 < len(chunks):
                        rw_state_load((ci + 1) % L, din['state_rwkv'][l, ci + 1])
                A('activation', out=ST_rwb[:, :, :], in_=ST_rw[ls][:, :, :], func=AF.Copy)
                W8 = 8 * T
                blk = lambda pg, i: PG[pg][:T, i * 512:i * 512 + W8 // 2].bitcast(BF16).rearrange("s (h t) -> s h t", h=8)
                Q, QT_, P_, AkM = blk(5, 0), blk(5, 1), blk(5, 2), blk(5, 3)
                RbM, RkM, Q2, QT2 = blk(6, 0), blk(6, 1), blk(6, 2), blk(6, 3)
                Vtok, UT, Bgt, Kgt = (PG[7][:T, i * 512:i * 512 + 256].bitcast(BF16) for i in range(4))
                RH, ytok = PG[8][:T, 0:256].bitcast(BF16), PG[8][:T, 512:1024]
                RHf = PG[8][:T, 256:512].bitcast(BF16)
                fs = lambda X, h: X[(h % 2) * 64:(h % 2) * 64 + 64, h // 2, c0:c0 + T]
                pss = [psr.next() for _ in range(5)]
                for h in range(8):
                    o = slice(h * T, (h + 1) * T)
                    MM(pss[0][:T, o], fs(BT_b, h), fs(AT_b, h), sg=(h == 7))
                    MM(pss[1][:T, o], fs(K_b, h), fs(AT_b, h), sg=(h == 7))
                    MM(pss[2][:T, o], fs(BT_b, h), fs(R_b, h), sg=(h == 7))
                    MM(pss[3][:T, o], fs(K_b, h), fs(R_b, h), sg=(h == 7))
                    MM(pss[4][:T, o], fs(AT_b, h), fs(BT_b, h), sg=(h == 7))
                pv = lambda p: p[:T, 0:W8].rearrange("s (h t) -> s h t", h=8)
                mb = lambda m: m[:T, :T].unsqueeze(1).broadcast_to([T, 8, T])
                V('tensor_tensor', out=Q, in0=pv(pss[0]), in1=mb(mlt), op=ALU.mult)
                V('tensor_tensor', out=AkM, in0=pv(pss[1]), in1=mb(mlt), op=ALU.mult)
                V('tensor_tensor', out=RbM, in0=pv(pss[2]), in1=mb(mle), op=ALU.mult)
                V('tensor_tensor', out=RkM, in0=pv(pss[3]), in1=mb(mle), op=ALU.mult)
                V('tensor_tensor', out=QT_, in0=pv(pss[4]), in1=mb(mgt), op=ALU.mult)
                V('tensor_tensor', out=P_, in0=Q, in1=mb(ident), op=ALU.add)
                if RWSTOP <= 3:
                    continue
                psv = psr.next()
                for c4 in range(4):
                    S.I('pe', 'transpose', psv[:T, c4 * 128:(c4 + 1) * 128], Vv[:, c4, c0:c0 + T], ident, sig=(c4 == 3))
                A('activation', out=Vtok, in_=psv[:T, 0:512], func=AF.Copy)
                for (srcX, dstX) in ((BT, Bgt), (K, Kgt)):
                    pst = psr.next()
                    for c4 in range(4):
                        tg = stg.next()
                        V('tensor_scalar', out=tg[:, :T], in0=srcX[:, c4, c0:c0 + T],
                          scalar1=EP[:, c4, c0 + T - 1:c0 + T], scalar2=None, op0=ALU.mult)
                        S.I('pe', 'transpose', pst[:T, c4 * 128:(c4 + 1) * 128], tg[:, :T], ident, sig=True)
                    A('activation', out=dstX, in_=pst[:T, 0:512], func=AF.Copy)
                nsteps = max(1, int(math.ceil(math.log2(T))))
                cq, cqt, nq, nqt = Q, QT_, Q2, QT2
                for j in range(nsteps - 1):
                    lastj = (j == nsteps - 2)
                    psQT = psr.next()
                    if not lastj:
                        psQ = psr.next()
                    for h in range(8):
                        MM(psQT[:T, h * T:(h + 1) * T], cq[:, h, :], cqt[:, h, :], sg=(h == 7))
                        if not lastj:
                            MM(psQ[:T, h * T:(h + 1) * T], cqt[:, h, :], cq[:, h, :], sg=(h == 7))
                    A('activation', out=nqt, in_=pv(psQT), func=AF.Copy)
                    if not lastj:
                        V('tensor_copy', out=nq, in_=pv(psQ))
                    cq, cqt, nq, nqt = nq, nqt, cq, cqt
                    psP = psr.next()
                    for h in range(8):
                        MM(psP[:T, h * T:(h + 1) * T], cqt[:, h, :], P_[:, h, :], sg=(h == 7))
                    V('tensor_tensor', out=P_, in0=P_, in1=pv(psP), op=ALU.add)
                if RWSTOP <= 4:
                    continue
                if RWSTOP <= 5:
                    continue
                psRe, psRo, psR2 = psr.next(), psr.next(), psr.next()
                for h in range(8):
                    o = slice(h * 64, (h + 1) * 64)
                    o2 = slice((h // 2) * 64, (h // 2 + 1) * 64)
                    MM((psRe, psRo)[h % 2][:T, o2], fs(AT_b, h), ST_rwb[(h % 2) * 64:(h % 2) * 64 + 64, h // 2, :], sg=(h >= 6))
                    MM(psR2[:T, o], AkM[:, h, :], Vtok[:, o], sg=(h == 7))
                RH4 = RH.rearrange("t (c hp v) -> t c hp v", c=4, hp=2)
                A('activation', out=RH4[:, :, 0, :], in_=psRe[:T, 0:256].rearrange("t (c v) -> t c v", c=4), func=AF.Copy)
                A('activation', out=RH4[:, :, 1, :], in_=psRo[:T, 0:256].rearrange("t (c v) -> t c v", c=4), func=AF.Copy)
                V('tensor_tensor', out=RH, in0=RH, in1=psR2[:T, 0:512], op=ALU.add)
                if RWSTOP <= 5.2:
                    continue
                psU = psr.next()
                for h in range(8):
                    o = slice(h * 64, (h + 1) * 64)
                    MM(psU[:T, o], P_[:, h, :], RH[:, o], sg=(h == 7))
                A('activation', out=UT, in_=psU[:T, 0:512], func=AF.Copy)
                if RWSTOP <= 5.4:
                    continue
                psYe, psYo, psY2 = psr.next(), psr.next(), psr.next()
                for h in range(8):
                    o = slice(h * 64, (h + 1) * 64)
                    o2 = slice((h // 2) * 64, (h // 2 + 1) * 64)
                    MM((psYe, psYo)[h % 2][:T, o2], fs(R_b, h), ST_rwb[(h % 2) * 64:(h % 2) * 64 + 64, h // 2, :], sg=(h >= 6))
                    MM(psY2[:T, o], RbM[:, h, :], UT[:, o], True, False)
                    MM(psY2[:T, o], RkM[:, h, :], Vtok[:, o], False, True, sg=(h == 7))
                Y4 = ytok.rearrange("t (c hp v) -> t c hp v", c=4, hp=2)
                A('activation', out=Y4[:, :, 0, :], in_=psYe[:T, 0:256].rearrange("t (c v) -> t c v", c=4), func=AF.Copy)
                A('activation', out=Y4[:, :, 1, :], in_=psYo[:T, 0:256].rearrange("t (c v) -> t c v", c=4), func=AF.Copy)
                V('tensor_tensor', out=ytok, in0=ytok, in1=psY2[:T, 0:512], op=ALU.add)
                if RWSTOP <= 5.6:
                    continue
                psyt = psr.next()
                for c4 in range(4):
                    S.I('pe', 'transpose', psyt[:, c4 * T:(c4 + 1) * T], ytok[:, c4 * 128:(c4 + 1) * 128], ident[:T, :T],
                        sig=(c4 == 3))
                A('activation', out=Yfm[:, :, c0:c0 + T], in_=psyt[:, 0:4 * T].rearrange("p (c t) -> p c t", c=4), func=AF.Copy)
                if RWSTOP <= 6:
                    continue
                for h in range(8):
                    c4, hp = h // 2, h % 2
                    o = slice(h * 64, (h + 1) * 64)
                    psS = psr.next()
                    MM(psS[:, 0:64], Bgt[:, c4 * 128:(c4 + 1) * 128], UT[:, o], True, False)
                    MM(psS[:, 0:64], Kgt[:, c4 * 128:(c4 + 1) * 128], Vtok[:, o], False, True)
                    rows = slice(hp * 64, hp * 64 + 64)
                    V('scalar_tensor_tensor', out=ST_rw[ls][rows, c4, :], in0=ST_rw[ls][rows, c4, :],
                      scalar=EP[rows, c4, c0 + T - 1:c0 + T], in1=psS[rows, 0:64], op0=ALU.mult, op1=ALU.add)
                if kind == 's':
                    rw_state_store(ls, dout['s_rwkv'][l, ci])
            if RWSTOP <= 7:
                continue
            for c4 in range(4):
                ps = psr.next()
                MM(ps[:, :nt], bd64, Yfm[:, c4, :nt])
                cen = t1.next()
                V('scalar_tensor_tensor', out=cen[:, :nt], in0=ps[:, :nt], scalar=-1.0 / 64, in1=Yfm[:, c4, :nt],
                  op0=ALU.mult, op1=ALU.add)
                sq = t1.next()
                A('activation', out=sq[:, :nt], in_=cen[:, :nt], func=AF.Square)
                ps2 = psr.next()
                MM(ps2[:, :nt], bd64, sq[:, :nt])
                V('tensor_scalar', out=sq[:, :nt], in0=ps2[:, :nt], scalar1=1.0 / 64, scalar2=64e-5, op0=ALU.mult, op1=ALU.add)
                A('sqrt', sq[:, :nt], sq[:, :nt])
                V('reciprocal', sq[:, :nt], sq[:, :nt])
                V('tensor_tensor', out=cen[:, :nt], in0=cen[:, :nt], in1=sq[:, :nt], op=ALU.mult)
                V('tensor_scalar', out=cen[:, :nt], in0=cen[:, :nt], scalar1=par[:, P_LNW + c4:P_LNW + c4 + 1],
                  scalar2=par[:, P_LNB + c4:P_LNB + c4 + 1], op0=ALU.mult, op1=ALU.add)
                V('tensor_tensor', out=cen[:, :nt], in0=cen[:, :nt], in1=BON[:, c4, :nt], op=ALU.add)
                V('tensor_tensor', out=ybf[c4][:, h0:h0 + nt], in0=cen[:, :nt], in1=G[:, c4, :nt], op=ALU.mult)
        if RWSTOP <= 8:
            return
        if kind == 's':
            fm_to_tok_store(lambda c: shs[:, c, 0:NSEQ], 14, NSEQ, dout['s_rwkv_shift'][l])
        elif last:
            fm_to_tok_store(lambda c: sh_rw[l][:, c, :], 14, 1, dout['p_rwkv_shift'][l].rearrange("(o c) -> o c", o=1))
            rw_state_store(l, dout['p_rwkv'][l])
        branch_out(l, 3, lambda k: ybf[k][:, :ntk], 4, ntk, 'w_br_rw')


    tiles = [('p', i * 512, min(512, SEQ - i * 512)) for i in range((SEQ + 511) // 512)]
    tiles.append(('s', 0, NSEQ * TS))
    nprompt = len(tiles) - 1

    for ti, (kind, t0, ntk) in enumerate(tiles):
        last = (ti == nprompt - 1)
        src = din['x_prompt'] if kind == 'p' else din['x_sample']
        dsty = dout['y_prompt'] if kind == 'p' else dout['y_sample']
        nsub = (ntk + 127) // 128
        if kind == 'p':
            chunks128 = [(j * 128, min(128, ntk - j * 128)) for j in range(nsub)]
            chunks64 = [(j * 64, min(64, ntk - j * 64)) for j in range((ntk + 63) // 64)]
        else:
            chunks128 = [(b * TS, TS) for b in range(NSEQ)]
            chunks64 = chunks128
        for j in range(nsub):
            n = min(128, ntk - j * 128)
            lt = PG[7 + j % 2][:, 0:1024]
            S.dma('sp', lt[:n, :], src[t0 + j * 128:t0 + j * 128 + n, :])
            for k0 in range(0, KC, 4):
                ps = psr.next()
                for k in range(k0, k0 + 4):
                    S.I('pe', 'transpose', ps[:, (k - k0) * 128:(k - k0) * 128 + n], lt[:n, k * 128:(k + 1) * 128],
                        ident[:n, :n], sig=(k == k0 + 3))
                A('activation', out=x[:, k0:k0 + 4, j * 128:j * 128 + n],
                  in_=ps[:, :].rearrange("p (k n) -> p k n", k=4)[:, :, :n], func=AF.Copy)

        for l in range(L):
            colload(par[:, P_N1:P_N1 + 8], din['norm1_w'][l])
            colload(par[:, P_N2:P_N2 + 8], din['norm2_w'][l])
            colload(par[:, P_FCB:P_FCB + 44], din['ffn_conv_b'][l])
            for j in range(3):
                colload(par[:, P_FCW + 44 * j:P_FCW + 44 * j + 44], din['ffn_conv_w'][l, j])
            colload(par[:, P_BM:P_BM + 32], din['b_merge'][l])
            rmsnorm_x(par[:, P_N1:P_N1 + 8], ntk, xn)
            S.mrg_first = True
            if 'ssd' in EN:
                ssd_phase(l, kind, ntk, chunks128, last)
            if 's5' in EN:
                s5_phase(l, kind, ntk, chunks64, last)
            if 'hg' in EN:
                hg_phase(l, kind, ntk, chunks64, last)
            if 'rw' in EN:
                rw_phase(l, kind, ntk, chunks64, last)
            if S.mrg_first:
                V('memset', mrg[:, :, :ntk], 0.0)
            A('activation', out=xn[:, :, :ntk], in_=mrg[:, :, :ntk], func=AF.Copy)
            for u in range(2):
                wv = loadw(din['w_out'][l], u * 512, 512)
                for cb in range(4):
                    ps = dense_fm(wv, cb, lambda k: xn[:, k, :ntk], KC, ntk)
                    kk = u * 4 + cb
                    V('tensor_tensor', out=x[:, kk, :ntk], in0=x[:, kk, :ntk], in1=ps[:, :ntk], op=ALU.add)
            rmsnorm_x(par[:, P_N2:P_N2 + 8], ntk, xn)
            ffh = [PG[c // 8][:, :].bitcast(BF16)[:, (c % 8) * 512:(c % 8) * 512 + 512] for c in range(22)]
            if kind == 's':
                tok_to_fm_load(din['state_ffn_conv'][l].rearrange("b j c -> (b j) c"), 44, NSEQ * 2,
                               lambda c: shf[:, c, :])
            for ub in range(11):
                wg = loadw(din['ffn_up'][l], ub * 256, 256)
                wvv = loadw(din['ffn_up'][l], D_FF + ub * 256, 256)
                for cb in range(2):
                    c = ub * 2 + cb
                    res = []
                    for (wsel, cc) in ((wg, c), (wvv, 22 + c)):
                        ps = dense_fm(wsel, cb, lambda k: xn[:, k, :ntk], KC, ntk)
                        cv = t1.next()
                        if kind == 'p':
                            hist = newst = cv_ffn[l][:, cc, :]
                        else:
                            hist = newst = shf[:, cc, :].rearrange("p (b j) -> p b j", j=2)
                        conv_block(ps, ntk, kind, hist, newst,
                                   [par[:, P_FCW + 44 * j + cc:P_FCW + 44 * j + cc + 1] for j in range(3)],
                                   par[:, P_FCB + cc:P_FCB + cc + 1], 3, cv[:, :ntk])
                        res.append(cv)
                    g, v = res
                    u1 = t1.next()
                    A('activation', out=u1[:, :ntk], in_=g[:, :ntk], func=AF.Square)
                    V('tensor_scalar', out=u1[:, :ntk], in0=u1[:, :ntk], scalar1=0.044715, scalar2=1.0,
                      op0=ALU.mult, op1=ALU.add)
                    V('tensor_tensor', out=u1[:, :ntk], in0=u1[:, :ntk], in1=g[:, :ntk], op=ALU.mult)
                    A('activation', out=u1[:, :ntk], in_=u1[:, :ntk], func=AF.Sigmoid, scale=1.5957691216)
                    V('tensor_tensor', out=u1[:, :ntk], in0=u1[:, :ntk], in1=g[:, :ntk], op=ALU.mult)
                    V('tensor_tensor', out=ffh[c][:, :ntk], in0=u1[:, :ntk], in1=v[:, :ntk], op=ALU.mult)
            if kind == 's':
                fm_to_tok_store(lambda c: shf[:, c, :], 44, NSEQ * 2,
                                dout['s_ffn_conv'][l].rearrange("b j c -> (b j) c"))
            elif last:
                fm_to_tok_store(lambda c: cv_ffn[l][:, c, :], 44, 2, dout['p_ffn_conv'][l])
            for cb in range(8):
                wv = loadw(din['ffn_down'][l], cb * 128, 128)
                ps = psr.next()
                for k in range(22):
                    MM(ps[:, :ntk], wv[:, k, :], ffh[k][:, :ntk], k == 0, k == 21)
                V('tensor_tensor', out=x[:, cb, :ntk], in0=x[:, cb, :ntk], in1=ps[:, :ntk], op=ALU.add)

        colload(par[:, 500:508], din['final_norm_w'])
        sumsq_rstd([x[:, k, :ntk] for k in range(KC)], ntk, D, EPS)
        yo = [PG[k // 4][:, (k % 4) * 512:(k % 4) * 512 + 512] for k in range(8)]
        for k in range(KC):
            V('scalar_tensor_tensor', out=yo[k][:, :ntk], in0=x[:, k, :ntk], scalar=par[:, 500 + k:501 + k],
              in1=rstd[:, :ntk], op0=ALU.mult, op1=ALU.mult)
        for j in range(nsub):
            n = min(128, ntk - j * 128)
            so = PG[7 + j % 2][:, 0:1024]
            for k0 in range(0, KC, 4):
                ps = psr.next()
                for k in range(k0, k0 + 4):
                    S.I('pe', 'transpose', ps[:n, (k - k0) * 128:(k - k0 + 1) * 128], yo[k][:, j * 128:j * 128 + n],
                        ident, sig=(k == k0 + 3))
                A('activation', out=so[:n, k0 * 128:(k0 + 4) * 128], in_=ps[:n, :], func=AF.Copy)
            S.dma('sp', dsty[t0 + j * 128:t0 + j * 128 + n, :], so[:n, :])

    S.finish()


OUT_ORDER = ['y_prompt', 'y_sample'] + ['p_' + n for n in STATE_NAMES] + ['s_' + n for n in STATE_NAMES]


def kernel(_cfg=None, **inputs):
    inputs = {k: np.asarray(v) for k, v in inputs.items()}
    x_prompt, x_sample = inputs['x_prompt'], inputs['x_sample']
    B, SEQ, _ = x_prompt.shape
    DB, TS, _ = x_sample.shape
    DEPTH = inputs['w_in'].shape[0]
    nseq = DB // NCORES
    cfg = dict(depth=DEPTH, seq=SEQ, nseq=nseq)
    if _cfg:
        cfg.update(_cfg)
    consts = make_consts()
    in_maps = []
    for c in range(NCORES):
        m = {}
        for k in IN_NAMES:
            a = inputs[k]
            if k == 'x_prompt':
                a = a[c]
            elif k == 'x_sample':
                a = a[c * nseq:(c + 1) * nseq].reshape(nseq * TS, D)
            elif k.startswith('state_'):
                a = a[:, c * nseq:(c + 1) * nseq]
            m[k] = np.ascontiguousarray(a, dtype=np.float32)
        m['consts'] = consts
        in_maps.append(m)
    shapes = {k: in_maps[0][k].shape for k in IN_NAMES}
    nc = build(cfg, shapes)
    res = run_bass_kernel_spmd(nc, in_maps, core_ids=list(range(NCORES)))
    r = res.results
    outs = []
    for name in OUT_ORDER:
        if name == 'y_prompt':
            outs.append(np.stack([r[c][name] for c in range(NCORES)], 0))
        elif name == 'y_sample':
            outs.append(np.concatenate([r[c][name].reshape(nseq, TS, D) for c in range(NCORES)], 0))
        elif name.startswith('p_'):
            outs.append(np.stack([r[c][name] for c in range(NCORES)], 1))
        else:
            outs.append(np.concatenate([r[c][name] for c in range(NCORES)], 1))
    return tuple(outs)
```
